# Optimizing a Trainium2 kernel written in Bass

```python
import jax
import jax.numpy as jnp
from jax import lax
import numpy as np

D_MODEL = 1024
BATCH = 4
SEQ = 8192
DEPTH = 2

D_MIX = D_MODEL
GLA_HEADS = 4
GLA_VW = D_MIX // 2
GLA_KW = GLA_VW // 2
GLA_DV = GLA_VW // GLA_HEADS
GLA_DK = GLA_KW // GLA_HEADS
GLA_GATE_RANK = 16
GLA_GATE_NORM = 16.0
GLA_CHUNK = 64
RWKV_VW = D_MIX - GLA_VW
RWKV_HEAD = 64
RWKV_HEADS = RWKV_VW // RWKV_HEAD
DECAY_LORA = 64
AAA_LORA = 64
GATE_LORA = 128
MV_LORA = 32
GN_EPS = 64e-5
D_FF = 128 * ((8 * D_MODEL // 3 + 127) // 128)
CONV_W = 3
NORM_EPS = 1e-6
N_MOD = 6
GLA_COLS = 2 * GLA_KW + 2 * GLA_VW + GLA_GATE_RANK
RWKV_COLS = 3 * RWKV_VW + DECAY_LORA + AAA_LORA + GATE_LORA
IN_COLS = GLA_COLS + RWKV_COLS

kernel_name = 'hybrid_gla_rwkv7_convffn_adaln'


def rms_norm(x, g):
    xf = x.astype(jnp.float32)
    y = xf * lax.rsqrt(jnp.mean(xf * xf, axis=-1, keepdims=True) + NORM_EPS)
    return (y * g.astype(jnp.float32)).astype(x.dtype)


def modulate(x, g, shift, scale):
    return rms_norm(x, g) * (1 + scale[:, None, :]) + shift[:, None, :]


def token_shift(p):
    return jnp.pad(p, ((0, 0), (1, 0), (0, 0)))[:, :-1]


def gla_chunked(q, k, v, gk):
    B, T, H, DK = q.shape
    DV = v.shape[-1]
    NC = T // GLA_CHUNK

    def to_chunks(t):
        return t.reshape(B, NC, GLA_CHUNK, H, t.shape[-1]).transpose(1, 0, 3, 2, 4).astype(jnp.float32)

    qc, kc, vc, gc = (to_chunks(t) for t in (q, k, v, gk))
    G = jnp.cumsum(gc, axis=3)
    causal = jnp.tril(jnp.ones((GLA_CHUNK, GLA_CHUNK), dtype=bool))[:, :, None]

    def step(S, inp):
        qi, ki, vi, Gi = inp
        inter = jnp.einsum('bhid,bhde->bhie', qi * jnp.exp(Gi), S)
        rel = jnp.where(causal, Gi[:, :, :, None, :] - Gi[:, :, None, :, :], -jnp.inf)
        A = jnp.einsum('bhid,bhjd,bhijd->bhij', qi, ki, jnp.exp(rel))
        o = inter + jnp.einsum('bhij,bhje->bhie', A, vi)
        G_last = Gi[:, :, -1, :]
        S = jnp.exp(G_last)[..., None] * S + jnp.einsum(
            'bhjd,bhje->bhde', ki * jnp.exp(G_last[:, :, None, :] - Gi), vi)
        return S, o

    S0 = jnp.zeros((B, H, DK, DV), jnp.float32)
    _, o = lax.scan(step, S0, (qc, kc, vc, G))
    return o.transpose(1, 0, 3, 2, 4).reshape(B, T, H, DV)


def gla_group(p, gk_up, gk_b, norm_g):
    B, T, _ = p.shape
    q = p[..., :GLA_KW].reshape(B, T, GLA_HEADS, GLA_DK) * (GLA_DK ** -0.5)
    k = p[..., GLA_KW:2 * GLA_KW].reshape(B, T, GLA_HEADS, GLA_DK)
    v = p[..., 2 * GLA_KW:2 * GLA_KW + GLA_VW].reshape(B, T, GLA_HEADS, GLA_DV)
    g = p[..., 2 * GLA_KW + GLA_VW:2 * GLA_KW + 2 * GLA_VW]
    z = p[..., 2 * GLA_KW + 2 * GLA_VW:]
    gk = jax.nn.log_sigmoid((z @ gk_up + gk_b).astype(jnp.float32)) / GLA_GATE_NORM
    o = gla_chunked(q, k, v, gk.reshape(B, T, GLA_HEADS, GLA_DK))
    o = rms_norm(o, norm_g).reshape(B, T, GLA_VW)
    return o.astype(p.dtype) * jax.nn.silu(g)


def wkv7_scan(r, w, k, v, a, b):
    B, T, H, N = r.shape
    xs = tuple(t.transpose(1, 0, 2, 3) for t in (r, w, k, v, a, b))

    def step(S, inp):
        rt, wt, kt, vt, at, bt = inp
        sa = jnp.einsum('bhvk,bhk->bhv', S, at)
        S = S * wt[:, :, None, :] + sa[..., None] * bt[:, :, None, :] + vt[..., None] * kt[:, :, None, :]
        return S, jnp.einsum('bhvk,bhk->bhv', S, rt)

    _, y = lax.scan(step, jnp.zeros((B, H, N, N), jnp.float32), xs)
    return y.transpose(1, 0, 2, 3)


def rwkv7_group(p, mu, w_up, w0, a_up, a0, g_up, k_k, k_a, r_k, gn_g, gn_b, v_first, v_up, v0):
    B, T, _ = p.shape
    H, N = RWKV_HEADS, RWKV_HEAD
    p = p + (token_shift(p) - p) * mu
    c0 = 3 * RWKV_VW
    c1 = c0 + DECAY_LORA
    c2 = c1 + AAA_LORA
    c3 = c2 + GATE_LORA
    r = p[..., :RWKV_VW]
    k = p[..., RWKV_VW:2 * RWKV_VW]
    v = p[..., 2 * RWKV_VW:c0]
    w_log = -jax.nn.softplus(-(w0 + jnp.tanh(p[..., c0:c1]) @ w_up)) - 0.5
    decay = jnp.exp(-jnp.exp(w_log.astype(jnp.float32)))
    a = jax.nn.sigmoid(a0 + p[..., c1:c2] @ a_up)
    g = jax.nn.sigmoid(p[..., c2:c3]) @ g_up
    if v_first is None:
        v_first = v
    else:
        v = v + (v_first - v) * jax.nn.sigmoid(v0 + p[..., c3:] @ v_up)
    kk = (k * k_k).reshape(B, T, H, N).astype(jnp.float32)
    kk = kk * lax.rsqrt(jnp.maximum(jnp.sum(kk * kk, axis=-1, keepdims=True), 1e-24))
    k = k * (1 + (a - 1) * k_a)

    def heads(t):
        return t.reshape(B, T, H, N).astype(jnp.float32)

    rh, kh, vh, ah = heads(r), heads(k), heads(v), heads(a)
    y = wkv7_scan(rh, decay.reshape(B, T, H, N), kh, vh, -kk, kk * ah)
    mean = jnp.mean(y, axis=-1, keepdims=True)
    var = jnp.mean(jnp.square(y - mean), axis=-1, keepdims=True)
    y = ((y - mean) * lax.rsqrt(var + GN_EPS)).reshape(B, T, RWKV_VW) * gn_g + gn_b
    bonus = jnp.sum(rh * kh * r_k, axis=-1, keepdims=True) * vh
    y = y + bonus.reshape(B, T, RWKV_VW)
    return (y * g).astype(p.dtype), v_first


def conv_ffn(h, w_up, conv_w, conv_b, w_down):
    T = h.shape[1]
    u = h @ w_up
    up = jnp.pad(u, ((0, 0), (CONV_W - 1, 0), (0, 0)))
    u = sum(conv_w[j] * up[:, j:j + T] for j in range(CONV_W)) + conv_b
    gate, val = jnp.split(u, 2, axis=-1)
    return (jax.nn.silu(gate) * val) @ w_down


def setup_inputs(seed: int = 0) -> dict:
    key = jax.random.key(seed)
    ks = jax.random.split(key, 31)
    L, Lv, D = DEPTH, DEPTH - 1, D_MODEL

    def nrm(k, shape, scale):
        return jax.random.normal(k, shape, jnp.float32) * scale

    def unif(k, shape, lo, hi):
        return jax.random.uniform(k, shape, jnp.float32, lo, hi)

    return {
        'x': nrm(ks[0], (BATCH, SEQ, D), 1.0),
        'c': nrm(ks[1], (BATCH, D), 1.0),
        'ada_w': nrm(ks[2], (L, D, N_MOD * D), 0.5 * D ** -0.5),
        'ada_b': nrm(ks[3], (L, N_MOD * D), 0.05),
        'norm_mix_g': 1.0 + nrm(ks[4], (L, D), 0.02),
        'w_in': nrm(ks[5], (L, D, IN_COLS), D ** -0.5),
        'w_in_vres': nrm(ks[6], (Lv, D, MV_LORA), D ** -0.5),
        'gla_gk_up': nrm(ks[7], (L, GLA_GATE_RANK, GLA_KW), GLA_GATE_RANK ** -0.5),
        'gla_gk_b': nrm(ks[8], (L, GLA_KW), 0.5),
        'gla_norm_g': 1.0 + nrm(ks[9], (L, GLA_DV), 0.02),
        'rwkv_mu': unif(ks[10], (L, RWKV_COLS), 0.0, 1.0),
        'rwkv_mu_vres': unif(ks[11], (Lv, MV_LORA), 0.0, 1.0),
        'w_lora_up': nrm(ks[12], (L, DECAY_LORA, RWKV_VW), 0.1 * DECAY_LORA ** -0.5),
        'w0': unif(ks[13], (L, RWKV_VW), -6.5, -1.5),
        'a_lora_up': nrm(ks[14], (L, AAA_LORA, RWKV_VW), AAA_LORA ** -0.5),
        'a0': nrm(ks[15], (L, RWKV_VW), 0.1),
        'g_lora_up': nrm(ks[16], (L, GATE_LORA, RWKV_VW), GATE_LORA ** -0.5),
        'v_lora_up': nrm(ks[17], (Lv, MV_LORA, RWKV_VW), MV_LORA ** -0.5),
        'v0': 1.0 + nrm(ks[18], (Lv, RWKV_VW), 0.1),
        'k_k': 0.85 + nrm(ks[19], (L, RWKV_VW), 0.02),
        'k_a': 1.0 + nrm(ks[20], (L, RWKV_VW), 0.02),
        'r_k': nrm(ks[21], (L, RWKV_HEADS, RWKV_HEAD), 0.1),
        'gn_g': 1.0 + nrm(ks[22], (L, RWKV_VW), 0.02),
        'gn_b': nrm(ks[23], (L, RWKV_VW), 0.02),
        'w_out': nrm(ks[24], (L, D_MIX, D), D_MIX ** -0.5),
        'norm_ffn_g': 1.0 + nrm(ks[25], (L, D), 0.02),
        'ffn_up': nrm(ks[26], (L, D, 2 * D_FF), D ** -0.5),
        'ffn_conv_w': nrm(ks[27], (L, CONV_W, 2 * D_FF), CONV_W ** -0.5),
        'ffn_conv_b': nrm(ks[28], (L, 2 * D_FF), 0.02),
        'ffn_down': nrm(ks[29], (L, D_FF, D), D_FF ** -0.5),
        'final_g': 1.0 + nrm(ks[30], (D,), 0.02),
    }


def reference(x, c, ada_w, ada_b, norm_mix_g, w_in, w_in_vres, gla_gk_up, gla_gk_b, gla_norm_g,
              rwkv_mu, rwkv_mu_vres, w_lora_up, w0, a_lora_up, a0, g_lora_up, v_lora_up, v0,
              k_k, k_a, r_k, gn_g, gn_b, w_out, norm_ffn_g, ffn_up, ffn_conv_w, ffn_conv_b,
              ffn_down, final_g):
    v_first = None
    for l in range(DEPTH):
        mod = jax.nn.silu(c) @ ada_w[l] + ada_b[l]
        sh_m, sc_m, gt_m, sh_f, sc_f, gt_f = jnp.split(mod, N_MOD, axis=-1)
        h = modulate(x, norm_mix_g[l], sh_m, sc_m)
        if l == 0:
            w_proj, mu, v_up_l, v0_l = w_in[0], rwkv_mu[0], None, None
        else:
            w_proj = jnp.concatenate([w_in[l], w_in_vres[l - 1]], axis=1)
            mu = jnp.concatenate([rwkv_mu[l], rwkv_mu_vres[l - 1]], axis=0)
            v_up_l, v0_l = v_lora_up[l - 1], v0[l - 1]
        proj = h @ w_proj
        o_gla = gla_group(proj[..., :GLA_COLS], gla_gk_up[l], gla_gk_b[l], gla_norm_g[l])
        o_rwkv, v_first = rwkv7_group(proj[..., GLA_COLS:], mu, w_lora_up[l], w0[l], a_lora_up[l], a0[l],
                                      g_lora_up[l], k_k[l], k_a[l], r_k[l], gn_g[l], gn_b[l],
                                      v_first, v_up_l, v0_l)
        mixed = jnp.concatenate([o_gla, o_rwkv], axis=-1) @ w_out[l]
        x = x + gt_m[:, None, :] * mixed
        h = modulate(x, norm_ffn_g[l], sh_f, sc_f)
        x = x + gt_f[:, None, :] * conv_ffn(h, ffn_up[l], ffn_conv_w[l], ffn_conv_b[l], ffn_down[l])
    return rms_norm(x, final_g)
```

```python
import contextlib
import numpy as np
import concourse.bass as bass
import concourse.mybir as mybir
from concourse.bass_utils import run_bass_kernel_spmd

F32 = mybir.dt.float32
BF16 = mybir.dt.bfloat16
AF = mybir.ActivationFunctionType
ALU = mybir.AluOpType

D = 1024
NKC = 8
TT = 256
C = 128
NCH = TT // C
NFF = 22
SEQ = 8192
GN_EPS = 64e-5
NORM_EPS = 1e-6
NFM = 24
RW0 = 9

def _playout():
    off = {}
    cur = 0
    def add(name, w):
        nonlocal cur
        off[name] = (cur, w)
        cur += w
    add('c', 8)
    add('final_g', 8)
    for l in range(2):
        add(f'ada_b{l}', 48)
        add(f'nmg{l}', 8)
        add(f'nfg{l}', 8)
        add(f'mu{l}', 15)
        for nm in ('w0', 'a0', 'v0', 'k_k', 'k_a', 'r_k', 'gn_g', 'gn_b'):
            add(f'{nm}{l}', 4)
        add(f'gng{l}', 1)
        add(f'cw{l}', 132)
        add(f'cb{l}', 44)
    return off, cur

POFF, NPARAM = _playout()


def _fm(vec):
    v = np.asarray(vec, np.float32).reshape(-1)
    n = v.shape[0] // 128
    return v.reshape(n, 128).T


def _wt(cols):
    K, w = cols.shape
    return cols.reshape(K // 128, 128, w).transpose(1, 0, 2)


def prep_shared(inp):
    f = lambda a: np.asarray(a, np.float32)
    w_in = f(inp['w_in']); w_vres = f(inp['w_in_vres'])
    sh = {}
    w_fm = np.zeros((2, NFM, 128, 8, 128), np.float32)
    RB = 1552
    for l in range(2):
        W = w_in[l]
        def put(idx, cols):
            w_fm[l, idx, :, :, :cols.shape[1]] = _wt(cols)
        put(0, W[:, 1536:1552])
        put(1, W[:, 0:128]); put(2, W[:, 128:256]); put(3, W[:, 256:384]); put(4, W[:, 384:512])
        for i in range(4):
            put(5 + i, W[:, 1024 + i * 128:1024 + (i + 1) * 128])
        put(9, W[:, RB + 1536:RB + 1664])
        put(10, W[:, RB + 1664:RB + 1792])
        if l == 1:
            put(11, w_vres[0])
        for ct in range(4):
            put(12 + 3 * ct, W[:, RB + ct * 128:RB + (ct + 1) * 128])
            put(13 + 3 * ct, W[:, RB + 512 + ct * 128:RB + 512 + (ct + 1) * 128])
            put(14 + 3 * ct, W[:, RB + 1024 + ct * 128:RB + 1024 + (ct + 1) * 128])
    sh['w_fm'] = w_fm
    sh['w_v'] = np.stack([_wt(w_in[l][:, 512:1024]) for l in range(2)])
    w_out = f(inp['w_out'])
    sh['w_o'] = np.stack([np.stack([_wt(w_out[l][:, j * 128:(j + 1) * 128]) for j in range(8)]) for l in range(2)])
    fu = f(inp['ffn_up'])
    wu = np.zeros((2, 44, 128, 8, 128), np.float32)
    for l in range(2):
        for j in range(NFF):
            wu[l, 2 * j] = _wt(fu[l][:, j * 128:(j + 1) * 128])
            wu[l, 2 * j + 1] = _wt(fu[l][:, 2816 + j * 128:2816 + (j + 1) * 128])
    sh['w_u'] = wu
    fd = f(inp['ffn_down'])
    sh['w_d'] = np.stack([np.stack([_wt(fd[l][:, j * 128:(j + 1) * 128]) for j in range(8)]) for l in range(2)])
    sh['ada'] = np.stack([_wt(f(inp['ada_w'])[l]) for l in range(2)])
    sh['wa_up'] = np.stack([np.concatenate([f(inp['w_lora_up'])[l], f(inp['a_lora_up'])[l]], 0) for l in range(2)])
    sh['g_up'] = f(inp['g_lora_up'])
    vup = np.zeros((128, 512), np.float32); vup[:32] = f(inp['v_lora_up'])[0]
    sh['v_up'] = vup
    gku = np.zeros((2, 128, 256), np.float32); gku[:, :16] = f(inp['gla_gk_up'])
    sh['gk_up'] = gku
    sh['gkb_bc'] = np.stack([np.broadcast_to(f(inp['gla_gk_b'])[l], (128, 256)) for l in range(2)]).copy()
    sh['w0_bc'] = np.stack([np.broadcast_to(f(inp['w0'])[l], (128, 512)) for l in range(2)]).copy()
    j = np.arange(128)[:, None]; t = np.arange(128)[None, :]
    su = (j < t).astype(np.float32); iu = (j <= t).astype(np.float32); sl = (j > t).astype(np.float32)
    eye = np.eye(128, dtype=np.float32)
    bones = np.zeros((128, 128), np.float32); bones[:64, :64] = 1; bones[64:, 64:] = 1
    cb = np.concatenate([eye, su, iu, sl, np.ones((128, 128), np.float32), bones, bones / 64.0], 1)
    sh['constb'] = cb
    ew = np.float32(np.exp(-0.5))
    sh['constf'] = np.concatenate([iu * ew, su * ew, iu / 16.0], 1).astype(np.float32)
    P = np.zeros((128, NPARAM), np.float32)
    def setp(name, arr):
        o, w = POFF[name]
        assert arr.shape == (128, w), (name, arr.shape, w)
        P[:, o:o + w] = arr
    setp('final_g', _fm(inp['final_g']))
    for l in range(2):
        setp(f'ada_b{l}', _fm(f(inp['ada_b'])[l]))
        setp(f'nmg{l}', _fm(f(inp['norm_mix_g'])[l]))
        setp(f'nfg{l}', _fm(f(inp['norm_ffn_g'])[l]))
        mu = f(inp['rwkv_mu'])[l]
        m = np.zeros((128, 15), np.float32)
        m[:, 0] = mu[1536:1664]; m[:, 1] = mu[1664:1792]
        if l == 1:
            m[:32, 2] = f(inp['rwkv_mu_vres'])[0]
        for ct in range(4):
            m[:, 3 + 3 * ct] = mu[ct * 128:(ct + 1) * 128]
            m[:, 4 + 3 * ct] = mu[512 + ct * 128:512 + (ct + 1) * 128]
            m[:, 5 + 3 * ct] = mu[1024 + ct * 128:1024 + (ct + 1) * 128]
        setp(f'mu{l}', m)
        for nm, key in (('w0', 'w0'), ('a0', 'a0'), ('k_k', 'k_k'), ('k_a', 'k_a'), ('gn_g', 'gn_g'), ('gn_b', 'gn_b')):
            setp(f'{nm}{l}', _fm(f(inp[key])[l]))
        setp(f'r_k{l}', _fm(f(inp['r_k'])[l].reshape(-1)))
        if l == 1:
            setp('v01', _fm(f(inp['v0'])[0]))
        setp(f'gng{l}', f(inp['gla_norm_g'])[l].reshape(128, 1))
        cw = f(inp['ffn_conv_w'])[l]; cbv = f(inp['ffn_conv_b'])[l]
        cwp = np.zeros((128, 44, 3), np.float32); cbp = np.zeros((128, 44), np.float32)
        for jj in range(NFF):
            for half in range(2):
                cols = slice(half * 2816 + jj * 128, half * 2816 + (jj + 1) * 128)
                cwp[:, 2 * jj + half, :] = cw[:, cols].T
                cbp[:, 2 * jj + half] = cbv[cols]
        setp(f'cw{l}', cwp.reshape(128, 132))
        setp(f'cb{l}', cbp)
    sh['params'] = P
    return sh


class Sched:
    def __init__(self, nc, es, ndma=8):
        self.nc = nc
        self.eng = {'pe': nc.tensor, 'act': nc.scalar, 'dve': nc.vector, 'pool': nc.gpsimd, 'sp': nc.sync}
        self.sem = {e: es.enter_context(nc.semaphore('s_' + e)) for e in ('pe', 'act', 'dve', 'pool')}
        self.cnt = {e: 0 for e in self.sem}
        self.dq = {}
        for q in ('sp', 'pool'):
            self.dq[q] = {'sems': [es.enter_context(nc.semaphore(f'd_{q}{i}')) for i in range(ndma)],
                          'cnt': [0] * ndma, 'nxt': 0}
        self.seen = {e: {} for e in self.eng}
        self.lastw = {}
        self.rd = {}
        self.n_inst = 0
        self.n_wait = 0

    def _semobj(self, sk):
        if sk[0] == 'e':
            return self.sem[sk[1]]
        return self.dq[sk[1]]['sems'][sk[2]]

    def _wait(self, eng, tk):
        sk, val, src = tk
        if self.seen[eng].get(sk, 0) >= val:
            return
        self.eng[eng].wait_ge(self._semobj(sk), val)
        self.seen[eng][sk] = val
        self.n_wait += 1

    def _deps(self, eng, reads, writes):
        deps = []
        for k in reads:
            if k in self.lastw:
                deps.append(self.lastw[k])
        for k in writes:
            if k in self.lastw:
                deps.append(self.lastw[k])
            for tk in self.rd.get(k, {}).values():
                deps.append(tk)
        for tk in deps:
            if eng == 'pe' and tk[2] == 'pe':
                continue
            self._wait(eng, tk)

    def _record(self, tk, reads, writes):
        for k in writes:
            self.lastw[k] = tk
            self.rd[k] = {}
        for k in reads:
            self.rd.setdefault(k, {})[tk[0]] = tk

    def emit(self, eng, fn, reads=(), writes=()):
        self._deps(eng, reads, writes)
        inst = fn()
        self.cnt[eng] += 1
        inst.then_inc(self.sem[eng], 1)
        self._record((('e', eng), self.cnt[eng], eng), reads, writes)
        self.n_inst += 1

    def dma(self, q, out, in_, reads=(), writes=()):
        Q = self.dq[q]
        i = Q['nxt']; Q['nxt'] = (i + 1) % len(Q['sems'])
        sk = ('d', q, i)
        if Q['cnt'][i] > 0:
            self._wait(q, (sk, Q['cnt'][i], 'dma'))
        self._deps(q, reads, writes)
        inst = self.eng[q].dma_start(out=out, in_=in_)
        Q['cnt'][i] += 16
        inst.then_inc(Q['sems'][i], 16)
        self._record((sk, Q['cnt'][i], 'dma'), reads, writes)
        self.n_inst += 1

    def barrier(self):
        for e in self.eng:
            for s in self.sem:
                if self.cnt[s] > 0:
                    self._wait(e, (('e', s), self.cnt[s], s))
            for q, Q in self.dq.items():
                for i, cval in enumerate(Q['cnt']):
                    if cval > 0:
                        self._wait(e, (('d', q, i), cval, 'dma'))


class _Stop(Exception):
    pass


def build_nc(T, debug=False, stop=None):
    NT = T // TT
    nc = bass.Bass("TRN2", target_bir_lowering=False)
    dr = lambda name, shape, dt=F32, kind="ExternalInput": nc.dram_tensor(name, list(shape), dt, kind=kind).ap()
    xT_d = dr("xT", [D, T])
    params_d = dr("params", [128, NPARAM])
    ada_d = dr("ada", [2, 128, 8, 6144])
    wfm_d = dr("w_fm", [2, NFM, 128, 8, 128])
    wv_d = dr("w_v", [2, 128, 8, 512])
    wo_d = dr("w_o", [2, 8, 128, 8, 128])
    wu_d = dr("w_u", [2, 44, 128, 8, 128])
    wd_d = dr("w_d", [2, 8, 128, NFF, 128])
    waup_d = dr("wa_up", [2, 128, 512])
    gup_d = dr("g_up", [2, 128, 512])
    vup_d = dr("v_up", [128, 512])
    gkup_d = dr("gk_up", [2, 128, 256])
    gkb_d = dr("gkb_bc", [2, 128, 256])
    w0bc_d = dr("w0_bc", [2, 128, 512])
    cb_d = dr("constb", [128, 7 * 128])
    cf_d = dr("constf", [128, 3 * 128])
    out_d = dr("outT", [D, T], F32, "ExternalOutput")
    s_fm = dr("s_fm", [2, NFM, 128, 8, 128], BF16, "Internal")
    s_v = dr("s_v", [2, 128, 8, 512], BF16, "Internal")
    s_o = dr("s_o", [2, 8, 128, 8, 128], BF16, "Internal")
    s_u = dr("s_u", [2, 44, 128, 8, 128], BF16, "Internal")
    s_d = dr("s_d", [2, 8, 128, NFF, 128], BF16, "Internal")
    dbg_d = {}
    if debug:
        for nm in ('d_rf', 'd_kf', 'd_vf', 'd_emL', 'd_kmod', 'd_kkn', 'd_asig', 'd_eL', 'd_eLp', 'd_ktr', 'd_btr'):
            dbg_d[nm] = dr("dbg_" + nm, [128, T], F32, "ExternalOutput")
        for nm in ('h0', 'oo0', 'xmix0', 'x0', 'h1', 'oo1', 'xmix1', 'x1', 'gla_o0', 'rwkv_y0'):
            rows = 512 if nm.startswith(('gla_o', 'rwkv_y')) else D
            dbg_d[nm] = dr("dbg_" + nm, [rows, T], F32, "ExternalOutput")

    with contextlib.ExitStack() as es:
        E = es.enter_context
        S = Sched(nc, es)
        sb = lambda name, shape, dt=F32: E(nc.sbuf_tensor("sb_" + name, list(shape), dt))

        prm = sb("prm", [128, NPARAM])
        cbt = sb("cbt", [128, 7 * 128], BF16)
        cft = sb("cft", [128, 3 * 128])
        waup = sb("waup", [128, 2, 512], BF16)
        gup = sb("gup", [128, 2, 512], BF16)
        vup = sb("vup", [128, 512], BF16)
        gkup = sb("gkup", [128, 2, 256], BF16)
        gkb = sb("gkb", [128, 2, 256])
        w0bc = sb("w0bc", [128, 2, 512])
        modt = sb("modt", [128, 2, 48])
        drv = sb("drv", [128, 2, 40])
        sc = sb("sc", [128, 8])
        ident = cbt[:, 0:128]; m_su = cbt[:, 128:256]; m_iu = cbt[:, 256:384]; m_sl = cbt[:, 384:512]
        m_suiu = cbt[:, 128:384]
        ones = cbt[:, 512:640]; bones = cbt[:, 640:768]; bones64 = cbt[:, 768:896]
        triW = cft[:, 0:256]; tri16 = cft[:, 256:384]

        def P(name, l=None):
            o, w = POFF[name if l is None else f'{name}{l}']
            return prm[:, o:o + w]

        def Pc(name, l, i):
            o, w = POFF[f'{name}{l}']
            return prm[:, o + i:o + i + 1]

        S.dma('sp', prm[:], params_d[:, :], writes=['prm'])
        S.dma('pool', cbt[:], cb_d[:, :], writes=['cbt'])
        S.dma('sp', cft[:], cf_d[:, :], writes=['cft'])
        for l in range(2):
            S.dma('pool', waup[:, l, :], waup_d[l], writes=['waup'])
            S.dma('pool', gup[:, l, :], gup_d[l], writes=['gup'])
            S.dma('pool', gkup[:, l, :], gkup_d[l], writes=['gkup'])
            S.dma('sp', gkb[:, l, :], gkb_d[l], writes=['gkb'])
            S.dma('sp', w0bc[:, l, :], w0bc_d[l], writes=['w0bc'])
        S.dma('pool', vup[:], vup_d[:, :], writes=['vup'])

        for l in range(2):
            for j in range(NFM):
                S.dma('pool', s_fm[l, j], wfm_d[l, j], writes=[f's_fm{l}'])
            for kc in range(8):
                S.dma('pool', s_v[l, :, kc, :], wv_d[l, :, kc, :], writes=[f's_v{l}'])
            for j in range(8):
                S.dma('pool', s_o[l, j], wo_d[l, j], writes=[f's_o{l}'])
            for j in range(44):
                S.dma('pool', s_u[l, j], wu_d[l, j], writes=[f's_u{l}'])
            for j in range(8):
                for h2 in range(2):
                    S.dma('pool', s_d[l, j, :, h2 * 11:(h2 + 1) * 11, :], wd_d[l, j, :, h2 * 11:(h2 + 1) * 11, :],
                          writes=[f's_d{l}'])

        NB = 6
        banks = [E(nc.psum_tensor(f"pb{i}", [128, 512], F32)) for i in range(NB)]
        ptbs = [E(nc.psum_tensor(f"ptb{i}", [128, 1024], BF16)) for i in range(2)]
        st = {'b': 0, 't': 0}

        def bank():
            i = st['b']; st['b'] = (i + 1) % NB
            return banks[i], f'pb{i}'

        def tbank():
            i = st['t']; st['t'] = (i + 1) % 2
            return ptbs[i][:, 0:512], f'pt{i}'

        def mm(out, lhsT, rhs, start, stop, r, w):
            S.emit('pe', lambda: nc.tensor.matmul(out, lhsT, rhs, start=start, stop=stop), r, w)

        def tr(out, in_, r, w):
            S.emit('pe', lambda: nc.tensor.transpose(out, in_, ident), list(r) + ['cbt'], w)

        def act(out, in_, func, r, w, scale=1.0, bias=0.0):
            S.emit('act', lambda: nc.scalar.activation(out=out, in_=in_, func=func, bias=bias, scale=scale), r, w)

        def tt(eng, out, in0, in1, op, r, w):
            e = nc.vector if eng == 'dve' else nc.gpsimd
            S.emit(eng, lambda: e.tensor_tensor(out=out, in0=in0, in1=in1, op=op), r, w)

        def ts(eng, out, in0, s1, op0, r, w, s2=None, op1=None):
            e = nc.vector if eng == 'dve' else nc.gpsimd
            if op1 is None:
                S.emit(eng, lambda: e.tensor_scalar(out=out, in0=in0, scalar1=s1, scalar2=None, op0=op0), r, w)
            else:
                S.emit(eng, lambda: e.tensor_scalar(out=out, in0=in0, scalar1=s1, scalar2=s2, op0=op0, op1=op1), r, w)

        def stt(out, in0, scalar, in1, op0, op1, r, w):
            S.emit('dve', lambda: nc.vector.scalar_tensor_tensor(out=out, in0=in0, scalar=scalar, in1=in1,
                                                                  op0=op0, op1=op1), r, w)

        def cp(eng, out, in_, r, w):
            if eng == 'act':
                S.emit('act', lambda: nc.scalar.activation(out=out, in_=in_, func=AF.Copy), r, w)
            else:
                e = nc.vector if eng == 'dve' else nc.gpsimd
                S.emit(eng, lambda: e.tensor_copy(out=out, in_=in_), r, w)

        def recip(out, in_, r, w):
            S.emit('dve', lambda: nc.vector.reciprocal(out=out, in_=in_), r, w)

        def memz(eng, ap, w):
            e = nc.vector if eng == 'dve' else nc.gpsimd
            S.emit(eng, lambda: e.memset(ap, 0.0), [], w)

        act(sc[:], P('c'), AF.Silu, ['prm'], ['sc'])
        with contextlib.ExitStack() as es2:
            adat = es2.enter_context(nc.sbuf_tensor("adat", [128, 8, 1024], F32))
            for l in range(2):
                for g6 in range(6):
                    S.dma('sp', adat[:], ada_d[l, :, :, g6 * 1024:(g6 + 1) * 1024], writes=['adat'])
                    ps, pk = bank()
                    for j in range(8):
                        for kc in range(8):
                            mm(ps[:, j:j + 1], adat[:, kc, j * 128:(j + 1) * 128], sc[:, kc:kc + 1],
                               kc == 0, kc == 7, ['adat', 'sc'], [pk])
                    tt('dve', modt[:, l, g6 * 8:(g6 + 1) * 8], ps[:, 0:8], P('ada_b', l)[:, g6 * 8:(g6 + 1) * 8],
                       ALU.add, [pk, 'prm'], ['modt'])
            S.barrier()
        for l in range(2):
            ts('dve', drv[:, l, 0:8], modt[:, l, 8:16], 1.0, ALU.add, ['modt'], ['drv'])
            tt('dve', drv[:, l, 0:8], drv[:, l, 0:8], P('nmg', l), ALU.mult, ['drv', 'prm'], ['drv'])
            ts('dve', drv[:, l, 8:16], modt[:, l, 32:40], 1.0, ALU.add, ['modt'], ['drv'])
            tt('dve', drv[:, l, 8:16], drv[:, l, 8:16], P('nfg', l), ALU.mult, ['drv', 'prm'], ['drv'])
            ts('dve', drv[:, l, 16:20], P('k_a', l), -1.0, ALU.mult, ['prm'], ['drv'], 1.0, ALU.add)
            ts('dve', drv[:, l, 20:35], P('mu', l), -1.0, ALU.mult, ['prm'], ['drv'], 1.0, ALU.add)
        A_m = lambda l, kc: drv[:, l, kc:kc + 1]
        A_f = lambda l, kc: drv[:, l, 8 + kc:9 + kc]
        omk = lambda l, ct: drv[:, l, 16 + ct:17 + ct]
        modc = lambda l, m, kc: modt[:, l, m * 8 + kc:m * 8 + kc + 1]

        xT = sb("xT", [128, 8, TT])
        hT = sb("hT", [128, 8, TT], BF16)
        sq = sb("sq", [128, 8, TT], BF16)
        rstd = sb("rstd", [128, TT])
        ftmp = [sb(f"ftmp{i}", [128, TT]) for i in range(3)]
        NA = 8
        ringA = [sb(f"ra{i}", [128, 8, 128], BF16) for i in range(NA)]
        wvb = sb("wvb", [128, 8, 512], BF16)
        ringD = [sb(f"rd{i}", [128, NFF, 128], BF16) for i in range(2)]
        rs = {'a': 0, 'd': 0, 'f': 0}

        def loadA(src, key):
            i = rs['a']; rs['a'] = (i + 1) % NA
            S.dma('sp', ringA[i][:], src, reads=[key], writes=[f'ra{i}'])
            return ringA[i], f'ra{i}'

        def loadD(src, key):
            i = rs['d']; rs['d'] = (i + 1) % 2
            S.dma('sp', ringD[i][:], src, reads=[key], writes=[f'rd{i}'])
            return ringD[i], f'rd{i}'

        def ft():
            i = rs['f']; rs['f'] = (i + 1) % 3
            return ftmp[i], f'ftmp{i}'

        ob = sb("ob", [128, 8, TT], BF16)
        qf = sb("qf", [128, 2, TT]); kfg = sb("kfg", [128, 2, TT])
        sg = sb("sg", [128, 4, TT], BF16)
        zb = sb("zb", [128, TT], BF16)
        vtok = sb("vtok", [128, NCH, 512], BF16)
        sptok = sb("sptok", [128, NCH, 256])
        eG = sb("eG", [128, 2, TT]); emG = sb("emG", [128, TT])
        qt = sb("qt", [128, 2, TT], BF16); ktg = sb("ktg", [128, 2, TT], BF16)
        ktgT = sb("ktgT", [128, NCH, 256], BF16)
        ATb = [sb(f"ATb{i}", [128, 128], BF16) for i in range(4)]
        osb = sb("osb", [128, 4, TT])
        Sg = sb("Sg", [128, 2, 2, 128]); Sgb = sb("Sgb", [128, 2, 2, 128], BF16)
        ptmp = [sb(f"ptmp{i}", [128, TT + 2]) for i in range(2)]
        dtmp = [sb(f"dtmp{i}", [128, TT]) for i in range(2)]
        halo = sb("halo", [128, 2, 16])
        wab = sb("wab", [128, TT], BF16)
        sgl = sb("sgl", [128, TT], BF16)
        vl = sb("vl", [128, TT], BF16)
        rkv = [[sb(f"rkv{i}_{j}", [128, TT]) for j in range(3)] for i in range(2)]
        vfirst = sb("vfirst", [128, 4, TT])
        nlw = sb("nlw", [128, NCH, 512])
        eL = sb("eL", [128, 4, TT])
        emL = [sb(f"emL{i}", [128, TT]) for i in range(2)]
        eLp = [sb(f"eLp{i}", [128, TT]) for i in range(2)]
        asig = [sb(f"asig{i}", [128, TT]) for i in range(2)]
        kkn = [sb(f"kkn{i}", [128, TT]) for i in range(2)]
        kmod = [sb(f"kmod{i}", [128, TT]) for i in range(2)]
        kk2 = sb("kk2", [128, TT], BF16)
        rkb = sb("rkb", [128, TT], BF16)
        vb = sb("vb", [128, TT], BF16)
        ktr = sb("ktr", [128, 4, TT], BF16); btr = sb("btr", [128, 4, TT], BF16)
        ar = sb("ar", [128, 4, NCH * 256], BF16)
        tokT = sb("tokT", [128, NCH, 4, 384], BF16)
        gate = sb("gate", [128, 4, TT], BF16)
        ysb = sb("ysb", [128, 4, TT])
        bonus = sb("bonus", [128, 4, TT])
        W0 = [sb(f"W0_{i}", [128, 384], BF16) for i in range(8)]
        Wpp = [[sb(f"Wpp{i}_{j}", [128, 384], BF16) for j in range(2)] for i in range(8)]
        MN = [sb(f"MN{i}", [128, 256], BF16) for i in range(8)]
        Nbr = [sb(f"Nbr{i}", [128, 128], BF16) for i in range(8)]
        Tfin = [sb(f"Tfin{i}", [128, 128], BF16) for i in range(8)]
        XTb = [sb(f"XTb{i}", [128, 128], BF16) for i in range(4)]
        UTb = [sb(f"UTb{i}", [128, 128], BF16) for i in range(4)]
        Sr = sb("Sr", [128, 2, 4, 64]); Srb = sb("Srb", [128, 2, 4, 64], BF16)
        mbuf = sb("mbuf", [128, NFF, TT], BF16)
        acc = [emL[0], emL[1], eLp[0], eLp[1]]
        acck = ['emL0', 'emL1', 'eLp0', 'eLp1']
        sgt = [asig[0], asig[1]]
        sgtk = ['asig0', 'asig1']
        chalo = sb("chalo", [128, 2, 2, 44, 2])

        memz('dve', Sg[:], ['Sg']); memz('dve', Sgb[:], ['Sgb'])
        memz('dve', Sr[:], ['Sr']); memz('dve', Srb[:], ['Srb'])
        memz('pool', halo[:], ['halo']); memz('pool', chalo[:], ['chalo'])
        memz('pool', vl[:], ['vl'])
        for h in range(8):
            cp('pool', W0[h][:, 256:384], ident, ['cbt'], [f'W0_{h}'])

        def dump(nm, ap, rows, ti, r, q='sp'):
            if debug and nm in dbg_d:
                dst = dbg_d[nm].rearrange("(k p) t -> p k t", p=128)[:, :, ti * TT:(ti + 1) * TT]
                S.dma(q, dst, ap, reads=r, writes=['dbg_' + nm])

        def rmsnorm_mod(l, Afn, shift_m):
            act(sq[:], xT[:], AF.Square, ['xT'], ['sq'])
            ps, pk = bank()
            for kc in range(8):
                mm(ps[:, 0:TT], ones, sq[:, kc, :], kc == 0, kc == 7, ['cbt', 'sq'], [pk])
            act(rstd[:], ps[:, 0:TT], AF.Sqrt, [pk], ['rstd'], scale=1.0 / D, bias=NORM_EPS)
            recip(rstd[:], rstd[:], ['rstd'], ['rstd'])
            for kc in range(8):
                f_, fk = ft()
                stt(f_[:], xT[:, kc, :], Afn(l, kc), rstd[:], ALU.mult, ALU.mult, ['xT', 'drv', 'rstd'], [fk])
                act(hT[:, kc, :], f_[:], AF.Identity, [fk, 'modt'], ['hT'], bias=modc(l, shift_m, kc))

        def proj_fm(l, idx, m):
            slot, sk = loadA(s_fm[l, idx], f's_fm{l}')
            ps, pk = bank()
            for kc in range(8):
                mm(ps[0:m, 0:TT], slot[:, kc, 0:m], hT[:, kc, :], kc == 0, kc == 7, [sk, 'hT'], [pk])
            return ps, pk

        def lerp(l, idx, ps, pk, m, out_ap, okey, pi):
            mi = idx - RW0
            pt_, ptk = ptmp[pi], f'ptmp{pi}'
            dt_, dtk = dtmp[pi], f'dtmp{pi}'
            cp('act', pt_[0:m, 1:TT + 1], ps[0:m, 0:TT], [pk], [ptk])
            cp('pool', pt_[0:m, 0:1], halo[0:m, l, mi:mi + 1], ['halo'], [ptk])
            tt('dve', dt_[0:m, :], pt_[0:m, 0:TT], pt_[0:m, 1:TT + 1], ALU.subtract, [ptk], [dtk])
            cp('pool', halo[0:m, l, mi:mi + 1], pt_[0:m, TT:TT + 1], [ptk], ['halo'])
            o, w = POFF[f'mu{l}']
            stt(out_ap, dt_[0:m, :], prm[0:m, o + mi:o + mi + 1], pt_[0:m, 1:TT + 1], ALU.mult, ALU.add,
                [dtk, ptk, 'prm'], [okey])

        def chk(n):
            if stop is not None and stop == n:
                raise _Stop()

        def _main():
          for ti in range(NT if stop != 0 else 0):
            tsl = slice(ti * TT, (ti + 1) * TT)
            S.dma('sp', xT[:], xT_d.rearrange("(k p) t -> p k t", p=128)[:, :, tsl], writes=['xT'])
            for l in range(2):
                rmsnorm_mod(l, A_m, 0)
                dump(f'h{l}', hT[:], D, ti, ['hT'], 'pool')
                ps, pk = proj_fm(l, 0, 16)
                cp('act', zb[0:16, :], ps[0:16, 0:TT], [pk], ['zb'])
                for i in range(2):
                    ps, pk = proj_fm(l, 1 + i, 128)
                    act(qf[:, i, :], ps[:, 0:TT], AF.Copy, [pk], ['qf'], scale=0.125)
                for i in range(2):
                    ps, pk = proj_fm(l, 3 + i, 128)
                    cp('act', kfg[:, i, :], ps[:, 0:TT], [pk], ['kfg'])
                for i in range(4):
                    ps, pk = proj_fm(l, 5 + i, 128)
                    act(sg[:, i, :], ps[:, 0:TT], AF.Silu, [pk], ['sg'])
                chk(1)
                S.dma('sp', wvb[:], s_v[l], reads=[f's_v{l}'], writes=['wvb'])
                for c in range(NCH):
                    ps, pk = bank()
                    for kc in range(8):
                        mm(ps[:, 0:512], hT[:, kc, c * 128:(c + 1) * 128], wvb[:, kc, :], kc == 0, kc == 7,
                           ['hT', 'wvb'], [pk])
                    cp('dve', vtok[:, c, :], ps[:, 0:512], [pk], ['vtok'])
                for c in range(NCH):
                    ps, pk = bank()
                    mm(ps[:, 0:256], zb[0:16, c * 128:(c + 1) * 128], gkup[0:16, l, :], True, True, ['zb', 'gkup'], [pk])
                    f_, fk = ft()
                    tt('dve', f_[:, 0:256], ps[:, 0:256], gkb[:, l, :], ALU.add, [pk, 'gkb'], [fk])
                    act(f_[:, 0:256], f_[:, 0:256], AF.Exp, [fk], [fk], scale=-1.0)
                    act(sptok[:, c, :], f_[:, 0:256], AF.Ln, [fk], ['sptok'], bias=1.0)
                for c2 in range(2):
                    ps, pk = bank()
                    for c in range(NCH):
                        mm(ps[:, c * 128:(c + 1) * 128], sptok[:, c, c2 * 128:(c2 + 1) * 128], tri16, True, True,
                           ['sptok', 'cft'], [pk])
                    act(eG[:, c2, :], ps[:, 0:TT], AF.Exp, [pk], ['eG'], scale=-1.0)
                    act(emG[:], ps[:, 0:TT], AF.Exp, [pk], ['emG'])
                    tt('dve', qt[:, c2, :], qf[:, c2, :], eG[:, c2, :], ALU.mult, ['qf', 'eG'], ['qt'])
                    tt('pool', ktg[:, c2, :], kfg[:, c2, :], emG[:], ALU.mult, ['kfg', 'emG'], ['ktg'])
                    for c in range(NCH):
                        tp, tk_ = tbank()
                        tr(tp[:, 0:128], ktg[:, c2, c * 128:(c + 1) * 128], ['ktg'], [tk_])
                        cp('act', ktgT[:, c, c2 * 128:(c2 + 1) * 128], tp[:, 0:128], [tk_], ['ktgT'])
                for c in range(NCH):
                    csl = slice(c * 128, (c + 1) * 128)
                    for h in range(4):
                        c2 = h // 2; po = (h % 2) * 64
                        ps, pk = bank()
                        mm(ps[:, 0:128], ktg[po:po + 64, c2, csl], qt[po:po + 64, c2, csl], True, True, ['ktg', 'qt'], [pk])
                        tt('dve', ATb[h][:], ps[:, 0:128], m_iu, ALU.mult, [pk, 'cbt'], [f'ATb{h}'])
                        ps2, pk2 = bank()
                        mm(ps2[:, 0:128], vtok[:, c, h * 128:(h + 1) * 128], ATb[h][:], True, False, ['vtok', f'ATb{h}'], [pk2])
                        mm(ps2[:, 0:128], Sgb[po:po + 64, l, c2, :], qt[po:po + 64, c2, csl], False, True, ['Sgb', 'qt'], [pk2])
                        cp('act', osb[:, h, csl], ps2[:, 0:128], [pk2], ['osb'])
                    for c2 in range(2):
                        ps, pk = bank()
                        for hh in range(2):
                            h = c2 * 2 + hh; po = hh * 64
                            mm(ps[po:po + 64, 0:128], ktgT[:, c, c2 * 128 + po:c2 * 128 + po + 64],
                               vtok[:, c, h * 128:(h + 1) * 128], True, True, ['ktgT', 'vtok'], [pk])
                        f_, fk = ft()
                        tt('dve', f_[:, 0:128], ps[:, 0:128], Sg[:, l, c2, :], ALU.add, [pk, 'Sg'], [fk])
                        ts('dve', Sg[:, l, c2, :], f_[:, 0:128], eG[:, c2, c * 128 + 127:c * 128 + 128], ALU.mult,
                           [fk, 'eG'], ['Sg'])
                        cp('pool', Sgb[:, l, c2, :], Sg[:, l, c2, :], ['Sg'], ['Sgb'])
                for h in range(4):
                    f_, fk = ft()
                    act(sq[:, h, :], osb[:, h, :], AF.Square, ['osb'], ['sq'])
                    ps, pk = bank()
                    mm(ps[:, 0:TT], ones, sq[:, h, :], True, True, ['cbt', 'sq'], [pk])
                    act(f_[:], ps[:, 0:TT], AF.Sqrt, [pk], [fk], scale=1.0 / 128, bias=NORM_EPS)
                    recip(f_[:], f_[:], [fk], [fk])
                    tt('dve', f_[:], osb[:, h, :], f_[:], ALU.mult, ['osb', fk], [fk])
                    stt(ob[:, h, :], f_[:], P('gng', l), sg[:, h, :], ALU.mult, ALU.mult, [fk, 'prm', 'sg'], ['ob'])
                if debug:
                    dst = dbg_d.get(f'gla_o{l}')
                    if dst is not None:
                        S.dma('sp', dst.rearrange("(k p) t -> p k t", p=128)[:, :, tsl], osb[:], reads=['osb'],
                              writes=['dbg_g'])

                chk(2)
                ps, pk = proj_fm(l, 9, 128)
                lerp(l, 9, ps, pk, 128, ftmp[0][:], 'ftmp0', 0)
                act(wab[0:64, :], ftmp[0][0:64, :], AF.Tanh, ['ftmp0'], ['wab'])
                cp('pool', wab[64:128, :], ftmp[0][64:128, :], ['ftmp0'], ['wab'])
                ps, pk = proj_fm(l, 10, 128)
                lerp(l, 10, ps, pk, 128, ftmp[1][:], 'ftmp1', 1)
                act(sgl[:], ftmp[1][:], AF.Sigmoid, ['ftmp1'], ['sgl'])
                if l == 1:
                    ps, pk = proj_fm(l, 11, 32)
                    lerp(l, 11, ps, pk, 32, ftmp[2][0:32, :], 'ftmp2', 0)
                    cp('pool', vl[0:32, :], ftmp[2][0:32, :], ['ftmp2'], ['vl'])
                chk(21)
                for c in range(NCH):
                    ps, pk = bank()
                    mm(ps[:, 0:512], wab[0:64, c * 128:(c + 1) * 128], waup[0:64, l, :], True, True, ['wab', 'waup'], [pk])
                    tt('dve', nlw[:, c, :], ps[:, 0:512], w0bc[:, l, :], ALU.add, [pk, 'w0bc'], ['nlw'])
                    act(nlw[:, c, :], nlw[:, c, :], AF.Sigmoid, ['nlw'], ['nlw'])
                chk(22)
                for ct in range(4):
                    pi = ct % 2
                    cts = slice(ct * 128, (ct + 1) * 128)
                    rf, kf, vf = rkv[pi]
                    rk_, kk_, vk_ = f'rkv{pi}_0', f'rkv{pi}_1', f'rkv{pi}_2'
                    ps, pk = proj_fm(l, 12 + 3 * ct, 128)
                    lerp(l, 12 + 3 * ct, ps, pk, 128, rf[:], rk_, 0)
                    ps, pk = proj_fm(l, 13 + 3 * ct, 128)
                    lerp(l, 13 + 3 * ct, ps, pk, 128, kf[:], kk_, 1)
                    ps, pk = proj_fm(l, 14 + 3 * ct, 128)
                    lerp(l, 14 + 3 * ct, ps, pk, 128, vf[:], vk_, 0)
                    ps, pk = bank()
                    for c in range(NCH):
                        mm(ps[:, c * 256:(c + 1) * 256], nlw[:, c, cts], triW, True, True, ['nlw', 'cft'], [pk])
                    psv = ps[:, 0:NCH * 256].rearrange("p (c x) -> p c x", x=256)
                    v3 = lambda ap: ap.rearrange("p (c x) -> p c x", x=128)
                    act(v3(eL[:, ct, :]), psv[:, :, 0:128], AF.Exp, [pk], ['eL'], scale=-1.0)
                    act(v3(emL[pi][:]), psv[:, :, 0:128], AF.Exp, [pk], [f'emL{pi}'])
                    act(v3(eLp[pi][:]), psv[:, :, 128:256], AF.Exp, [pk], [f'eLp{pi}'], scale=-1.0)
                    chk(23)
                    ps, pk = bank()
                    mm(ps[:, 0:TT], waup[64:128, l, cts], wab[64:128, :], True, True, ['waup', 'wab'], [pk])
                    act(asig[pi][:], ps[:, 0:TT], AF.Sigmoid, [pk, 'prm'], [f'asig{pi}'], bias=Pc('a0', l, ct))
                    ps, pk = bank()
                    mm(ps[:, 0:TT], gup[:, l, cts], sgl[:], True, True, ['gup', 'sgl'], [pk])
                    cp('act', gate[:, ct, :], ps[:, 0:TT], [pk], ['gate'])
                    if l == 1:
                        ps, pk = bank()
                        mm(ps[:, 0:TT], vup[0:32, cts], vl[0:32, :], True, True, ['vup', 'vl'], [pk])
                        f_, fk = ft()
                        act(f_[:], ps[:, 0:TT], AF.Sigmoid, [pk, 'prm'], [fk], bias=Pc('v0', 1, ct))
                        f2, fk2 = ft()
                        tt('pool', f2[:], vfirst[:, ct, :], vf[:], ALU.subtract, ['vfirst', vk_], [fk2])
                        tt('pool', f2[:], f2[:], f_[:], ALU.mult, [fk2, fk], [fk2])
                        tt('pool', vf[:], vf[:], f2[:], ALU.add, [vk_, fk2], [vk_])
                    else:
                        cp('pool', vfirst[:, ct, :], vf[:], [vk_], ['vfirst'])
                    chk(24)
                    act(kk2[:], kf[:], AF.Square, [kk_, 'prm'], ['kk2'], scale=Pc('k_k', l, ct))
                    ps, pk = bank()
                    mm(ps[:, 0:TT], bones, kk2[:], True, True, ['cbt', 'kk2'], [pk])
                    f_, fk = ft()
                    ts('dve', f_[:], ps[:, 0:TT], 1e-24, ALU.max, [pk], [fk])
                    act(f_[:], f_[:], AF.Sqrt, [fk], [fk])
                    recip(f_[:], f_[:], [fk], [fk])
                    stt(kkn[pi][:], kf[:], Pc('k_k', l, ct), f_[:], ALU.mult, ALU.mult, [kk_, 'prm', fk], [f'kkn{pi}'])
                    f2, fk2 = ft()
                    ts('dve', f2[:], asig[pi][:], Pc('k_a', l, ct), ALU.mult, [f'asig{pi}', 'prm', 'drv'], [fk2],
                       omk(l, ct), ALU.add)
                    tt('pool', kmod[pi][:], kf[:], f2[:], ALU.mult, [kk_, fk2], [f'kmod{pi}'])
                    tt('dve', ktr[:, ct, :], kmod[pi][:], emL[pi][:], ALU.mult, [f'kmod{pi}', f'emL{pi}'], ['ktr'])
                    f3, fk3 = ft()
                    tt('pool', f3[:], kkn[pi][:], asig[pi][:], ALU.mult, [f'kkn{pi}', f'asig{pi}'], [fk3])
                    tt('dve', btr[:, ct, :], f3[:], emL[pi][:], ALU.mult, [fk3, f'emL{pi}'], ['btr'])
                    arv = ar[:, ct, :].rearrange("p (c x) -> p c x", x=256)
                    stt(arv[:, :, 0:128], v3(kkn[pi][:]), -1.0, v3(eLp[pi][:]), ALU.mult, ALU.mult,
                        [f'kkn{pi}', f'eLp{pi}'], ['ar'])
                    tt('dve', arv[:, :, 128:256], v3(rf[:]), v3(eL[:, ct, :]), ALU.mult, [rk_, 'eL'], ['ar'])
                    chk(25)
                    stt(rkb[:], rf[:], Pc('r_k', l, ct), kmod[pi][:], ALU.mult, ALU.mult, [rk_, 'prm', f'kmod{pi}'], ['rkb'])
                    ps, pk = bank()
                    mm(ps[:, 0:TT], bones, rkb[:], True, True, ['cbt', 'rkb'], [pk])
                    tt('dve', bonus[:, ct, :], ps[:, 0:TT], vf[:], ALU.mult, [pk, vk_], ['bonus'])
                    if debug and ct == 0:
                        for nm_, ap_, k_ in (('d_rf', rf[:], rk_), ('d_kf', kf[:], kk_), ('d_vf', vf[:], vk_), ('d_emL', emL[pi][:], f'emL{pi}'),
                                             ('d_kmod', kmod[pi][:], f'kmod{pi}'), ('d_kkn', kkn[pi][:], f'kkn{pi}'), ('d_asig', asig[pi][:], f'asig{pi}'),
                                             ('d_eL', eL[:, ct, :], 'eL'), ('d_eLp', eLp[pi][:], f'eLp{pi}'), ('d_ktr', ktr[:, ct, :], 'ktr'), ('d_btr', btr[:, ct, :], 'btr')):
                            S.dma('pool', dbg_d[nm_][:, tsl], ap_, reads=[k_], writes=['dbg_' + nm_])
                    chk(251)
                    cp('pool', vb[:], vf[:], [vk_], ['vb'])
                    chk(252)
                    for c in range(NCH):
                        csl = slice(c * 128, (c + 1) * 128)
                        for ii, (src_, sk_) in enumerate(((vb[:, csl], 'vb'), (ktr[:, ct, csl], 'ktr'), (btr[:, ct, csl], 'btr'))):
                            tp, tk_ = tbank()
                            import os
                            if str(ii) not in os.environ.get('XII', '012'):
                                continue
                            if os.environ.get('XTR', '1') == '1':
                                tr(tp[:, 0:128], src_, [sk_], [tk_])
                            if os.environ.get('XCP', '1') == '1':
                                cp(os.environ.get('XENG', 'act'), tokT[:, c, ct, ii * 128:(ii + 1) * 128], tp[:, 0:128], [tk_], ['tokT'])
                    chk(26 + ct)

                chk(3)
                for c in range(NCH):
                    csl = slice(c * 128, (c + 1) * 128)
                    for ct in range(4):
                        for hh in range(2):
                            h = ct * 2 + hh; po = hh * 64
                            kt_h = ktr[po:po + 64, ct, csl]; bt_h = btr[po:po + 64, ct, csl]
                            ar_h = ar[po:po + 64, ct, c * 256:(c + 1) * 256]
                            at_h = ar[po:po + 64, ct, c * 256:c * 256 + 128]
                            ps, pk = bank()
                            mm(ps[:, 0:256], kt_h, ar_h, True, True, ['ktr', 'ar'], [pk])
                            mm(ps[:, 256:512], bt_h, ar_h, True, True, ['btr', 'ar'], [pk])
                            ps2, pk2 = bank()
                            mm(ps2[:, 0:128], at_h, bt_h, True, True, ['ar', 'btr'], [pk2])
                            tt('dve', MN[h][:], ps[:, 0:256], m_suiu, ALU.mult, [pk, 'cbt'], [f'MN{h}'])
                            tt('dve', W0[h][:, 128:256], ps[:, 256:384], m_su, ALU.mult, [pk, 'cbt'], [f'W0_{h}'])
                            tt('dve', Nbr[h][:], ps[:, 384:512], m_iu, ALU.mult, [pk, 'cbt'], [f'Nbr{h}'])
                            tt('dve', W0[h][:, 0:128], ps2[:, 0:128], m_sl, ALU.mult, [pk2, 'cbt'], [f'W0_{h}'])
                    for k in range(7):
                        for h in range(8):
                            cur, ck = (W0[h], f'W0_{h}') if k == 0 else (Wpp[h][(k - 1) % 2], f'Wpp{h}_{(k - 1) % 2}')
                            ps, pk = bank()
                            if k < 6:
                                nxt, nk = Wpp[h][k % 2], f'Wpp{h}_{k % 2}'
                                mm(ps[:, 0:128], cur[:, 128:256], cur[:, 0:128], True, True, [ck], [pk])
                                mm(ps[:, 128:384], cur[:, 0:128], cur[:, 128:384], True, True, [ck], [pk])
                                cp('act', nxt[:, 0:256], ps[:, 0:256], [pk], [nk])
                                tt('dve', nxt[:, 256:384], ps[:, 256:384], cur[:, 256:384], ALU.add, [pk, ck], [nk])
                            else:
                                mm(ps[:, 0:128], cur[:, 0:128], cur[:, 256:384], True, True, [ck], [pk])
                                tt('dve', Tfin[h][:], ps[:, 0:128], cur[:, 256:384], ALU.add, [pk, ck], [f'Tfin{h}'])
                                psn, pkn = bank()
                                mm(psn[:, 0:128], W0[h][:, 0:128], Tfin[h][:], True, True, [f'W0_{h}', f'Tfin{h}'], [pkn])
                                fn_, fnk = ft()
                                tt('dve', fn_[:, 0:128], psn[:, 0:128], Tfin[h][:], ALU.subtract, [pkn, f'Tfin{h}'], [fnk])
                                tt('pool', Wpp[h][0][:, 0:128], fn_[:, 0:128], ident, ALU.add, [fnk, 'cbt'], [f'Wpp{h}_0'])
                    psx = {}
                    for ct in range(4):
                        ps, pk = bank(); psx[ct] = (ps, pk)
                        for hh in range(2):
                            h = ct * 2 + hh; po = hh * 64
                            at_h = ar[po:po + 64, ct, c * 256:c * 256 + 128]
                            mm(ps[:, hh * 64:(hh + 1) * 64], at_h, Srb[po:po + 64, l, ct, :], True, False, ['ar', 'Srb'], [pk])
                            mm(ps[:, hh * 64:(hh + 1) * 64], MN[h][:, 0:128], tokT[:, c, ct, po:po + 64], False, True,
                               [f'MN{h}', 'tokT'], [pk])
                    for ct in range(4):
                        ps, pk = psx[ct]
                        cp('act', XTb[ct][:], ps[:, 0:128], [pk], [f'XTb{ct}'])
                    for ct in range(4):
                        ps, pk = bank(); psx[ct] = (ps, pk)
                        for hh in range(2):
                            h = ct * 2 + hh
                            mm(ps[:, hh * 64:(hh + 1) * 64], Tfin[h][:], XTb[ct][:, hh * 64:(hh + 1) * 64], True, True,
                               [f'Tfin{h}', f'XTb{ct}'], [pk])
                    for ct in range(4):
                        ps, pk = psx[ct]
                        cp('act', UTb[ct][:], ps[:, 0:128], [pk], [f'UTb{ct}'])
                    for ct in range(4):
                        ps, pk = bank(); psx[ct] = (ps, pk)
                        for hh in range(2):
                            h = ct * 2 + hh
                            mm(ps[:, hh * 64:(hh + 1) * 64], Wpp[h][0][:, 0:128], UTb[ct][:, hh * 64:(hh + 1) * 64], True, True,
                               [f'Wpp{h}_0', f'UTb{ct}'], [pk])
                    for ct in range(4):
                        ps, pk = psx[ct]
                        tt('dve', XTb[ct][:], ps[:, 0:128], UTb[ct][:], ALU.add, [pk, f'UTb{ct}'], [f'XTb{ct}'])
                    for ct in range(4):
                        ps, pk = bank()
                        for hh in range(2):
                            h = ct * 2 + hh; po = hh * 64
                            rt_h = ar[po:po + 64, ct, c * 256 + 128:(c + 1) * 256]
                            o_ = ps[po:po + 64, 0:128]
                            mm(o_, Srb[po:po + 64, l, ct, :], rt_h, True, False, ['Srb', 'ar'], [pk])
                            mm(o_, tokT[:, c, ct, po:po + 64], MN[h][:, 128:256], False, False, ['tokT', f'MN{h}'], [pk])
                            mm(o_, XTb[ct][:, hh * 64:(hh + 1) * 64], Nbr[h][:], False, True, [f'XTb{ct}', f'Nbr{h}'], [pk])
                        cp('act', ysb[:, ct, csl], ps[:, 0:128], [pk], ['ysb'])
                        ps2, pk2 = bank()
                        for hh in range(2):
                            po = hh * 64
                            o_ = ps2[po:po + 64, 0:64]
                            mm(o_, tokT[:, c, ct, 128 + po:128 + po + 64], tokT[:, c, ct, po:po + 64], True, False, ['tokT'], [pk2])
                            mm(o_, tokT[:, c, ct, 256 + po:256 + po + 64], XTb[ct][:, hh * 64:(hh + 1) * 64], False, True,
                               ['tokT', f'XTb{ct}'], [pk2])
                        f_, fk = ft()
                        tt('dve', f_[:, 0:64], ps2[:, 0:64], Sr[:, l, ct, :], ALU.add, [pk2, 'Sr'], [fk])
                        ts('dve', Sr[:, l, ct, :], f_[:, 0:64], eL[:, ct, c * 128 + 127:c * 128 + 128], ALU.mult,
                           [fk, 'eL'], ['Sr'])
                        cp('pool', Srb[:, l, ct, :], Sr[:, l, ct, :], ['Sr'], ['Srb'])
                if debug:
                    dst = dbg_d.get(f'rwkv_y{l}')
                    if dst is not None:
                        S.dma('sp', dst.rearrange("(k p) t -> p k t", p=128)[:, :, tsl], ysb[:], reads=['ysb'],
                              writes=['dbg_y'])
                chk(4)
                for ct in range(4):
                    f_, fk = ft(); f2, fk2 = ft()
                    cp('pool', vb[:], ysb[:, ct, :], ['ysb'], ['vb'])
                    ps, pk = bank()
                    mm(ps[:, 0:TT], bones64, vb[:], True, True, ['cbt', 'vb'], [pk])
                    tt('dve', f_[:], ysb[:, ct, :], ps[:, 0:TT], ALU.subtract, ['ysb', pk], [fk])
                    act(kk2[:], f_[:], AF.Square, [fk], ['kk2'])
                    ps, pk = bank()
                    mm(ps[:, 0:TT], bones64, kk2[:], True, True, ['cbt', 'kk2'], [pk])
                    act(f2[:], ps[:, 0:TT], AF.Sqrt, [pk], [fk2], bias=GN_EPS)
                    recip(f2[:], f2[:], [fk2], [fk2])
                    tt('dve', f_[:], f_[:], f2[:], ALU.mult, [fk, fk2], [fk])
                    ts('dve', f_[:], f_[:], Pc('gn_g', l, ct), ALU.mult, [fk, 'prm'], [fk], Pc('gn_b', l, ct), ALU.add)
                    tt('pool', f_[:], f_[:], bonus[:, ct, :], ALU.add, [fk, 'bonus'], [fk])
                    tt('dve', ob[:, 4 + ct, :], f_[:], gate[:, ct, :], ALU.mult, [fk, 'gate'], ['ob'])
                dump(f'oo{l}', ob[:], D, ti, ['ob'], 'pool')

                for j in range(8):
                    slot, sk = loadA(s_o[l, j], f's_o{l}')
                    ps, pk = bank()
                    for kc in range(8):
                        mm(ps[:, 0:TT], slot[:, kc, :], ob[:, kc, :], kc == 0, kc == 7, [sk, 'ob'], [pk])
                    stt(xT[:, j, :], ps[:, 0:TT], modc(l, 2, j), xT[:, j, :], ALU.mult, ALU.add, [pk, 'modt', 'xT'], ['xT'])
                dump(f'xmix{l}', xT[:], D, ti, ['xT'])

                chk(5)
                rmsnorm_mod(l, A_f, 3)
                par = ti % 2
                for j in range(NFF):
                    accs = []
                    for half in range(2):
                        u = 2 * j + half
                        slot, sk = loadA(s_u[l, u], f's_u{l}')
                        ps, pk = bank()
                        for kc in range(8):
                            mm(ps[:, 0:TT], slot[:, kc, :], hT[:, kc, :], kc == 0, kc == 7, [sk, 'hT'], [pk])
                        a_ = acc[half * 2 + (j % 2)]; ak = acck[half * 2 + (j % 2)]
                        o, w = POFF[f'cw{l}']
                        cw = lambda tap: prm[:, o + u * 3 + tap:o + u * 3 + tap + 1]
                        ob_, wb_ = POFF[f'cb{l}']
                        ub, ubk = ptmp[half], f'ptmp{half}'
                        cp('act', ub[:, 2:TT + 2], ps[:, 0:TT], [pk], [ubk])
                        cp('pool', ub[:, 0:2], chalo[:, par, l, u, :], ['chalo'], [ubk])
                        ts('dve', a_[:], ub[:, 2:TT + 2], cw(2), ALU.mult, [ubk, 'prm'], [ak], prm[:, ob_ + u:ob_ + u + 1], ALU.add)
                        cp('pool', chalo[:, 1 - par, l, u, :], ub[:, TT:TT + 2], [ubk], ['chalo'])
                        stt(a_[:], ub[:, 1:TT + 1], cw(1), a_[:], ALU.mult, ALU.add, [ubk, 'prm', ak], [ak])
                        stt(a_[:], ub[:, 0:TT], cw(0), a_[:], ALU.mult, ALU.add, [ubk, 'prm', ak], [ak])
                        accs.append((a_, ak))
                    s_, skk = sgt[j % 2], sgtk[j % 2]
                    act(s_[:], accs[0][0][:], AF.Silu, [accs[0][1]], [skk])
                    tt('pool', mbuf[:, j, :], s_[:], accs[1][0][:], ALU.mult, [skk, accs[1][1]], [f'mbuf{j}'])
                for jo in range(8):
                    slot, sk = loadD(s_d[l, jo], f's_d{l}')
                    ps, pk = bank()
                    for kc in range(NFF):
                        mm(ps[:, 0:TT], slot[:, kc, :], mbuf[:, kc, :], kc == 0, kc == NFF - 1, [sk, f'mbuf{kc}'], [pk])
                    stt(xT[:, jo, :], ps[:, 0:TT], modc(l, 5, jo), xT[:, jo, :], ALU.mult, ALU.add, [pk, 'modt', 'xT'], ['xT'])
                dump(f'x{l}', xT[:], D, ti, ['xT'])
                chk(51 + l)

            chk(6)
            act(sq[:], xT[:], AF.Square, ['xT'], ['sq'])
            ps, pk = bank()
            for kc in range(8):
                mm(ps[:, 0:TT], ones, sq[:, kc, :], kc == 0, kc == 7, ['cbt', 'sq'], [pk])
            act(rstd[:], ps[:, 0:TT], AF.Sqrt, [pk], ['rstd'], scale=1.0 / D, bias=NORM_EPS)
            recip(rstd[:], rstd[:], ['rstd'], ['rstd'])
            fo, fw = POFF['final_g']
            for kc in range(8):
                dst_, dk_ = (osb, 'osb') if kc < 4 else (ysb, 'ysb')
                stt(dst_[:, kc % 4, :], xT[:, kc, :], prm[:, fo + kc:fo + kc + 1], rstd[:], ALU.mult, ALU.mult,
                    ['xT', 'prm', 'rstd'], [dk_])
            odv = out_d.rearrange("(k p) t -> p k t", p=128)
            S.dma('sp', odv[:, 0:4, tsl], osb[:], reads=['osb'], writes=['out_d'])
            S.dma('sp', odv[:, 4:8, tsl], ysb[:], reads=['ysb'], writes=['out_d'])
        try:
            _main()
        except _Stop:
            pass
        S.barrier()
        print(f"[kernel] T={T} instructions={S.n_inst} waits={S.n_wait}", flush=True)
    return nc


def core_inputs(inputs, shared, b, T):
    m = dict(shared)
    m['xT'] = np.ascontiguousarray(np.asarray(inputs['x'], np.float32)[b, :T].T)
    P = shared['params'].copy()
    o, w = POFF['c']
    P[:, o:o + w] = _fm(np.asarray(inputs['c'], np.float32)[b])
    m['params'] = P
    return m


def kernel(**inputs):
    T = SEQ
    shared = prep_shared(inputs)
    nc = build_nc(T)
    in_maps = [core_inputs(inputs, shared, c // 2, T) for c in range(8)]
    res = run_bass_kernel_spmd(nc, in_maps, core_ids=list(range(8)))
    out = np.zeros((4, T, D), np.float32)
    H = T // 2
    for c in range(8):
        b, s = c // 2, c % 2
        o = res.results[c]["outT"]
        out[b, s * H:(s + 1) * H] = o[:, s * H:(s + 1) * H].T
    return out
```

```python
import contextlib
import numpy as np
import concourse.bass as bass
import concourse.mybir as mybir
from concourse.bass_utils import run_bass_kernel_spmd

F32 = mybir.dt.float32
BF16 = mybir.dt.bfloat16
AF = mybir.ActivationFunctionType
ALU = mybir.AluOpType

D = 1024
NKC = 8
TT = 256
C = 128
NCH = TT // C
NFF = 22
SEQ = 8192
GN_EPS = 64e-5
NORM_EPS = 1e-6
NFM = 24
RW0 = 9

def _playout():
    off = {}
    cur = 0
    def add(name, w):
        nonlocal cur
        off[name] = (cur, w)
        cur += w
    add('c', 8)
    add('final_g', 8)
    for l in range(2):
        add(f'ada_b{l}', 48)
        add(f'nmg{l}', 8)
        add(f'nfg{l}', 8)
        add(f'mu{l}', 15)
        for nm in ('w0', 'a0', 'v0', 'k_k', 'k_a', 'r_k', 'gn_g', 'gn_b'):
            add(f'{nm}{l}', 4)
        add(f'gng{l}', 1)
        add(f'cw{l}', 132)
        add(f'cb{l}', 44)
    return off, cur

POFF, NPARAM = _playout()


def _fm(vec):
    v = np.asarray(vec, np.float32).reshape(-1)
    n = v.shape[0] // 128
    return v.reshape(n, 128).T


def _wt(cols):
    K, w = cols.shape
    return cols.reshape(K // 128, 128, w).transpose(1, 0, 2)


def prep_shared(inp):
    f = lambda a: np.asarray(a, np.float32)
    w_in = f(inp['w_in']); w_vres = f(inp['w_in_vres'])
    sh = {}
    w_fm = np.zeros((2, NFM, 128, 8, 128), np.float32)
    RB = 1552
    for l in range(2):
        W = w_in[l]
        def put(idx, cols):
            w_fm[l, idx, :, :, :cols.shape[1]] = _wt(cols)
        put(0, W[:, 1536:1552])
        put(1, W[:, 0:128]); put(2, W[:, 128:256]); put(3, W[:, 256:384]); put(4, W[:, 384:512])
        for i in range(4):
            put(5 + i, W[:, 1024 + i * 128:1024 + (i + 1) * 128])
        put(9, W[:, RB + 1536:RB + 1664])
        put(10, W[:, RB + 1664:RB + 1792])
        if l == 1:
            put(11, w_vres[0])
        for ct in range(4):
            put(12 + 3 * ct, W[:, RB + ct * 128:RB + (ct + 1) * 128])
            put(13 + 3 * ct, W[:, RB + 512 + ct * 128:RB + 512 + (ct + 1) * 128])
            put(14 + 3 * ct, W[:, RB + 1024 + ct * 128:RB + 1024 + (ct + 1) * 128])
    sh['w_fm'] = w_fm
    sh['w_v'] = np.stack([_wt(w_in[l][:, 512:1024]) for l in range(2)])
    w_out = f(inp['w_out'])
    sh['w_o'] = np.stack([np.stack([_wt(w_out[l][:, j * 128:(j + 1) * 128]) for j in range(8)]) for l in range(2)])
    fu = f(inp['ffn_up'])
    wu = np.zeros((2, 44, 128, 8, 128), np.float32)
    for l in range(2):
        for j in range(NFF):
            wu[l, 2 * j] = _wt(fu[l][:, j * 128:(j + 1) * 128])
            wu[l, 2 * j + 1] = _wt(fu[l][:, 2816 + j * 128:2816 + (j + 1) * 128])
    sh['w_u'] = wu
    fd = f(inp['ffn_down'])
    sh['w_d'] = np.stack([np.stack([_wt(fd[l][:, j * 128:(j + 1) * 128]) for j in range(8)]) for l in range(2)])
    sh['ada'] = np.stack([_wt(f(inp['ada_w'])[l]) for l in range(2)])
    sh['wa_up'] = np.stack([np.concatenate([f(inp['w_lora_up'])[l], f(inp['a_lora_up'])[l]], 0) for l in range(2)])
    sh['g_up'] = f(inp['g_lora_up'])
    vup = np.zeros((128, 512), np.float32); vup[:32] = f(inp['v_lora_up'])[0]
    sh['v_up'] = vup
    gku = np.zeros((2, 128, 256), np.float32); gku[:, :16] = f(inp['gla_gk_up'])
    sh['gk_up'] = gku
    sh['gkb_bc'] = np.stack([np.broadcast_to(f(inp['gla_gk_b'])[l], (128, 256)) for l in range(2)]).copy()
    sh['w0_bc'] = np.stack([np.broadcast_to(f(inp['w0'])[l], (128, 512)) for l in range(2)]).copy()
    j = np.arange(128)[:, None]; t = np.arange(128)[None, :]
    su = (j < t).astype(np.float32); iu = (j <= t).astype(np.float32); sl = (j > t).astype(np.float32)
    eye = np.eye(128, dtype=np.float32)
    bones = np.zeros((128, 128), np.float32); bones[:64, :64] = 1; bones[64:, 64:] = 1
    cb = np.concatenate([eye, su, iu, sl, np.ones((128, 128), np.float32), bones, bones / 64.0], 1)
    sh['constb'] = cb
    ew = np.float32(np.exp(-0.5))
    sh['constf'] = np.concatenate([iu * ew, su * ew, iu / 16.0], 1).astype(np.float32)
    P = np.zeros((128, NPARAM), np.float32)
    def setp(name, arr):
        o, w = POFF[name]
        assert arr.shape == (128, w), (name, arr.shape, w)
        P[:, o:o + w] = arr
    setp('final_g', _fm(inp['final_g']))
    for l in range(2):
        setp(f'ada_b{l}', _fm(f(inp['ada_b'])[l]))
        setp(f'nmg{l}', _fm(f(inp['norm_mix_g'])[l]))
        setp(f'nfg{l}', _fm(f(inp['norm_ffn_g'])[l]))
        mu = f(inp['rwkv_mu'])[l]
        m = np.zeros((128, 15), np.float32)
        m[:, 0] = mu[1536:1664]; m[:, 1] = mu[1664:1792]
        if l == 1:
            m[:32, 2] = f(inp['rwkv_mu_vres'])[0]
        for ct in range(4):
            m[:, 3 + 3 * ct] = mu[ct * 128:(ct + 1) * 128]
            m[:, 4 + 3 * ct] = mu[512 + ct * 128:512 + (ct + 1) * 128]
            m[:, 5 + 3 * ct] = mu[1024 + ct * 128:1024 + (ct + 1) * 128]
        setp(f'mu{l}', m)
        for nm, key in (('w0', 'w0'), ('a0', 'a0'), ('k_k', 'k_k'), ('k_a', 'k_a'), ('gn_g', 'gn_g'), ('gn_b', 'gn_b')):
            setp(f'{nm}{l}', _fm(f(inp[key])[l]))
        setp(f'r_k{l}', _fm(f(inp['r_k'])[l].reshape(-1)))
        if l == 1:
            setp('v01', _fm(f(inp['v0'])[0]))
        setp(f'gng{l}', f(inp['gla_norm_g'])[l].reshape(128, 1))
        cw = f(inp['ffn_conv_w'])[l]; cbv = f(inp['ffn_conv_b'])[l]
        cwp = np.zeros((128, 44, 3), np.float32); cbp = np.zeros((128, 44), np.float32)
        for jj in range(NFF):
            for half in range(2):
                cols = slice(half * 2816 + jj * 128, half * 2816 + (jj + 1) * 128)
                cwp[:, 2 * jj + half, :] = cw[:, cols].T
                cbp[:, 2 * jj + half] = cbv[cols]
        setp(f'cw{l}', cwp.reshape(128, 132))
        setp(f'cb{l}', cbp)
    sh['params'] = P
    return sh


class Sched:
    def __init__(self, nc, es, ndma=8):
        self.nc = nc
        self.eng = {'pe': nc.tensor, 'act': nc.scalar, 'dve': nc.vector, 'pool': nc.gpsimd, 'sp': nc.sync}
        self.sem = {e: es.enter_context(nc.semaphore('s_' + e)) for e in ('pe', 'act', 'dve', 'pool')}
        self.cnt = {e: 0 for e in self.sem}
        self.dq = {}
        for q in ('sp', 'pool'):
            self.dq[q] = {'sems': [es.enter_context(nc.semaphore(f'd_{q}{i}')) for i in range(ndma)],
                          'cnt': [0] * ndma, 'nxt': 0}
        self.seen = {e: {} for e in self.eng}
        self.lastw = {}
        self.rd = {}
        self.n_inst = 0
        self.n_wait = 0

    def _semobj(self, sk):
        if sk[0] == 'e':
            return self.sem[sk[1]]
        return self.dq[sk[1]]['sems'][sk[2]]

    def _wait(self, eng, tk):
        sk, val, src = tk
        if self.seen[eng].get(sk, 0) >= val:
            return
        self.eng[eng].wait_ge(self._semobj(sk), val)
        self.seen[eng][sk] = val
        self.n_wait += 1

    def _deps(self, eng, reads, writes):
        deps = []
        for k in reads:
            if k in self.lastw:
                deps.append(self.lastw[k])
        for k in writes:
            if k in self.lastw:
                deps.append(self.lastw[k])
            for tk in self.rd.get(k, {}).values():
                deps.append(tk)
        for tk in deps:
            if eng == 'pe' and tk[2] == 'pe':
                continue
            self._wait(eng, tk)

    def _record(self, tk, reads, writes):
        for k in writes:
            self.lastw[k] = tk
            self.rd[k] = {}
        for k in reads:
            self.rd.setdefault(k, {})[tk[0]] = tk

    def emit(self, eng, fn, reads=(), writes=()):
        self._deps(eng, reads, writes)
        inst = fn()
        self.cnt[eng] += 1
        inst.then_inc(self.sem[eng], 1)
        self._record((('e', eng), self.cnt[eng], eng), reads, writes)
        self.n_inst += 1

    def dma(self, q, out, in_, reads=(), writes=()):
        Q = self.dq[q]
        i = Q['nxt']; Q['nxt'] = (i + 1) % len(Q['sems'])
        sk = ('d', q, i)
        if Q['cnt'][i] > 0:
            self._wait(q, (sk, Q['cnt'][i], 'dma'))
        self._deps(q, reads, writes)
        inst = self.eng[q].dma_start(out=out, in_=in_)
        Q['cnt'][i] += 16
        inst.then_inc(Q['sems'][i], 16)
        self._record((sk, Q['cnt'][i], 'dma'), reads, writes)
        self.n_inst += 1

    def barrier(self):
        for e in self.eng:
            for s in self.sem:
                if self.cnt[s] > 0:
                    self._wait(e, (('e', s), self.cnt[s], s))
            for q, Q in self.dq.items():
                for i, cval in enumerate(Q['cnt']):
                    if cval > 0:
                        self._wait(e, (('d', q, i), cval, 'dma'))


class _Stop(Exception):
    pass


def build_nc(T, debug=False, stop=None):
    NT = T // TT
    nc = bass.Bass("TRN2", target_bir_lowering=False)
    dr = lambda name, shape, dt=F32, kind="ExternalInput": nc.dram_tensor(name, list(shape), dt, kind=kind).ap()
    xT_d = dr("xT", [D, T])
    params_d = dr("params", [128, NPARAM])
    ada_d = dr("ada", [2, 128, 8, 6144])
    wfm_d = dr("w_fm", [2, NFM, 128, 8, 128])
    wv_d = dr("w_v", [2, 128, 8, 512])
    wo_d = dr("w_o", [2, 8, 128, 8, 128])
    wu_d = dr("w_u", [2, 44, 128, 8, 128])
    wd_d = dr("w_d", [2, 8, 128, NFF, 128])
    waup_d = dr("wa_up", [2, 128, 512])
    gup_d = dr("g_up", [2, 128, 512])
    vup_d = dr("v_up", [128, 512])
    gkup_d = dr("gk_up", [2, 128, 256])
    gkb_d = dr("gkb_bc", [2, 128, 256])
    w0bc_d = dr("w0_bc", [2, 128, 512])
    cb_d = dr("constb", [128, 7 * 128])
    cf_d = dr("constf", [128, 3 * 128])
    out_d = dr("outT", [D, T], F32, "ExternalOutput")
    s_fm = dr("s_fm", [2, NFM, 128, 8, 128], BF16, "Internal")
    s_v = dr("s_v", [2, 128, 8, 512], BF16, "Internal")
    s_o = dr("s_o", [2, 8, 128, 8, 128], BF16, "Internal")
    s_u = dr("s_u", [2, 44, 128, 8, 128], BF16, "Internal")
    s_d = dr("s_d", [2, 8, 128, NFF, 128], BF16, "Internal")
    dbg_d = {}
    if debug:
        for nm in ('d_rf', 'd_kf', 'd_vf', 'd_emL', 'd_kmod', 'd_kkn', 'd_asig', 'd_eL', 'd_eLp', 'd_ktr', 'd_btr'):
            dbg_d[nm] = dr("dbg_" + nm, [128, T], F32, "ExternalOutput")
        for nm in ('h0', 'oo0', 'xmix0', 'x0', 'h1', 'oo1', 'xmix1', 'x1', 'gla_o0', 'rwkv_y0'):
            rows = 512 if nm.startswith(('gla_o', 'rwkv_y')) else D
            dbg_d[nm] = dr("dbg_" + nm, [rows, T], F32, "ExternalOutput")

    with contextlib.ExitStack() as es:
        E = es.enter_context
        S = Sched(nc, es)
        sb = lambda name, shape, dt=F32: E(nc.sbuf_tensor("sb_" + name, list(shape), dt))

        prm = sb("prm", [128, NPARAM])
        cbt = sb("cbt", [128, 7 * 128], BF16)
        cft = sb("cft", [128, 3 * 128])
        waup = sb("waup", [128, 2, 512], BF16)
        gup = sb("gup", [128, 2, 512], BF16)
        vup = sb("vup", [128, 512], BF16)
        gkup = sb("gkup", [128, 2, 256], BF16)
        gkb = sb("gkb", [128, 2, 256])
        w0bc = sb("w0bc", [128, 2, 512])
        modt = sb("modt", [128, 2, 48])
        drv = sb("drv", [128, 2, 40])
        sc = sb("sc", [128, 8])
        ident = cbt[:, 0:128]; m_su = cbt[:, 128:256]; m_iu = cbt[:, 256:384]; m_sl = cbt[:, 384:512]
        m_suiu = cbt[:, 128:384]
        ones = cbt[:, 512:640]; bones = cbt[:, 640:768]; bones64 = cbt[:, 768:896]
        triW = cft[:, 0:256]; tri16 = cft[:, 256:384]

        def P(name, l=None):
            o, w = POFF[name if l is None else f'{name}{l}']
            return prm[:, o:o + w]

        def Pc(name, l, i):
            o, w = POFF[f'{name}{l}']
            return prm[:, o + i:o + i + 1]

        S.dma('sp', prm[:], params_d[:, :], writes=['prm'])
        S.dma('pool', cbt[:], cb_d[:, :], writes=['cbt'])
        S.dma('sp', cft[:], cf_d[:, :], writes=['cft'])
        for l in range(2):
            S.dma('pool', waup[:, l, :], waup_d[l], writes=['waup'])
            S.dma('pool', gup[:, l, :], gup_d[l], writes=['gup'])
            S.dma('pool', gkup[:, l, :], gkup_d[l], writes=['gkup'])
            S.dma('sp', gkb[:, l, :], gkb_d[l], writes=['gkb'])
            S.dma('sp', w0bc[:, l, :], w0bc_d[l], writes=['w0bc'])
        S.dma('pool', vup[:], vup_d[:, :], writes=['vup'])

        for l in range(2):
            for j in range(NFM):
                S.dma('pool', s_fm[l, j], wfm_d[l, j], writes=[f's_fm{l}'])
            for kc in range(8):
                S.dma('pool', s_v[l, :, kc, :], wv_d[l, :, kc, :], writes=[f's_v{l}'])
            for j in range(8):
                S.dma('pool', s_o[l, j], wo_d[l, j], writes=[f's_o{l}'])
            for j in range(44):
                S.dma('pool', s_u[l, j], wu_d[l, j], writes=[f's_u{l}'])
            for j in range(8):
                for h2 in range(2):
                    S.dma('pool', s_d[l, j, :, h2 * 11:(h2 + 1) * 11, :], wd_d[l, j, :, h2 * 11:(h2 + 1) * 11, :],
                          writes=[f's_d{l}'])

        NB = 6
        banks = [E(nc.psum_tensor(f"pb{i}", [128, 512], F32)) for i in range(NB)]
        ptbs = [E(nc.psum_tensor(f"ptb{i}", [128, 1024], BF16)) for i in range(2)]
        st = {'b': 0, 't': 0}

        def bank():
            i = st['b']; st['b'] = (i + 1) % NB
            return banks[i], f'pb{i}'

        def tbank():
            i = st['t']; st['t'] = (i + 1) % 2
            return ptbs[i][:, 0:512], f'pt{i}'

        def mm(out, lhsT, rhs, start, stop, r, w):
            S.emit('pe', lambda: nc.tensor.matmul(out, lhsT, rhs, start=start, stop=stop), r, w)

        def tr(out, in_, r, w):
            S.emit('pe', lambda: nc.tensor.transpose(out, in_, ident), list(r) + ['cbt'], w)

        def act(out, in_, func, r, w, scale=1.0, bias=0.0):
            S.emit('act', lambda: nc.scalar.activation(out=out, in_=in_, func=func, bias=bias, scale=scale), r, w)

        def tt(eng, out, in0, in1, op, r, w):
            e = nc.vector if eng == 'dve' else nc.gpsimd
            S.emit(eng, lambda: e.tensor_tensor(out=out, in0=in0, in1=in1, op=op), r, w)

        def ts(eng, out, in0, s1, op0, r, w, s2=None, op1=None):
            e = nc.vector if eng == 'dve' else nc.gpsimd
            if op1 is None:
                S.emit(eng, lambda: e.tensor_scalar(out=out, in0=in0, scalar1=s1, scalar2=None, op0=op0), r, w)
            else:
                S.emit(eng, lambda: e.tensor_scalar(out=out, in0=in0, scalar1=s1, scalar2=s2, op0=op0, op1=op1), r, w)

        def stt(out, in0, scalar, in1, op0, op1, r, w):
            S.emit('dve', lambda: nc.vector.scalar_tensor_tensor(out=out, in0=in0, scalar=scalar, in1=in1,
                                                                  op0=op0, op1=op1), r, w)

        def cp(eng, out, in_, r, w):
            if eng == 'act':
                S.emit('act', lambda: nc.scalar.activation(out=out, in_=in_, func=AF.Copy), r, w)
            else:
                e = nc.vector if eng == 'dve' else nc.gpsimd
                S.emit(eng, lambda: e.tensor_copy(out=out, in_=in_), r, w)

        def recip(out, in_, r, w):
            S.emit('dve', lambda: nc.vector.reciprocal(out=out, in_=in_), r, w)

        def memz(eng, ap, w):
            e = nc.vector if eng == 'dve' else nc.gpsimd
            S.emit(eng, lambda: e.memset(ap, 0.0), [], w)

        act(sc[:], P('c'), AF.Silu, ['prm'], ['sc'])
        with contextlib.ExitStack() as es2:
            adat = es2.enter_context(nc.sbuf_tensor("adat", [128, 8, 1024], F32))
            for l in range(2):
                for g6 in range(6):
                    S.dma('sp', adat[:], ada_d[l, :, :, g6 * 1024:(g6 + 1) * 1024], writes=['adat'])
                    ps, pk = bank()
                    for j in range(8):
                        for kc in range(8):
                            mm(ps[:, j:j + 1], adat[:, kc, j * 128:(j + 1) * 128], sc[:, kc:kc + 1],
                               kc == 0, kc == 7, ['adat', 'sc'], [pk])
                    tt('dve', modt[:, l, g6 * 8:(g6 + 1) * 8], ps[:, 0:8], P('ada_b', l)[:, g6 * 8:(g6 + 1) * 8],
                       ALU.add, [pk, 'prm'], ['modt'])
            S.barrier()
        for l in range(2):
            ts('dve', drv[:, l, 0:8], modt[:, l, 8:16], 1.0, ALU.add, ['modt'], ['drv'])
            tt('dve', drv[:, l, 0:8], drv[:, l, 0:8], P('nmg', l), ALU.mult, ['drv', 'prm'], ['drv'])
            ts('dve', drv[:, l, 8:16], modt[:, l, 32:40], 1.0, ALU.add, ['modt'], ['drv'])
            tt('dve', drv[:, l, 8:16], drv[:, l, 8:16], P('nfg', l), ALU.mult, ['drv', 'prm'], ['drv'])
            ts('dve', drv[:, l, 16:20], P('k_a', l), -1.0, ALU.mult, ['prm'], ['drv'], 1.0, ALU.add)
            ts('dve', drv[:, l, 20:35], P('mu', l), -1.0, ALU.mult, ['prm'], ['drv'], 1.0, ALU.add)
        A_m = lambda l, kc: drv[:, l, kc:kc + 1]
        A_f = lambda l, kc: drv[:, l, 8 + kc:9 + kc]
        omk = lambda l, ct: drv[:, l, 16 + ct:17 + ct]
        modc = lambda l, m, kc: modt[:, l, m * 8 + kc:m * 8 + kc + 1]

        xT = sb("xT", [128, 8, TT])
        hT = sb("hT", [128, 8, TT], BF16)
        sq = sb("sq", [128, 8, TT], BF16)
        rstd = sb("rstd", [128, TT])
        ftmp = [sb(f"ftmp{i}", [128, TT]) for i in range(3)]
        NA = 8
        ringA = [sb(f"ra{i}", [128, 8, 128], BF16) for i in range(NA)]
        wvb = sb("wvb", [128, 8, 512], BF16)
        ringD = [sb(f"rd{i}", [128, NFF, 128], BF16) for i in range(2)]
        rs = {'a': 0, 'd': 0, 'f': 0}

        def loadA(src, key):
            i = rs['a']; rs['a'] = (i + 1) % NA
            S.dma('sp', ringA[i][:], src, reads=[key], writes=[f'ra{i}'])
            return ringA[i], f'ra{i}'

        def loadD(src, key):
            i = rs['d']; rs['d'] = (i + 1) % 2
            S.dma('sp', ringD[i][:], src, reads=[key], writes=[f'rd{i}'])
            return ringD[i], f'rd{i}'

        def ft():
            i = rs['f']; rs['f'] = (i + 1) % 3
            return ftmp[i], f'ftmp{i}'

        ob = sb("ob", [128, 8, TT], BF16)
        qf = sb("qf", [128, 2, TT]); kfg = sb("kfg", [128, 2, TT])
        sg = sb("sg", [128, 4, TT], BF16)
        zb = sb("zb", [128, TT], BF16)
        vtok = sb("vtok", [128, NCH, 512], BF16)
        sptok = sb("sptok", [128, NCH, 256])
        eG = sb("eG", [128, 2, TT]); emG = sb("emG", [128, TT])
        qt = sb("qt", [128, 2, TT], BF16); ktg = sb("ktg", [128, 2, TT], BF16)
        ktgT = sb("ktgT", [128, NCH, 256], BF16)
        ATb = [sb(f"ATb{i}", [128, 128], BF16) for i in range(4)]
        osb = sb("osb", [128, 4, TT])
        Sg = sb("Sg", [128, 2, 2, 128]); Sgb = sb("Sgb", [128, 2, 2, 128], BF16)
        ptmp = [sb(f"ptmp{i}", [128, TT + 2]) for i in range(2)]
        dtmp = [sb(f"dtmp{i}", [128, TT]) for i in range(2)]
        halo = sb("halo", [128, 2, 16])
        wab = sb("wab", [128, TT], BF16)
        sgl = sb("sgl", [128, TT], BF16)
        vl = sb("vl", [128, TT], BF16)
        rkv = [[sb(f"rkv{i}_{j}", [128, TT]) for j in range(3)] for i in range(2)]
        vfirst = sb("vfirst", [128, 4, TT])
        nlw = sb("nlw", [128, NCH, 512])
        eL = sb("eL", [128, 4, TT])
        emL = [sb(f"emL{i}", [128, TT]) for i in range(2)]
        eLp = [sb(f"eLp{i}", [128, TT]) for i in range(2)]
        asig = [sb(f"asig{i}", [128, TT]) for i in range(2)]
        kkn = [sb(f"kkn{i}", [128, TT]) for i in range(2)]
        kmod = [sb(f"kmod{i}", [128, TT]) for i in range(2)]
        kk2 = sb("kk2", [128, TT], BF16)
        rkb = sb("rkb", [128, TT], BF16)
        vb = sb("vb", [128, TT], BF16)
        big = sb("big", [128, 6144], BF16)
        tokT = big[:, 0:3072].rearrange("p (c t x) -> p c t x", c=NCH, t=4)
        ar = big[:, 3072:5120].rearrange("p (t x) -> p t x", t=4)
        ktr = big[:, 5120:6144].rearrange("p (t x) -> p t x", t=4)
        btr = sb("btr", [128, 4, TT], BF16)
        BIGK = ['tokT', 'ar', 'ktr']
        gate = sb("gate", [128, 4, TT], BF16)
        ysb = sb("ysb", [128, 4, TT])
        bonus = sb("bonus", [128, 4, TT])
        W0 = [sb(f"W0_{i}", [128, 384], BF16) for i in range(8)]
        Wpp = [[sb(f"Wpp{i}_{j}", [128, 384], BF16) for j in range(2)] for i in range(8)]
        MN2 = [[sb(f"MN{q}_{i}", [128, 256], BF16) for i in range(8)] for q in range(2)]
        Nbr2 = [[sb(f"Nbr{q}_{i}", [128, 128], BF16) for i in range(8)] for q in range(2)]
        Tfin2 = [[sb(f"Tfin{q}_{i}", [128, 128], BF16) for i in range(8)] for q in range(2)]
        Rn2 = [[sb(f"Rn{q}_{i}", [128, 128], BF16) for i in range(8)] for q in range(2)]
        XTb = [sb(f"XTb{i}", [128, 128], BF16) for i in range(4)]
        UTb = [sb(f"UTb{i}", [128, 128], BF16) for i in range(4)]
        Sr = sb("Sr", [128, 2, 4, 64]); Srb = sb("Srb", [128, 2, 4, 64], BF16)
        mbuf = big[:, 0:NFF * TT].rearrange("p (j x) -> p j x", j=NFF)
        acc = [emL[0], emL[1], eLp[0], eLp[1]]
        acck = ['emL0', 'emL1', 'eLp0', 'eLp1']
        sgt = [asig[0], asig[1]]
        sgtk = ['asig0', 'asig1']
        chalo = sb("chalo", [128, 2, 2, 44, 2])

        memz('dve', Sg[:], ['Sg']); memz('dve', Sgb[:], ['Sgb'])
        memz('dve', Sr[:], ['Sr']); memz('dve', Srb[:], ['Srb'])
        memz('pool', halo[:], ['halo']); memz('pool', chalo[:], ['chalo'])
        memz('pool', vl[:], ['vl'])
        for h in range(8):
            cp('pool', W0[h][:, 256:384], ident, ['cbt'], [f'W0_{h}'])

        def dump(nm, ap, rows, ti, r, q='sp'):
            if debug and nm in dbg_d:
                dst = dbg_d[nm].rearrange("(k p) t -> p k t", p=128)[:, :, ti * TT:(ti + 1) * TT]
                S.dma(q, dst, ap, reads=r, writes=['dbg_' + nm])

        def rmsnorm_mod(l, Afn, shift_m):
            act(sq[:], xT[:], AF.Square, ['xT'], ['sq'])
            ps, pk = bank()
            for kc in range(8):
                mm(ps[:, 0:TT], ones, sq[:, kc, :], kc == 0, kc == 7, ['cbt', 'sq'], [pk])
            act(rstd[:], ps[:, 0:TT], AF.Sqrt, [pk], ['rstd'], scale=1.0 / D, bias=NORM_EPS)
            recip(rstd[:], rstd[:], ['rstd'], ['rstd'])
            for kc in range(8):
                f_, fk = ft()
                stt(f_[:], xT[:, kc, :], Afn(l, kc), rstd[:], ALU.mult, ALU.mult, ['xT', 'drv', 'rstd'], [fk])
                act(hT[:, kc, :], f_[:], AF.Identity, [fk, 'modt'], ['hT'], bias=modc(l, shift_m, kc))

        def proj_fm(l, idx, m):
            slot, sk = loadA(s_fm[l, idx], f's_fm{l}')
            ps, pk = bank()
            for kc in range(8):
                mm(ps[0:m, 0:TT], slot[:, kc, 0:m], hT[:, kc, :], kc == 0, kc == 7, [sk, 'hT'], [pk])
            return ps, pk

        def lerp(l, idx, ps, pk, m, out_ap, okey, pi):
            mi = idx - RW0
            pt_, ptk = ptmp[pi], f'ptmp{pi}'
            dt_, dtk = dtmp[pi], f'dtmp{pi}'
            cp('act', pt_[0:m, 1:TT + 1], ps[0:m, 0:TT], [pk], [ptk])
            cp('pool', pt_[0:m, 0:1], halo[0:m, l, mi:mi + 1], ['halo'], [ptk])
            tt('dve', dt_[0:m, :], pt_[0:m, 0:TT], pt_[0:m, 1:TT + 1], ALU.subtract, [ptk], [dtk])
            cp('pool', halo[0:m, l, mi:mi + 1], pt_[0:m, TT:TT + 1], [ptk], ['halo'])
            o, w = POFF[f'mu{l}']
            stt(out_ap, dt_[0:m, :], prm[0:m, o + mi:o + mi + 1], pt_[0:m, 1:TT + 1], ALU.mult, ALU.add,
                [dtk, ptk, 'prm'], [okey])

        def inter(*gens):
            gens = [g for g in gens if g is not None]
            while gens:
                for g in list(gens):
                    try:
                        next(g)
                    except StopIteration:
                        gens.remove(g)
                yield

        def chk(n):
            if stop is not None and stop == n:
                raise _Stop()

        def _main():
          for ti in range(NT if stop != 0 else 0):
            tsl = slice(ti * TT, (ti + 1) * TT)
            S.dma('sp', xT[:], xT_d.rearrange("(k p) t -> p k t", p=128)[:, :, tsl], writes=['xT'])
            for l in range(2):
                rmsnorm_mod(l, A_m, 0)
                dump(f'h{l}', hT[:], D, ti, ['hT'], 'pool')
                def gla_thread():
                    ps, pk = proj_fm(l, 0, 16)
                    cp('act', zb[0:16, :], ps[0:16, 0:TT], [pk], ['zb'])
                    yield
                    for i in range(2):
                        ps, pk = proj_fm(l, 1 + i, 128)
                        act(qf[:, i, :], ps[:, 0:TT], AF.Copy, [pk], ['qf'], scale=0.125)
                        yield
                    for i in range(2):
                        ps, pk = proj_fm(l, 3 + i, 128)
                        cp('act', kfg[:, i, :], ps[:, 0:TT], [pk], ['kfg'])
                        yield
                    for i in range(4):
                        ps, pk = proj_fm(l, 5 + i, 128)
                        act(sg[:, i, :], ps[:, 0:TT], AF.Silu, [pk], ['sg'])
                        yield
                    chk(1)
                    S.dma('sp', wvb[:], s_v[l], reads=[f's_v{l}'], writes=['wvb'])
                    for c in range(NCH):
                        ps, pk = bank()
                        for kc in range(8):
                            mm(ps[:, 0:512], hT[:, kc, c * 128:(c + 1) * 128], wvb[:, kc, :], kc == 0, kc == 7,
                               ['hT', 'wvb'], [pk])
                        cp('dve', vtok[:, c, :], ps[:, 0:512], [pk], ['vtok'])
                        yield
                    for c in range(NCH):
                        ps, pk = bank()
                        mm(ps[:, 0:256], zb[0:16, c * 128:(c + 1) * 128], gkup[0:16, l, :], True, True, ['zb', 'gkup'], [pk])
                        f_, fk = ft()
                        tt('dve', f_[:, 0:256], ps[:, 0:256], gkb[:, l, :], ALU.add, [pk, 'gkb'], [fk])
                        act(f_[:, 0:256], f_[:, 0:256], AF.Exp, [fk], [fk], scale=-1.0)
                        act(sptok[:, c, :], f_[:, 0:256], AF.Ln, [fk], ['sptok'], bias=1.0)
                        yield
                    for c2 in range(2):
                        ps, pk = bank()
                        for c in range(NCH):
                            mm(ps[:, c * 128:(c + 1) * 128], sptok[:, c, c2 * 128:(c2 + 1) * 128], tri16, True, True,
                               ['sptok', 'cft'], [pk])
                        act(eG[:, c2, :], ps[:, 0:TT], AF.Exp, [pk], ['eG'], scale=-1.0)
                        act(emG[:], ps[:, 0:TT], AF.Exp, [pk], ['emG'])
                        tt('dve', qt[:, c2, :], qf[:, c2, :], eG[:, c2, :], ALU.mult, ['qf', 'eG'], ['qt'])
                        tt('pool', ktg[:, c2, :], kfg[:, c2, :], emG[:], ALU.mult, ['kfg', 'emG'], ['ktg'])
                        for c in range(NCH):
                            tp, tk_ = tbank()
                            tr(tp[:, 0:128], ktg[:, c2, c * 128:(c + 1) * 128], ['ktg'], [tk_])
                            cp('act', ktgT[:, c, c2 * 128:(c2 + 1) * 128], tp[:, 0:128], [tk_], ['ktgT'])
                            yield
                    for c in range(NCH):
                        csl = slice(c * 128, (c + 1) * 128)
                        for h in range(4):
                            c2 = h // 2; po = (h % 2) * 64
                            ps, pk = bank()
                            mm(ps[:, 0:128], ktg[po:po + 64, c2, csl], qt[po:po + 64, c2, csl], True, True, ['ktg', 'qt'], [pk])
                            tt('dve', ATb[h][:], ps[:, 0:128], m_iu, ALU.mult, [pk, 'cbt'], [f'ATb{h}'])
                            ps2, pk2 = bank()
                            mm(ps2[:, 0:128], vtok[:, c, h * 128:(h + 1) * 128], ATb[h][:], True, False, ['vtok', f'ATb{h}'], [pk2])
                            mm(ps2[:, 0:128], Sgb[po:po + 64, l, c2, :], qt[po:po + 64, c2, csl], False, True, ['Sgb', 'qt'], [pk2])
                            cp('act', osb[:, h, csl], ps2[:, 0:128], [pk2], ['osb'])
                            yield
                        for c2 in range(2):
                            ps, pk = bank()
                            for hh in range(2):
                                h = c2 * 2 + hh; po = hh * 64
                                mm(ps[po:po + 64, 0:128], ktgT[:, c, c2 * 128 + po:c2 * 128 + po + 64],
                                   vtok[:, c, h * 128:(h + 1) * 128], True, True, ['ktgT', 'vtok'], [pk])
                            f_, fk = ft()
                            tt('dve', f_[:, 0:128], ps[:, 0:128], Sg[:, l, c2, :], ALU.add, [pk, 'Sg'], [fk])
                            ts('dve', Sg[:, l, c2, :], f_[:, 0:128], eG[:, c2, c * 128 + 127:c * 128 + 128], ALU.mult,
                               [fk, 'eG'], ['Sg'])
                            cp('pool', Sgb[:, l, c2, :], Sg[:, l, c2, :], ['Sg'], ['Sgb'])
                            yield
                    for h in range(4):
                        f_, fk = ft()
                        act(sq[:, h, :], osb[:, h, :], AF.Square, ['osb'], ['sq'])
                        ps, pk = bank()
                        mm(ps[:, 0:TT], ones, sq[:, h, :], True, True, ['cbt', 'sq'], [pk])
                        act(f_[:], ps[:, 0:TT], AF.Sqrt, [pk], [fk], scale=1.0 / 128, bias=NORM_EPS)
                        recip(f_[:], f_[:], [fk], [fk])
                        tt('dve', f_[:], osb[:, h, :], f_[:], ALU.mult, ['osb', fk], [fk])
                        stt(ob[:, h, :], f_[:], P('gng', l), sg[:, h, :], ALU.mult, ALU.mult, [fk, 'prm', 'sg'], ['ob'])
                        yield
                    if debug:
                        dst = dbg_d.get(f'gla_o{l}')
                        if dst is not None:
                            S.dma('sp', dst.rearrange("(k p) t -> p k t", p=128)[:, :, tsl], osb[:], reads=['osb'],
                                  writes=['dbg_g'])

                def rwkv_thread():
                    chk(2)
                    ps, pk = proj_fm(l, 9, 128)
                    lerp(l, 9, ps, pk, 128, ftmp[0][:], 'ftmp0', 0)
                    act(wab[0:64, :], ftmp[0][0:64, :], AF.Tanh, ['ftmp0'], ['wab'])
                    cp('pool', wab[64:128, :], ftmp[0][64:128, :], ['ftmp0'], ['wab'])
                    yield
                    ps, pk = proj_fm(l, 10, 128)
                    lerp(l, 10, ps, pk, 128, ftmp[1][:], 'ftmp1', 1)
                    act(sgl[:], ftmp[1][:], AF.Sigmoid, ['ftmp1'], ['sgl'])
                    yield
                    if l == 1:
                        ps, pk = proj_fm(l, 11, 32)
                        lerp(l, 11, ps, pk, 32, ftmp[2][0:32, :], 'ftmp2', 0)
                        cp('pool', vl[0:32, :], ftmp[2][0:32, :], ['ftmp2'], ['vl'])
                        yield
                    chk(21)
                    for c in range(NCH):
                        ps, pk = bank()
                        mm(ps[:, 0:512], wab[0:64, c * 128:(c + 1) * 128], waup[0:64, l, :], True, True, ['wab', 'waup'], [pk])
                        tt('dve', nlw[:, c, :], ps[:, 0:512], w0bc[:, l, :], ALU.add, [pk, 'w0bc'], ['nlw'])
                        act(nlw[:, c, :], nlw[:, c, :], AF.Sigmoid, ['nlw'], ['nlw'])
                        yield
                    chk(22)
                    for ct in range(4):
                        pi = ct % 2
                        cts = slice(ct * 128, (ct + 1) * 128)
                        rf, kf, vf = rkv[pi]
                        rk_, kk_, vk_ = f'rkv{pi}_0', f'rkv{pi}_1', f'rkv{pi}_2'
                        ps, pk = proj_fm(l, 12 + 3 * ct, 128)
                        lerp(l, 12 + 3 * ct, ps, pk, 128, rf[:], rk_, 0)
                        yield
                        ps, pk = proj_fm(l, 13 + 3 * ct, 128)
                        lerp(l, 13 + 3 * ct, ps, pk, 128, kf[:], kk_, 1)
                        yield
                        ps, pk = proj_fm(l, 14 + 3 * ct, 128)
                        lerp(l, 14 + 3 * ct, ps, pk, 128, vf[:], vk_, 0)
                        yield
                        ps, pk = bank()
                        for c in range(NCH):
                            mm(ps[:, c * 256:(c + 1) * 256], nlw[:, c, cts], triW, True, True, ['nlw', 'cft'], [pk])
                        psv = ps[:, 0:NCH * 256].rearrange("p (c x) -> p c x", x=256)
                        v3 = lambda ap: ap.rearrange("p (c x) -> p c x", x=128)
                        act(v3(eL[:, ct, :]), psv[:, :, 0:128], AF.Exp, [pk], ['eL'], scale=-1.0)
                        act(v3(emL[pi][:]), psv[:, :, 0:128], AF.Exp, [pk], [f'emL{pi}'])
                        act(v3(eLp[pi][:]), psv[:, :, 128:256], AF.Exp, [pk], [f'eLp{pi}'], scale=-1.0)
                        yield
                        chk(23)
                        ps, pk = bank()
                        mm(ps[:, 0:TT], waup[64:128, l, cts], wab[64:128, :], True, True, ['waup', 'wab'], [pk])
                        act(asig[pi][:], ps[:, 0:TT], AF.Sigmoid, [pk, 'prm'], [f'asig{pi}'], bias=Pc('a0', l, ct))
                        yield
                        ps, pk = bank()
                        mm(ps[:, 0:TT], gup[:, l, cts], sgl[:], True, True, ['gup', 'sgl'], [pk])
                        cp('act', gate[:, ct, :], ps[:, 0:TT], [pk], ['gate'])
                        yield
                        if l == 1:
                            ps, pk = bank()
                            mm(ps[:, 0:TT], vup[0:32, cts], vl[0:32, :], True, True, ['vup', 'vl'], [pk])
                            f_, fk = ft()
                            act(f_[:], ps[:, 0:TT], AF.Sigmoid, [pk, 'prm'], [fk], bias=Pc('v0', 1, ct))
                            f2, fk2 = ft()
                            tt('pool', f2[:], vfirst[:, ct, :], vf[:], ALU.subtract, ['vfirst', vk_], [fk2])
                            tt('pool', f2[:], f2[:], f_[:], ALU.mult, [fk2, fk], [fk2])
                            tt('pool', vf[:], vf[:], f2[:], ALU.add, [vk_, fk2], [vk_])
                            yield
                        else:
                            cp('pool', vfirst[:, ct, :], vf[:], [vk_], ['vfirst'])
                            yield
                        chk(24)
                        act(kk2[:], kf[:], AF.Square, [kk_, 'prm'], ['kk2'], scale=Pc('k_k', l, ct))
                        ps, pk = bank()
                        mm(ps[:, 0:TT], bones, kk2[:], True, True, ['cbt', 'kk2'], [pk])
                        f_, fk = ft()
                        ts('dve', f_[:], ps[:, 0:TT], 1e-24, ALU.max, [pk], [fk])
                        act(f_[:], f_[:], AF.Sqrt, [fk], [fk])
                        recip(f_[:], f_[:], [fk], [fk])
                        stt(kkn[pi][:], kf[:], Pc('k_k', l, ct), f_[:], ALU.mult, ALU.mult, [kk_, 'prm', fk], [f'kkn{pi}'])
                        yield
                        f2, fk2 = ft()
                        ts('dve', f2[:], asig[pi][:], Pc('k_a', l, ct), ALU.mult, [f'asig{pi}', 'prm', 'drv'], [fk2],
                           omk(l, ct), ALU.add)
                        tt('pool', kmod[pi][:], kf[:], f2[:], ALU.mult, [kk_, fk2], [f'kmod{pi}'])
                        tt('dve', ktr[:, ct, :], kmod[pi][:], emL[pi][:], ALU.mult, [f'kmod{pi}', f'emL{pi}'], ['ktr'])
                        yield
                        f3, fk3 = ft()
                        tt('pool', f3[:], kkn[pi][:], asig[pi][:], ALU.mult, [f'kkn{pi}', f'asig{pi}'], [fk3])
                        tt('dve', btr[:, ct, :], f3[:], emL[pi][:], ALU.mult, [fk3, f'emL{pi}'], ['btr'])
                        yield
                        arv = ar[:, ct, :].rearrange("p (c x) -> p c x", x=256)
                        stt(arv[:, :, 0:128], v3(kkn[pi][:]), -1.0, v3(eLp[pi][:]), ALU.mult, ALU.mult,
                            [f'kkn{pi}', f'eLp{pi}'], ['ar'])
                        tt('dve', arv[:, :, 128:256], v3(rf[:]), v3(eL[:, ct, :]), ALU.mult, [rk_, 'eL'], ['ar'])
                        yield
                        chk(25)
                        stt(rkb[:], rf[:], Pc('r_k', l, ct), kmod[pi][:], ALU.mult, ALU.mult, [rk_, 'prm', f'kmod{pi}'], ['rkb'])
                        ps, pk = bank()
                        mm(ps[:, 0:TT], bones, rkb[:], True, True, ['cbt', 'rkb'], [pk])
                        tt('dve', bonus[:, ct, :], ps[:, 0:TT], vf[:], ALU.mult, [pk, vk_], ['bonus'])
                        yield
                        if debug and ct == 0:
                            for nm_, ap_, k_ in (('d_rf', rf[:], rk_), ('d_kf', kf[:], kk_), ('d_vf', vf[:], vk_), ('d_emL', emL[pi][:], f'emL{pi}'),
                                                 ('d_kmod', kmod[pi][:], f'kmod{pi}'), ('d_kkn', kkn[pi][:], f'kkn{pi}'), ('d_asig', asig[pi][:], f'asig{pi}'),
                                                 ('d_eL', eL[:, ct, :], 'eL'), ('d_eLp', eLp[pi][:], f'eLp{pi}'), ('d_ktr', ktr[:, ct, :], 'ktr'), ('d_btr', btr[:, ct, :], 'btr')):
                                S.dma('pool', dbg_d[nm_][:, tsl], ap_, reads=[k_], writes=['dbg_' + nm_])
                        chk(251)
                        cp('pool', vb[:], vf[:], [vk_], ['vb'])
                        chk(252)
                        for c in range(NCH):
                            csl = slice(c * 128, (c + 1) * 128)
                            for ii, (src_, sk_) in enumerate(((vb[:, csl], 'vb'), (ktr[:, ct, csl], 'ktr'), (btr[:, ct, csl], 'btr'))):
                                tp, tk_ = tbank()
                                import os
                                if str(ii) not in os.environ.get('XII', '012'):
                                    continue
                                if os.environ.get('XTR', '1') == '1':
                                    tr(tp[:, 0:128], src_, [sk_], [tk_])
                                if os.environ.get('XCP', '1') == '1':
                                    cp(os.environ.get('XENG', 'act'), tokT[:, c, ct, ii * 128:(ii + 1) * 128], tp[:, 0:128], [tk_], ['tokT'])
                                    yield
                        chk(26 + ct)

                    chk(3)
                    def stage_A(c):
                        csl = slice(c * 128, (c + 1) * 128)
                        q = c % 2
                        MN, Nbr, Tfin, Rn = MN2[q], Nbr2[q], Tfin2[q], Rn2[q]
                        for ct in range(4):
                            for hh in range(2):
                                h = ct * 2 + hh; po = hh * 64
                                kt_h = ktr[po:po + 64, ct, csl]; bt_h = btr[po:po + 64, ct, csl]
                                ar_h = ar[po:po + 64, ct, c * 256:(c + 1) * 256]
                                at_h = ar[po:po + 64, ct, c * 256:c * 256 + 128]
                                ps, pk = bank()
                                mm(ps[:, 0:256], kt_h, ar_h, True, True, ['ktr', 'ar'], [pk])
                                mm(ps[:, 256:512], bt_h, ar_h, True, True, ['btr', 'ar'], [pk])
                                ps2, pk2 = bank()
                                mm(ps2[:, 0:128], at_h, bt_h, True, True, ['ar', 'btr'], [pk2])
                                tt('dve', MN[h][:], ps[:, 0:256], m_suiu, ALU.mult, [pk, 'cbt'], [f'MN{q}_{h}'])
                                tt('dve', W0[h][:, 128:256], ps[:, 256:384], m_su, ALU.mult, [pk, 'cbt'], [f'W0_{h}'])
                                tt('dve', Nbr[h][:], ps[:, 384:512], m_iu, ALU.mult, [pk, 'cbt'], [f'Nbr{q}_{h}'])
                                tt('dve', W0[h][:, 0:128], ps2[:, 0:128], m_sl, ALU.mult, [pk2, 'cbt'], [f'W0_{h}'])
                                yield
                        for k in range(7):
                            for h in range(8):
                                cur, ck = (W0[h], f'W0_{h}') if k == 0 else (Wpp[h][(k - 1) % 2], f'Wpp{h}_{(k - 1) % 2}')
                                ps, pk = bank()
                                if k < 6:
                                    nxt, nk = Wpp[h][k % 2], f'Wpp{h}_{k % 2}'
                                    mm(ps[:, 0:128], cur[:, 128:256], cur[:, 0:128], True, True, [ck], [pk])
                                    mm(ps[:, 128:384], cur[:, 0:128], cur[:, 128:384], True, True, [ck], [pk])
                                    cp('act', nxt[:, 0:256], ps[:, 0:256], [pk], [nk])
                                    tt('dve', nxt[:, 256:384], ps[:, 256:384], cur[:, 256:384], ALU.add, [pk, ck], [nk])
                                else:
                                    mm(ps[:, 0:128], cur[:, 0:128], cur[:, 256:384], True, True, [ck], [pk])
                                    tt('dve', Tfin[h][:], ps[:, 0:128], cur[:, 256:384], ALU.add, [pk, ck], [f'Tfin{q}_{h}'])
                                    psn, pkn = bank()
                                    mm(psn[:, 0:128], W0[h][:, 0:128], Tfin[h][:], True, True, [f'W0_{h}', f'Tfin{q}_{h}'], [pkn])
                                    fn_, fnk = ft()
                                    tt('dve', fn_[:, 0:128], psn[:, 0:128], Tfin[h][:], ALU.subtract, [pkn, f'Tfin{q}_{h}'], [fnk])
                                    tt('pool', Rn[h][:], fn_[:, 0:128], ident, ALU.add, [fnk, 'cbt'], [f'Rn{q}_{h}'])
                                if h % 2 == 1:
                                    yield

                    def stage_B(c):
                        csl = slice(c * 128, (c + 1) * 128)
                        q = c % 2
                        MN, Nbr, Tfin, Rn = MN2[q], Nbr2[q], Tfin2[q], Rn2[q]
                        for ct in range(4):
                            ps, pk = bank()
                            for hh in range(2):
                                h = ct * 2 + hh; po = hh * 64
                                at_h = ar[po:po + 64, ct, c * 256:c * 256 + 128]
                                mm(ps[:, hh * 64:(hh + 1) * 64], at_h, Srb[po:po + 64, l, ct, :], True, False, ['ar', 'Srb'], [pk])
                                mm(ps[:, hh * 64:(hh + 1) * 64], MN[h][:, 0:128], tokT[:, c, ct, po:po + 64], False, True,
                                   [f'MN{q}_{h}', 'tokT'], [pk])
                            cp('act', XTb[ct][:], ps[:, 0:128], [pk], [f'XTb{ct}'])
                            yield
                        for ct in range(4):
                            ps, pk = bank()
                            for hh in range(2):
                                h = ct * 2 + hh
                                mm(ps[:, hh * 64:(hh + 1) * 64], Tfin[h][:], XTb[ct][:, hh * 64:(hh + 1) * 64], True, True,
                                   [f'Tfin{q}_{h}', f'XTb{ct}'], [pk])
                            cp('act', UTb[ct][:], ps[:, 0:128], [pk], [f'UTb{ct}'])
                            yield
                        for ct in range(4):
                            ps, pk = bank()
                            for hh in range(2):
                                h = ct * 2 + hh
                                mm(ps[:, hh * 64:(hh + 1) * 64], Rn[h][:], UTb[ct][:, hh * 64:(hh + 1) * 64], True, True,
                                   [f'Rn{q}_{h}', f'UTb{ct}'], [pk])
                            tt('dve', XTb[ct][:], ps[:, 0:128], UTb[ct][:], ALU.add, [pk, f'UTb{ct}'], [f'XTb{ct}'])
                            yield
                        for ct in range(4):
                            ps, pk = bank()
                            for hh in range(2):
                                h = ct * 2 + hh; po = hh * 64
                                rt_h = ar[po:po + 64, ct, c * 256 + 128:(c + 1) * 256]
                                o_ = ps[po:po + 64, 0:128]
                                mm(o_, Srb[po:po + 64, l, ct, :], rt_h, True, False, ['Srb', 'ar'], [pk])
                                mm(o_, tokT[:, c, ct, po:po + 64], MN[h][:, 128:256], False, False, ['tokT', f'MN{q}_{h}'], [pk])
                                mm(o_, XTb[ct][:, hh * 64:(hh + 1) * 64], Nbr[h][:], False, True, [f'XTb{ct}', f'Nbr{q}_{h}'], [pk])
                            cp('act', ysb[:, ct, csl], ps[:, 0:128], [pk], [f'ysb{ct}'])
                            ps2, pk2 = bank()
                            for hh in range(2):
                                po = hh * 64
                                o_ = ps2[po:po + 64, 0:64]
                                mm(o_, tokT[:, c, ct, 128 + po:128 + po + 64], tokT[:, c, ct, po:po + 64], True, False, ['tokT'], [pk2])
                                mm(o_, tokT[:, c, ct, 256 + po:256 + po + 64], XTb[ct][:, hh * 64:(hh + 1) * 64], False, True,
                                   ['tokT', f'XTb{ct}'], [pk2])
                            f_, fk = ft()
                            tt('dve', f_[:, 0:64], ps2[:, 0:64], Sr[:, l, ct, :], ALU.add, [pk2, 'Sr'], [fk])
                            ts('dve', Sr[:, l, ct, :], f_[:, 0:64], eL[:, ct, c * 128 + 127:c * 128 + 128], ALU.mult,
                               [fk, 'eL'], ['Sr'])
                            cp('pool', Srb[:, l, ct, :], Sr[:, l, ct, :], ['Sr'], ['Srb'])
                            yield

                    yield from stage_A(0)
                    for c in range(NCH):
                        yield from inter(stage_B(c), stage_A(c + 1) if c + 1 < NCH else None)

                for _ in inter(gla_thread(), rwkv_thread()):
                    pass
                if debug:
                    dst = dbg_d.get(f'rwkv_y{l}')
                    if dst is not None:
                        S.dma('sp', dst.rearrange("(k p) t -> p k t", p=128)[:, :, tsl], ysb[:], reads=[f'ysb{i}' for i in range(4)],
                              writes=['dbg_y'])

                chk(4)
                for ct in range(4):
                    f_, fk = ft(); f2, fk2 = ft()
                    cp('pool', vb[:], ysb[:, ct, :], [f'ysb{ct}'], ['vb'])
                    ps, pk = bank()
                    mm(ps[:, 0:TT], bones64, vb[:], True, True, ['cbt', 'vb'], [pk])
                    tt('dve', f_[:], ysb[:, ct, :], ps[:, 0:TT], ALU.subtract, [f'ysb{ct}', pk], [fk])
                    act(kk2[:], f_[:], AF.Square, [fk], ['kk2'])
                    ps, pk = bank()
                    mm(ps[:, 0:TT], bones64, kk2[:], True, True, ['cbt', 'kk2'], [pk])
                    act(f2[:], ps[:, 0:TT], AF.Sqrt, [pk], [fk2], bias=GN_EPS)
                    recip(f2[:], f2[:], [fk2], [fk2])
                    tt('dve', f_[:], f_[:], f2[:], ALU.mult, [fk, fk2], [fk])
                    ts('dve', f_[:], f_[:], Pc('gn_g', l, ct), ALU.mult, [fk, 'prm'], [fk], Pc('gn_b', l, ct), ALU.add)
                    tt('pool', f_[:], f_[:], bonus[:, ct, :], ALU.add, [fk, 'bonus'], [fk])
                    tt('dve', ob[:, 4 + ct, :], f_[:], gate[:, ct, :], ALU.mult, [fk, 'gate'], ['ob'])
                dump(f'oo{l}', ob[:], D, ti, ['ob'], 'pool')

                for j in range(8):
                    slot, sk = loadA(s_o[l, j], f's_o{l}')
                    ps, pk = bank()
                    for kc in range(8):
                        mm(ps[:, 0:TT], slot[:, kc, :], ob[:, kc, :], kc == 0, kc == 7, [sk, 'ob'], [pk])
                    stt(xT[:, j, :], ps[:, 0:TT], modc(l, 2, j), xT[:, j, :], ALU.mult, ALU.add, [pk, 'modt', 'xT'], ['xT'])
                dump(f'xmix{l}', xT[:], D, ti, ['xT'])

                chk(5)
                rmsnorm_mod(l, A_f, 3)
                par = ti % 2
                for j in range(NFF):
                    accs = []
                    for half in range(2):
                        u = 2 * j + half
                        slot, sk = loadA(s_u[l, u], f's_u{l}')
                        ps, pk = bank()
                        for kc in range(8):
                            mm(ps[:, 0:TT], slot[:, kc, :], hT[:, kc, :], kc == 0, kc == 7, [sk, 'hT'], [pk])
                        a_ = acc[half * 2 + (j % 2)]; ak = acck[half * 2 + (j % 2)]
                        o, w = POFF[f'cw{l}']
                        cw = lambda tap: prm[:, o + u * 3 + tap:o + u * 3 + tap + 1]
                        ob_, wb_ = POFF[f'cb{l}']
                        ub, ubk = ptmp[half], f'ptmp{half}'
                        cp('act', ub[:, 2:TT + 2], ps[:, 0:TT], [pk], [ubk])
                        cp('pool', ub[:, 0:2], chalo[:, par, l, u, :], ['chalo'], [ubk])
                        ts('dve', a_[:], ub[:, 2:TT + 2], cw(2), ALU.mult, [ubk, 'prm'], [ak], prm[:, ob_ + u:ob_ + u + 1], ALU.add)
                        cp('pool', chalo[:, 1 - par, l, u, :], ub[:, TT:TT + 2], [ubk], ['chalo'])
                        stt(a_[:], ub[:, 1:TT + 1], cw(1), a_[:], ALU.mult, ALU.add, [ubk, 'prm', ak], [ak])
                        stt(a_[:], ub[:, 0:TT], cw(0), a_[:], ALU.mult, ALU.add, [ubk, 'prm', ak], [ak])
                        accs.append((a_, ak))
                    s_, skk = sgt[j % 2], sgtk[j % 2]
                    act(s_[:], accs[0][0][:], AF.Silu, [accs[0][1]], [skk])
                    tt('pool', mbuf[:, j, :], s_[:], accs[1][0][:], ALU.mult, [skk, accs[1][1]], [f'mbuf{j}'] + BIGK)
                for jo in range(8):
                    slot, sk = loadD(s_d[l, jo], f's_d{l}')
                    ps, pk = bank()
                    for kc in range(NFF):
                        mm(ps[:, 0:TT], slot[:, kc, :], mbuf[:, kc, :], kc == 0, kc == NFF - 1, [sk, f'mbuf{kc}'] + BIGK, [pk])
                    stt(xT[:, jo, :], ps[:, 0:TT], modc(l, 5, jo), xT[:, jo, :], ALU.mult, ALU.add, [pk, 'modt', 'xT'], ['xT'])
                dump(f'x{l}', xT[:], D, ti, ['xT'])
                chk(51 + l)

            chk(6)
            act(sq[:], xT[:], AF.Square, ['xT'], ['sq'])
            ps, pk = bank()
            for kc in range(8):
                mm(ps[:, 0:TT], ones, sq[:, kc, :], kc == 0, kc == 7, ['cbt', 'sq'], [pk])
            act(rstd[:], ps[:, 0:TT], AF.Sqrt, [pk], ['rstd'], scale=1.0 / D, bias=NORM_EPS)
            recip(rstd[:], rstd[:], ['rstd'], ['rstd'])
            fo, fw = POFF['final_g']
            for kc in range(8):
                dst_, dk_ = (osb, 'osb') if kc < 4 else (ysb, f'ysb{kc % 4}')
                stt(dst_[:, kc % 4, :], xT[:, kc, :], prm[:, fo + kc:fo + kc + 1], rstd[:], ALU.mult, ALU.mult,
                    ['xT', 'prm', 'rstd'], [dk_])
            odv = out_d.rearrange("(k p) t -> p k t", p=128)
            S.dma('sp', odv[:, 0:4, tsl], osb[:], reads=['osb'], writes=['out_d'])
            S.dma('sp', odv[:, 4:8, tsl], ysb[:], reads=[f'ysb{i}' for i in range(4)], writes=['out_d'])
        try:
            _main()
        except _Stop:
            pass
        S.barrier()
        print(f"[kernel] T={T} instructions={S.n_inst} waits={S.n_wait}", flush=True)
    return nc


def core_inputs(inputs, shared, b, T):
    m = dict(shared)
    m['xT'] = np.ascontiguousarray(np.asarray(inputs['x'], np.float32)[b, :T].T)
    P = shared['params'].copy()
    o, w = POFF['c']
    P[:, o:o + w] = _fm(np.asarray(inputs['c'], np.float32)[b])
    m['params'] = P
    return m


def kernel(**inputs):
    T = SEQ
    shared = prep_shared(inputs)
    nc = build_nc(T)
    in_maps = [core_inputs(inputs, shared, c // 2, T) for c in range(8)]
    res = run_bass_kernel_spmd(nc, in_maps, core_ids=list(range(8)))
    out = np.zeros((4, T, D), np.float32)
    H = T // 2
    for c in range(8):
        b, s = c // 2, c % 2
        o = res.results[c]["outT"]
        out[b, s * H:(s + 1) * H] = o[:, s * H:(s + 1) * H].T
    return out
```

```python
import contextlib
import numpy as np
import concourse.bass as bass
import concourse.mybir as mybir
from concourse.bass_utils import run_bass_kernel_spmd

F32 = mybir.dt.float32
BF16 = mybir.dt.bfloat16
AF = mybir.ActivationFunctionType
ALU = mybir.AluOpType

D = 1024
NKC = 8
TT = 256
C = 128
NCH = TT // C
NFF = 22
SEQ = 8192
GN_EPS = 64e-5
NORM_EPS = 1e-6
NFM = 24
RW0 = 9

def _playout():
    off = {}
    cur = 0
    def add(name, w):
        nonlocal cur
        off[name] = (cur, w)
        cur += w
    add('c', 8)
    add('final_g', 8)
    for l in range(2):
        add(f'ada_b{l}', 48)
        add(f'nmg{l}', 8)
        add(f'nfg{l}', 8)
        add(f'mu{l}', 15)
        for nm in ('w0', 'a0', 'v0', 'k_k', 'k_a', 'r_k', 'gn_g', 'gn_b'):
            add(f'{nm}{l}', 4)
        add(f'gng{l}', 1)
        add(f'cw{l}', 132)
        add(f'cb{l}', 44)
    return off, cur

POFF, NPARAM = _playout()


def _fm(vec):
    v = np.asarray(vec, np.float32).reshape(-1)
    n = v.shape[0] // 128
    return v.reshape(n, 128).T


def _wt(cols):
    K, w = cols.shape
    return cols.reshape(K // 128, 128, w).transpose(1, 0, 2)


def prep_shared(inp):
    f = lambda a: np.asarray(a, np.float32)
    w_in = f(inp['w_in']); w_vres = f(inp['w_in_vres'])
    sh = {}
    w_fm = np.zeros((2, NFM, 128, 8, 128), np.float32)
    RB = 1552
    for l in range(2):
        W = w_in[l]
        def put(idx, cols):
            w_fm[l, idx, :, :, :cols.shape[1]] = _wt(cols)
        put(0, W[:, 1536:1552])
        put(1, W[:, 0:128]); put(2, W[:, 128:256]); put(3, W[:, 256:384]); put(4, W[:, 384:512])
        for i in range(4):
            put(5 + i, W[:, 1024 + i * 128:1024 + (i + 1) * 128])
        put(9, W[:, RB + 1536:RB + 1664])
        put(10, W[:, RB + 1664:RB + 1792])
        if l == 1:
            put(11, w_vres[0])
        for ct in range(4):
            put(12 + 3 * ct, W[:, RB + ct * 128:RB + (ct + 1) * 128])
            put(13 + 3 * ct, W[:, RB + 512 + ct * 128:RB + 512 + (ct + 1) * 128])
            put(14 + 3 * ct, W[:, RB + 1024 + ct * 128:RB + 1024 + (ct + 1) * 128])
    sh['w_fm'] = w_fm
    sh['w_v'] = np.stack([_wt(w_in[l][:, 512:1024]) for l in range(2)])
    w_out = f(inp['w_out'])
    sh['w_o'] = np.stack([np.stack([_wt(w_out[l][:, j * 128:(j + 1) * 128]) for j in range(8)]) for l in range(2)])
    fu = f(inp['ffn_up'])
    wu = np.zeros((2, 44, 128, 8, 128), np.float32)
    for l in range(2):
        for j in range(NFF):
            wu[l, 2 * j] = _wt(fu[l][:, j * 128:(j + 1) * 128])
            wu[l, 2 * j + 1] = _wt(fu[l][:, 2816 + j * 128:2816 + (j + 1) * 128])
    sh['w_u'] = wu
    fd = f(inp['ffn_down'])
    sh['w_d'] = np.stack([np.stack([_wt(fd[l][:, j * 128:(j + 1) * 128]) for j in range(8)]) for l in range(2)])
    sh['ada'] = np.stack([_wt(f(inp['ada_w'])[l]) for l in range(2)])
    sh['wa_up'] = np.stack([np.concatenate([f(inp['w_lora_up'])[l], f(inp['a_lora_up'])[l]], 0) for l in range(2)])
    sh['g_up'] = f(inp['g_lora_up'])
    vup = np.zeros((128, 512), np.float32); vup[:32] = f(inp['v_lora_up'])[0]
    sh['v_up'] = vup
    gku = np.zeros((2, 128, 256), np.float32); gku[:, :16] = f(inp['gla_gk_up'])
    sh['gk_up'] = gku
    sh['gkb_bc'] = np.stack([np.broadcast_to(f(inp['gla_gk_b'])[l], (128, 256)) for l in range(2)]).copy()
    sh['w0_bc'] = np.stack([np.broadcast_to(f(inp['w0'])[l], (128, 512)) for l in range(2)]).copy()
    j = np.arange(128)[:, None]; t = np.arange(128)[None, :]
    su = (j < t).astype(np.float32); iu = (j <= t).astype(np.float32); sl = (j > t).astype(np.float32)
    eye = np.eye(128, dtype=np.float32)
    bones = np.zeros((128, 128), np.float32); bones[:64, :64] = 1; bones[64:, 64:] = 1
    cb = np.concatenate([eye, su, iu, sl, np.ones((128, 128), np.float32), bones, bones / 64.0], 1)
    sh['constb'] = cb
    ew = np.float32(np.exp(-0.5))
    sh['constf'] = np.concatenate([iu * ew, su * ew, iu / 16.0], 1).astype(np.float32)
    P = np.zeros((128, NPARAM), np.float32)
    def setp(name, arr):
        o, w = POFF[name]
        assert arr.shape == (128, w), (name, arr.shape, w)
        P[:, o:o + w] = arr
    setp('final_g', _fm(inp['final_g']))
    for l in range(2):
        setp(f'ada_b{l}', _fm(f(inp['ada_b'])[l]))
        setp(f'nmg{l}', _fm(f(inp['norm_mix_g'])[l]))
        setp(f'nfg{l}', _fm(f(inp['norm_ffn_g'])[l]))
        mu = f(inp['rwkv_mu'])[l]
        m = np.zeros((128, 15), np.float32)
        m[:, 0] = mu[1536:1664]; m[:, 1] = mu[1664:1792]
        if l == 1:
            m[:32, 2] = f(inp['rwkv_mu_vres'])[0]
        for ct in range(4):
            m[:, 3 + 3 * ct] = mu[ct * 128:(ct + 1) * 128]
            m[:, 4 + 3 * ct] = mu[512 + ct * 128:512 + (ct + 1) * 128]
            m[:, 5 + 3 * ct] = mu[1024 + ct * 128:1024 + (ct + 1) * 128]
        setp(f'mu{l}', m)
        for nm, key in (('w0', 'w0'), ('a0', 'a0'), ('k_k', 'k_k'), ('k_a', 'k_a'), ('gn_g', 'gn_g'), ('gn_b', 'gn_b')):
            setp(f'{nm}{l}', _fm(f(inp[key])[l]))
        setp(f'r_k{l}', _fm(f(inp['r_k'])[l].reshape(-1)))
        if l == 1:
            setp('v01', _fm(f(inp['v0'])[0]))
        setp(f'gng{l}', f(inp['gla_norm_g'])[l].reshape(128, 1))
        cw = f(inp['ffn_conv_w'])[l]; cbv = f(inp['ffn_conv_b'])[l]
        cwp = np.zeros((128, 44, 3), np.float32); cbp = np.zeros((128, 44), np.float32)
        for jj in range(NFF):
            for half in range(2):
                cols = slice(half * 2816 + jj * 128, half * 2816 + (jj + 1) * 128)
                cwp[:, 2 * jj + half, :] = cw[:, cols].T
                cbp[:, 2 * jj + half] = cbv[cols]
        setp(f'cw{l}', cwp.reshape(128, 132))
        setp(f'cb{l}', cbp)
    sh['params'] = P
    return sh


class Sched:
    def __init__(self, nc, es, ndma=8):
        self.nc = nc
        self.eng = {'pe': nc.tensor, 'act': nc.scalar, 'dve': nc.vector, 'pool': nc.gpsimd, 'sp': nc.sync}
        self.sem = {e: es.enter_context(nc.semaphore('s_' + e)) for e in ('pe', 'act', 'dve', 'pool')}
        self.cnt = {e: 0 for e in self.sem}
        self.dq = {}
        for q in ('sp', 'pool'):
            self.dq[q] = {'sems': [es.enter_context(nc.semaphore(f'd_{q}{i}')) for i in range(ndma)],
                          'cnt': [0] * ndma, 'nxt': 0}
        self.seen = {e: {} for e in self.eng}
        self.lastw = {}
        self.rd = {}
        self.n_inst = 0
        self.n_wait = 0
        self.pend = {}

    def _semobj(self, sk):
        if sk[0] == 'e':
            return self.sem[sk[1]]
        return self.dq[sk[1]]['sems'][sk[2]]

    def _wait(self, eng, tk):
        sk, val, src = tk
        if self.seen[eng].get(sk, 0) >= val:
            return
        self.eng[eng].wait_ge(self._semobj(sk), val)
        self.seen[eng][sk] = val
        self.n_wait += 1

    def _deps(self, eng, reads, writes):
        deps = []
        for k in reads:
            if k in self.lastw:
                deps.append(self.lastw[k])
        for k in writes:
            if k in self.lastw:
                deps.append(self.lastw[k])
            for tk in self.rd.get(k, {}).values():
                deps.append(tk)
        for tk in deps:
            if eng == 'pe' and tk[2] == 'pe':
                continue
            self._wait(eng, tk)

    def _record(self, tk, reads, writes):
        for k in writes:
            self.lastw[k] = tk
            self.rd[k] = {}
        for k in reads:
            self.rd.setdefault(k, {})[tk[0]] = tk

    def emit(self, eng, fn, reads=(), writes=(), signal=True):
        self._deps(eng, reads, writes)
        inst = fn()
        self.n_inst += 1
        pr, pw = self.pend.setdefault(eng, ([], []))
        if not signal:
            pr.extend(reads); pw.extend(writes)
            return
        self.cnt[eng] += 1
        inst.then_inc(self.sem[eng], 1)
        self._record((('e', eng), self.cnt[eng], eng), list(reads) + pr, list(writes) + pw)
        self.pend[eng] = ([], [])

    def dma(self, q, out, in_, reads=(), writes=()):
        Q = self.dq[q]
        i = Q['nxt']; Q['nxt'] = (i + 1) % len(Q['sems'])
        sk = ('d', q, i)
        if Q['cnt'][i] > 0:
            self._wait(q, (sk, Q['cnt'][i], 'dma'))
        self._deps(q, reads, writes)
        inst = self.eng[q].dma_start(out=out, in_=in_)
        Q['cnt'][i] += 16
        inst.then_inc(Q['sems'][i], 16)
        self._record((sk, Q['cnt'][i], 'dma'), reads, writes)
        self.n_inst += 1

    def barrier(self):
        for e in self.eng:
            for s in self.sem:
                if self.cnt[s] > 0:
                    self._wait(e, (('e', s), self.cnt[s], s))
            for q, Q in self.dq.items():
                for i, cval in enumerate(Q['cnt']):
                    if cval > 0:
                        self._wait(e, (('d', q, i), cval, 'dma'))


class _Stop(Exception):
    pass


def build_nc(T, debug=False, stop=None):
    NT = T // TT
    nc = bass.Bass("TRN2", target_bir_lowering=False)
    dr = lambda name, shape, dt=F32, kind="ExternalInput": nc.dram_tensor(name, list(shape), dt, kind=kind).ap()
    xT_d = dr("xT", [D, T])
    params_d = dr("params", [128, NPARAM])
    ada_d = dr("ada", [2, 128, 8, 6144])
    wfm_d = dr("w_fm", [2, NFM, 128, 8, 128])
    wv_d = dr("w_v", [2, 128, 8, 512])
    wo_d = dr("w_o", [2, 8, 128, 8, 128])
    wu_d = dr("w_u", [2, 44, 128, 8, 128])
    wd_d = dr("w_d", [2, 8, 128, NFF, 128])
    waup_d = dr("wa_up", [2, 128, 512])
    gup_d = dr("g_up", [2, 128, 512])
    vup_d = dr("v_up", [128, 512])
    gkup_d = dr("gk_up", [2, 128, 256])
    gkb_d = dr("gkb_bc", [2, 128, 256])
    w0bc_d = dr("w0_bc", [2, 128, 512])
    cb_d = dr("constb", [128, 7 * 128])
    cf_d = dr("constf", [128, 3 * 128])
    out_d = dr("outT", [D, T], F32, "ExternalOutput")
    s_fm = dr("s_fm", [2, NFM, 128, 8, 128], BF16, "Internal")
    s_v = dr("s_v", [2, 128, 8, 512], BF16, "Internal")
    s_o = dr("s_o", [2, 8, 128, 8, 128], BF16, "Internal")
    s_u = dr("s_u", [2, 44, 128, 8, 128], BF16, "Internal")
    s_d = dr("s_d", [2, 8, 128, NFF, 128], BF16, "Internal")
    dbg_d = {}
    if debug:
        for nm in ('d_rf', 'd_kf', 'd_vf', 'd_emL', 'd_kmod', 'd_kkn', 'd_asig', 'd_eL', 'd_eLp', 'd_ktr', 'd_btr'):
            dbg_d[nm] = dr("dbg_" + nm, [128, T], F32, "ExternalOutput")
        for nm in ('h0', 'oo0', 'xmix0', 'x0', 'h1', 'oo1', 'xmix1', 'x1', 'gla_o0', 'rwkv_y0'):
            rows = 512 if nm.startswith(('gla_o', 'rwkv_y')) else D
            dbg_d[nm] = dr("dbg_" + nm, [rows, T], F32, "ExternalOutput")

    with contextlib.ExitStack() as es:
        E = es.enter_context
        S = Sched(nc, es)
        sb = lambda name, shape, dt=F32: E(nc.sbuf_tensor("sb_" + name, list(shape), dt))

        prm = sb("prm", [128, NPARAM])
        cbt = sb("cbt", [128, 7 * 128], BF16)
        cft = sb("cft", [128, 3 * 128])
        waup = sb("waup", [128, 2, 512], BF16)
        gup = sb("gup", [128, 2, 512], BF16)
        vup = sb("vup", [128, 512], BF16)
        gkup = sb("gkup", [128, 2, 256], BF16)
        gkb = sb("gkb", [128, 2, 256])
        w0bc = sb("w0bc", [128, 2, 512])
        modt = sb("modt", [128, 2, 48])
        drv = sb("drv", [128, 2, 40])
        sc = sb("sc", [128, 8])
        ident = cbt[:, 0:128]; m_su = cbt[:, 128:256]; m_iu = cbt[:, 256:384]; m_sl = cbt[:, 384:512]
        m_suiu = cbt[:, 128:384]
        ones = cbt[:, 512:640]; bones = cbt[:, 640:768]; bones64 = cbt[:, 768:896]
        triW = cft[:, 0:256]; tri16 = cft[:, 256:384]

        def P(name, l=None):
            o, w = POFF[name if l is None else f'{name}{l}']
            return prm[:, o:o + w]

        def Pc(name, l, i):
            o, w = POFF[f'{name}{l}']
            return prm[:, o + i:o + i + 1]

        S.dma('sp', prm[:], params_d[:, :], writes=['prm'])
        S.dma('pool', cbt[:], cb_d[:, :], writes=['cbt'])
        S.dma('sp', cft[:], cf_d[:, :], writes=['cft'])
        for l in range(2):
            S.dma('pool', waup[:, l, :], waup_d[l], writes=['waup'])
            S.dma('pool', gup[:, l, :], gup_d[l], writes=['gup'])
            S.dma('pool', gkup[:, l, :], gkup_d[l], writes=['gkup'])
            S.dma('sp', gkb[:, l, :], gkb_d[l], writes=['gkb'])
            S.dma('sp', w0bc[:, l, :], w0bc_d[l], writes=['w0bc'])
        S.dma('pool', vup[:], vup_d[:, :], writes=['vup'])

        for l in range(2):
            for j in range(NFM):
                S.dma('pool', s_fm[l, j], wfm_d[l, j], writes=[f's_fm{l}'])
            for kc in range(8):
                S.dma('pool', s_v[l, :, kc, :], wv_d[l, :, kc, :], writes=[f's_v{l}'])
            for j in range(8):
                S.dma('pool', s_o[l, j], wo_d[l, j], writes=[f's_o{l}'])
            for j in range(44):
                S.dma('pool', s_u[l, j], wu_d[l, j], writes=[f's_u{l}'])
            for j in range(8):
                for h2 in range(2):
                    S.dma('pool', s_d[l, j, :, h2 * 11:(h2 + 1) * 11, :], wd_d[l, j, :, h2 * 11:(h2 + 1) * 11, :],
                          writes=[f's_d{l}'])

        NB = 6
        banks = [E(nc.psum_tensor(f"pb{i}", [128, 512], F32)) for i in range(NB)]
        ptbs = [E(nc.psum_tensor(f"ptb{i}", [128, 1024], BF16)) for i in range(2)]
        st = {'b': 0, 't': 0}

        def bank():
            i = st['b']; st['b'] = (i + 1) % NB
            return banks[i], f'pb{i}'

        def tbank():
            i = st['t']; st['t'] = (i + 1) % 2
            return ptbs[i][:, 0:512], f'pt{i}'

        def mm(out, lhsT, rhs, start, stop, r, w):
            S.emit('pe', lambda: nc.tensor.matmul(out, lhsT, rhs, start=start, stop=stop), r, w, signal=bool(stop))

        def tr(out, in_, r, w):
            S.emit('pe', lambda: nc.tensor.transpose(out, in_, ident), list(r) + ['cbt'], w)

        def act(out, in_, func, r, w, scale=1.0, bias=0.0):
            S.emit('act', lambda: nc.scalar.activation(out=out, in_=in_, func=func, bias=bias, scale=scale), r, w)

        def tt(eng, out, in0, in1, op, r, w):
            e = nc.vector if eng == 'dve' else nc.gpsimd
            S.emit(eng, lambda: e.tensor_tensor(out=out, in0=in0, in1=in1, op=op), r, w)

        def ts(eng, out, in0, s1, op0, r, w, s2=None, op1=None):
            e = nc.vector if eng == 'dve' else nc.gpsimd
            if op1 is None:
                S.emit(eng, lambda: e.tensor_scalar(out=out, in0=in0, scalar1=s1, scalar2=None, op0=op0), r, w)
            else:
                S.emit(eng, lambda: e.tensor_scalar(out=out, in0=in0, scalar1=s1, scalar2=s2, op0=op0, op1=op1), r, w)

        def stt(out, in0, scalar, in1, op0, op1, r, w):
            S.emit('dve', lambda: nc.vector.scalar_tensor_tensor(out=out, in0=in0, scalar=scalar, in1=in1,
                                                                  op0=op0, op1=op1), r, w)

        def cp(eng, out, in_, r, w):
            if eng == 'act':
                S.emit('act', lambda: nc.scalar.activation(out=out, in_=in_, func=AF.Copy), r, w)
            else:
                e = nc.vector if eng == 'dve' else nc.gpsimd
                S.emit(eng, lambda: e.tensor_copy(out=out, in_=in_), r, w)

        def recip(out, in_, r, w):
            S.emit('dve', lambda: nc.vector.reciprocal(out=out, in_=in_), r, w)

        def memz(eng, ap, w):
            e = nc.vector if eng == 'dve' else nc.gpsimd
            S.emit(eng, lambda: e.memset(ap, 0.0), [], w)

        act(sc[:], P('c'), AF.Silu, ['prm'], ['sc'])
        with contextlib.ExitStack() as es2:
            adat = es2.enter_context(nc.sbuf_tensor("adat", [128, 8, 1024], F32))
            for l in range(2):
                for g6 in range(6):
                    S.dma('sp', adat[:], ada_d[l, :, :, g6 * 1024:(g6 + 1) * 1024], writes=['adat'])
                    ps, pk = bank()
                    for j in range(8):
                        for kc in range(8):
                            mm(ps[:, j:j + 1], adat[:, kc, j * 128:(j + 1) * 128], sc[:, kc:kc + 1],
                               kc == 0, kc == 7, ['adat', 'sc'], [pk])
                    tt('dve', modt[:, l, g6 * 8:(g6 + 1) * 8], ps[:, 0:8], P('ada_b', l)[:, g6 * 8:(g6 + 1) * 8],
                       ALU.add, [pk, 'prm'], ['modt'])
            S.barrier()
        for l in range(2):
            ts('dve', drv[:, l, 0:8], modt[:, l, 8:16], 1.0, ALU.add, ['modt'], ['drv'])
            tt('dve', drv[:, l, 0:8], drv[:, l, 0:8], P('nmg', l), ALU.mult, ['drv', 'prm'], ['drv'])
            ts('dve', drv[:, l, 8:16], modt[:, l, 32:40], 1.0, ALU.add, ['modt'], ['drv'])
            tt('dve', drv[:, l, 8:16], drv[:, l, 8:16], P('nfg', l), ALU.mult, ['drv', 'prm'], ['drv'])
            ts('dve', drv[:, l, 16:20], P('k_a', l), -1.0, ALU.mult, ['prm'], ['drv'], 1.0, ALU.add)
            ts('dve', drv[:, l, 20:35], P('mu', l), -1.0, ALU.mult, ['prm'], ['drv'], 1.0, ALU.add)
        A_m = lambda l, kc: drv[:, l, kc:kc + 1]
        A_f = lambda l, kc: drv[:, l, 8 + kc:9 + kc]
        omk = lambda l, ct: drv[:, l, 16 + ct:17 + ct]
        modc = lambda l, m, kc: modt[:, l, m * 8 + kc:m * 8 + kc + 1]

        xT = sb("xT", [128, 8, TT])
        hT = sb("hT", [128, 8, TT], BF16)
        sq = sb("sq", [128, 8, TT], BF16)
        rstd = sb("rstd", [128, TT])
        ftmp = [sb(f"ftmp{i}", [128, TT]) for i in range(3)]
        NA = 8
        ringA = [sb(f"ra{i}", [128, 8, 128], BF16) for i in range(NA)]
        wvb = sb("wvb", [128, 8, 512], BF16)
        ringD = [sb(f"rd{i}", [128, NFF, 128], BF16) for i in range(2)]
        rs = {'a': 0, 'd': 0, 'f': 0}

        def loadA(src, key):
            i = rs['a']; rs['a'] = (i + 1) % NA
            S.dma('sp', ringA[i][:], src, reads=[key], writes=[f'ra{i}'])
            return ringA[i], f'ra{i}'

        def loadD(src, key):
            i = rs['d']; rs['d'] = (i + 1) % 2
            S.dma('sp', ringD[i][:], src, reads=[key], writes=[f'rd{i}'])
            return ringD[i], f'rd{i}'

        def ft():
            i = rs['f']; rs['f'] = (i + 1) % 3
            return ftmp[i], f'ftmp{i}'

        ob = sb("ob", [128, 8, TT], BF16)
        qf = sb("qf", [128, 2, TT]); kfg = sb("kfg", [128, 2, TT])
        sg = sb("sg", [128, 4, TT], BF16)
        zb = sb("zb", [128, TT], BF16)
        vtok = sb("vtok", [128, NCH, 512], BF16)
        sptok = sb("sptok", [128, NCH, 256])
        eG = sb("eG", [128, 2, TT]); emG = sb("emG", [128, TT])
        qt = sb("qt", [128, 2, TT], BF16); ktg = sb("ktg", [128, 2, TT], BF16)
        ktgT = sb("ktgT", [128, NCH, 256], BF16)
        ATb = [sb(f"ATb{i}", [128, 128], BF16) for i in range(4)]
        osb = sb("osb", [128, 4, TT])
        Sg = sb("Sg", [128, 2, 2, 128]); Sgb = sb("Sgb", [128, 2, 2, 128], BF16)
        ptmp = [sb(f"ptmp{i}", [128, TT + 2]) for i in range(2)]
        dtmp = [sb(f"dtmp{i}", [128, TT]) for i in range(2)]
        halo = sb("halo", [128, 2, 16])
        wab = sb("wab", [128, TT], BF16)
        sgl = sb("sgl", [128, TT], BF16)
        vl = sb("vl", [128, TT], BF16)
        rkv = [[sb(f"rkv{i}_{j}", [128, TT]) for j in range(3)] for i in range(2)]
        vfirst = sb("vfirst", [128, 4, TT])
        nlw = sb("nlw", [128, NCH, 512])
        eL = sb("eL", [128, 4, TT])
        emL = [sb(f"emL{i}", [128, TT]) for i in range(2)]
        eLp = [sb(f"eLp{i}", [128, TT]) for i in range(2)]
        asig = [sb(f"asig{i}", [128, TT]) for i in range(2)]
        kkn = [sb(f"kkn{i}", [128, TT]) for i in range(2)]
        kmod = [sb(f"kmod{i}", [128, TT]) for i in range(2)]
        kk2 = sb("kk2", [128, TT], BF16)
        rkb = sb("rkb", [128, TT], BF16)
        vb = sb("vb", [128, TT], BF16)
        big = sb("big", [128, 6144], BF16)
        tokT = big[:, 0:3072].rearrange("p (c t x) -> p c t x", c=NCH, t=4)
        ar = big[:, 3072:5120].rearrange("p (t x) -> p t x", t=4)
        ktr = big[:, 5120:6144].rearrange("p (t x) -> p t x", t=4)
        btr = sb("btr", [128, 4, TT], BF16)
        BIGK = ['tokT', 'ar', 'ktr']
        gate = sb("gate", [128, 4, TT], BF16)
        ysb = sb("ysb", [128, 4, TT])
        bonus = sb("bonus", [128, 4, TT])
        W0 = [sb(f"W0_{i}", [128, 384], BF16) for i in range(8)]
        Wpp = [[sb(f"Wpp{i}_{j}", [128, 384], BF16) for j in range(2)] for i in range(8)]
        MN2 = [[sb(f"MN{q}_{i}", [128, 256], BF16) for i in range(8)] for q in range(2)]
        Nbr2 = [[sb(f"Nbr{q}_{i}", [128, 128], BF16) for i in range(8)] for q in range(2)]
        Tfin2 = [[sb(f"Tfin{q}_{i}", [128, 128], BF16) for i in range(8)] for q in range(2)]
        Rn2 = [[sb(f"Rn{q}_{i}", [128, 128], BF16) for i in range(8)] for q in range(2)]
        XTb = [sb(f"XTb{i}", [128, 128], BF16) for i in range(4)]
        UTb = [sb(f"UTb{i}", [128, 128], BF16) for i in range(4)]
        Sr = sb("Sr", [128, 2, 4, 64]); Srb = sb("Srb", [128, 2, 4, 64], BF16)
        mbuf = big[:, 0:NFF * TT].rearrange("p (j x) -> p j x", j=NFF)
        acc = [emL[0], emL[1], eLp[0], eLp[1]]
        acck = ['emL0', 'emL1', 'eLp0', 'eLp1']
        sgt = [asig[0], asig[1]]
        sgtk = ['asig0', 'asig1']
        chalo = sb("chalo", [128, 2, 2, 44, 2])

        memz('dve', Sg[:], ['Sg']); memz('dve', Sgb[:], ['Sgb'])
        memz('dve', Sr[:], ['Sr']); memz('dve', Srb[:], ['Srb'])
        memz('pool', halo[:], ['halo']); memz('pool', chalo[:], ['chalo'])
        memz('pool', vl[:], ['vl'])
        for h in range(8):
            cp('pool', W0[h][:, 256:384], ident, ['cbt'], [f'W0_{h}'])

        def dump(nm, ap, rows, ti, r, q='sp'):
            if debug and nm in dbg_d:
                dst = dbg_d[nm].rearrange("(k p) t -> p k t", p=128)[:, :, ti * TT:(ti + 1) * TT]
                S.dma(q, dst, ap, reads=r, writes=['dbg_' + nm])

        def rmsnorm_mod(l, Afn, shift_m):
            act(sq[:], xT[:], AF.Square, ['xT'], [f'sq{i}' for i in range(8)])
            ps, pk = bank()
            for kc in range(8):
                mm(ps[:, 0:TT], ones, sq[:, kc, :], kc == 0, kc == 7, ['cbt', f'sq{kc}'], [pk])
            act(rstd[:], ps[:, 0:TT], AF.Ln, [pk], ['rstd'], scale=1.0 / D, bias=NORM_EPS)
            act(rstd[:], rstd[:], AF.Exp, ['rstd'], ['rstd'], scale=-0.5)
            for kc in range(8):
                f_, fk = ft()
                stt(f_[:], xT[:, kc, :], Afn(l, kc), rstd[:], ALU.mult, ALU.mult, ['xT', 'drv', 'rstd'], [fk])
                act(hT[:, kc, :], f_[:], AF.Identity, [fk, 'modt'], ['hT'], bias=modc(l, shift_m, kc))

        def proj_fm(l, idx, m):
            slot, sk = loadA(s_fm[l, idx], f's_fm{l}')
            ps, pk = bank()
            for kc in range(8):
                mm(ps[0:m, 0:TT], slot[:, kc, 0:m], hT[:, kc, :], kc == 0, kc == 7, [sk, 'hT'], [pk])
            return ps, pk

        def lerp(l, idx, ps, pk, m, out_ap, okey, pi):
            mi = idx - RW0
            pt_, ptk = ptmp[pi], f'ptmp{pi}'
            dt_, dtk = dtmp[pi], f'dtmp{pi}'
            act(dt_[0:m, :], ps[0:m, 0:TT], AF.Identity, [pk, 'drv'], [dtk], scale=drv[0:m, l, 20 + mi:21 + mi])
            cp('act', pt_[0:m, 1:TT + 1], ps[0:m, 0:TT], [pk], [ptk])
            cp('pool', pt_[0:m, 0:1], halo[0:m, l, mi:mi + 1], ['halo'], [ptk])
            cp('pool', halo[0:m, l, mi:mi + 1], pt_[0:m, TT:TT + 1], [ptk], ['halo'])
            o, w = POFF[f'mu{l}']
            stt(out_ap, pt_[0:m, 0:TT], prm[0:m, o + mi:o + mi + 1], dt_[0:m, :], ALU.mult, ALU.add,
                [dtk, ptk, 'prm'], [okey])

        def inter(*gens):
            gens = [g for g in gens if g is not None]
            while gens:
                for g in list(gens):
                    try:
                        next(g)
                    except StopIteration:
                        gens.remove(g)
                yield

        def chk(n):
            if stop is not None and stop == n:
                raise _Stop()

        def _main():
          for ti in range(NT if stop != 0 else 0):
            tsl = slice(ti * TT, (ti + 1) * TT)
            S.dma('sp', xT[:], xT_d.rearrange("(k p) t -> p k t", p=128)[:, :, tsl], writes=['xT'])
            for l in range(2):
                rmsnorm_mod(l, A_m, 0)
                dump(f'h{l}', hT[:], D, ti, ['hT'], 'pool')
                def gla_thread():
                    ps, pk = proj_fm(l, 0, 16)
                    cp('act', zb[0:16, :], ps[0:16, 0:TT], [pk], ['zb'])
                    yield
                    for i in range(2):
                        ps, pk = proj_fm(l, 1 + i, 128)
                        act(qf[:, i, :], ps[:, 0:TT], AF.Copy, [pk], ['qf'], scale=0.125)
                        yield
                    for i in range(2):
                        ps, pk = proj_fm(l, 3 + i, 128)
                        cp('act', kfg[:, i, :], ps[:, 0:TT], [pk], ['kfg'])
                        yield
                    for i in range(4):
                        ps, pk = proj_fm(l, 5 + i, 128)
                        act(sg[:, i, :], ps[:, 0:TT], AF.Silu, [pk], ['sg'])
                        yield
                    chk(1)
                    S.dma('sp', wvb[:], s_v[l], reads=[f's_v{l}'], writes=['wvb'])
                    for c in range(NCH):
                        ps, pk = bank()
                        for kc in range(8):
                            mm(ps[:, 0:512], hT[:, kc, c * 128:(c + 1) * 128], wvb[:, kc, :], kc == 0, kc == 7,
                               ['hT', 'wvb'], [pk])
                        cp('dve', vtok[:, c, :], ps[:, 0:512], [pk], ['vtok'])
                        yield
                    for c in range(NCH):
                        ps, pk = bank()
                        mm(ps[:, 0:256], zb[0:16, c * 128:(c + 1) * 128], gkup[0:16, l, :], True, True, ['zb', 'gkup'], [pk])
                        f_, fk = ft()
                        tt('dve', f_[:, 0:256], ps[:, 0:256], gkb[:, l, :], ALU.add, [pk, 'gkb'], [fk])
                        act(f_[:, 0:256], f_[:, 0:256], AF.Exp, [fk], [fk], scale=-1.0)
                        act(sptok[:, c, :], f_[:, 0:256], AF.Ln, [fk], ['sptok'], bias=1.0)
                        yield
                    for c2 in range(2):
                        ps, pk = bank()
                        for c in range(NCH):
                            mm(ps[:, c * 128:(c + 1) * 128], sptok[:, c, c2 * 128:(c2 + 1) * 128], tri16, True, True,
                               ['sptok', 'cft'], [pk])
                        act(eG[:, c2, :], ps[:, 0:TT], AF.Exp, [pk], ['eG'], scale=-1.0)
                        act(emG[:], ps[:, 0:TT], AF.Exp, [pk], ['emG'])
                        tt('dve', qt[:, c2, :], qf[:, c2, :], eG[:, c2, :], ALU.mult, ['qf', 'eG'], ['qt'])
                        tt('pool', ktg[:, c2, :], kfg[:, c2, :], emG[:], ALU.mult, ['kfg', 'emG'], ['ktg'])
                        for c in range(NCH):
                            tp, tk_ = tbank()
                            tr(tp[:, 0:128], ktg[:, c2, c * 128:(c + 1) * 128], ['ktg'], [tk_])
                            cp('act', ktgT[:, c, c2 * 128:(c2 + 1) * 128], tp[:, 0:128], [tk_], ['ktgT'])
                            yield
                    for c in range(NCH):
                        csl = slice(c * 128, (c + 1) * 128)
                        for h in range(4):
                            c2 = h // 2; po = (h % 2) * 64
                            ps, pk = bank()
                            mm(ps[:, 0:128], ktg[po:po + 64, c2, csl], qt[po:po + 64, c2, csl], True, True, ['ktg', 'qt'], [pk])
                            tt('dve', ATb[h][:], ps[:, 0:128], m_iu, ALU.mult, [pk, 'cbt'], [f'ATb{h}'])
                            ps2, pk2 = bank()
                            mm(ps2[:, 0:128], vtok[:, c, h * 128:(h + 1) * 128], ATb[h][:], True, False, ['vtok', f'ATb{h}'], [pk2])
                            mm(ps2[:, 0:128], Sgb[po:po + 64, l, c2, :], qt[po:po + 64, c2, csl], False, True, ['Sgb', 'qt'], [pk2])
                            cp('act', osb[:, h, csl], ps2[:, 0:128], [pk2], ['osb'])
                            yield
                        for c2 in range(2):
                            ps, pk = bank()
                            for hh in range(2):
                                h = c2 * 2 + hh; po = hh * 64
                                mm(ps[po:po + 64, 0:128], ktgT[:, c, c2 * 128 + po:c2 * 128 + po + 64],
                                   vtok[:, c, h * 128:(h + 1) * 128], True, True, ['ktgT', 'vtok'], [pk])
                            f_, fk = ft()
                            tt('dve', f_[:, 0:128], ps[:, 0:128], Sg[:, l, c2, :], ALU.add, [pk, 'Sg'], [fk])
                            ts('dve', Sg[:, l, c2, :], f_[:, 0:128], eG[:, c2, c * 128 + 127:c * 128 + 128], ALU.mult,
                               [fk, 'eG'], ['Sg'])
                            cp('pool', Sgb[:, l, c2, :], Sg[:, l, c2, :], ['Sg'], ['Sgb'])
                            yield
                    if debug:
                        dst = dbg_d.get(f'gla_o{l}')
                        if dst is not None:
                            S.dma('sp', dst.rearrange("(k p) t -> p k t", p=128)[:, :, tsl], osb[:], reads=['osb'],
                                  writes=['dbg_g'])

                def rwkv_thread():
                    chk(2)
                    ps, pk = proj_fm(l, 9, 128)
                    lerp(l, 9, ps, pk, 128, ftmp[0][:], 'ftmp0', 0)
                    act(wab[0:64, :], ftmp[0][0:64, :], AF.Tanh, ['ftmp0'], ['wab'])
                    cp('pool', wab[64:128, :], ftmp[0][64:128, :], ['ftmp0'], ['wab'])
                    yield
                    ps, pk = proj_fm(l, 10, 128)
                    lerp(l, 10, ps, pk, 128, ftmp[1][:], 'ftmp1', 1)
                    act(sgl[:], ftmp[1][:], AF.Sigmoid, ['ftmp1'], ['sgl'])
                    yield
                    if l == 1:
                        ps, pk = proj_fm(l, 11, 32)
                        lerp(l, 11, ps, pk, 32, ftmp[2][0:32, :], 'ftmp2', 0)
                        cp('pool', vl[0:32, :], ftmp[2][0:32, :], ['ftmp2'], ['vl'])
                        yield
                    chk(21)
                    for c in range(NCH):
                        ps, pk = bank()
                        mm(ps[:, 0:512], wab[0:64, c * 128:(c + 1) * 128], waup[0:64, l, :], True, True, ['wab', 'waup'], [pk])
                        tt('dve', nlw[:, c, :], ps[:, 0:512], w0bc[:, l, :], ALU.add, [pk, 'w0bc'], ['nlw'])
                        act(nlw[:, c, :], nlw[:, c, :], AF.Sigmoid, ['nlw'], ['nlw'])
                        yield
                    chk(22)
                    for ct in range(4):
                        pi = ct % 2
                        cts = slice(ct * 128, (ct + 1) * 128)
                        rf, kf, vf = rkv[pi]
                        rk_, kk_, vk_ = f'rkv{pi}_0', f'rkv{pi}_1', f'rkv{pi}_2'
                        ps, pk = proj_fm(l, 12 + 3 * ct, 128)
                        lerp(l, 12 + 3 * ct, ps, pk, 128, rf[:], rk_, 0)
                        yield
                        ps, pk = proj_fm(l, 13 + 3 * ct, 128)
                        lerp(l, 13 + 3 * ct, ps, pk, 128, kf[:], kk_, 1)
                        yield
                        ps, pk = proj_fm(l, 14 + 3 * ct, 128)
                        lerp(l, 14 + 3 * ct, ps, pk, 128, vf[:], vk_, 0)
                        yield
                        ps, pk = bank()
                        for c in range(NCH):
                            mm(ps[:, c * 256:(c + 1) * 256], nlw[:, c, cts], triW, True, True, ['nlw', 'cft'], [pk])
                        psv = ps[:, 0:NCH * 256].rearrange("p (c x) -> p c x", x=256)
                        v3 = lambda ap: ap.rearrange("p (c x) -> p c x", x=128)
                        act(v3(eL[:, ct, :]), psv[:, :, 0:128], AF.Exp, [pk], ['eL'], scale=-1.0)
                        act(v3(emL[pi][:]), psv[:, :, 0:128], AF.Exp, [pk], [f'emL{pi}'])
                        act(v3(eLp[pi][:]), psv[:, :, 128:256], AF.Exp, [pk], [f'eLp{pi}'], scale=-1.0)
                        yield
                        chk(23)
                        ps, pk = bank()
                        mm(ps[:, 0:TT], waup[64:128, l, cts], wab[64:128, :], True, True, ['waup', 'wab'], [pk])
                        act(asig[pi][:], ps[:, 0:TT], AF.Sigmoid, [pk, 'prm'], [f'asig{pi}'], bias=Pc('a0', l, ct))
                        yield
                        ps, pk = bank()
                        mm(ps[:, 0:TT], gup[:, l, cts], sgl[:], True, True, ['gup', 'sgl'], [pk])
                        cp('act', gate[:, ct, :], ps[:, 0:TT], [pk], ['gate'])
                        yield
                        if l == 1:
                            ps, pk = bank()
                            mm(ps[:, 0:TT], vup[0:32, cts], vl[0:32, :], True, True, ['vup', 'vl'], [pk])
                            f_, fk = ft()
                            act(f_[:], ps[:, 0:TT], AF.Sigmoid, [pk, 'prm'], [fk], bias=Pc('v0', 1, ct))
                            f2, fk2 = ft()
                            tt('pool', f2[:], vfirst[:, ct, :], vf[:], ALU.subtract, ['vfirst', vk_], [fk2])
                            tt('pool', f2[:], f2[:], f_[:], ALU.mult, [fk2, fk], [fk2])
                            tt('pool', vf[:], vf[:], f2[:], ALU.add, [vk_, fk2], [vk_])
                            yield
                        else:
                            cp('pool', vfirst[:, ct, :], vf[:], [vk_], ['vfirst'])
                            yield
                        chk(24)
                        act(kk2[:], kf[:], AF.Square, [kk_, 'prm'], ['kk2'], scale=Pc('k_k', l, ct))
                        ps, pk = bank()
                        mm(ps[:, 0:TT], bones, kk2[:], True, True, ['cbt', 'kk2'], [pk])
                        f_, fk = ft()
                        ts('dve', f_[:], ps[:, 0:TT], 1e-24, ALU.max, [pk], [fk])
                        act(f_[:], f_[:], AF.Ln, [fk], [fk])
                        act(f_[:], f_[:], AF.Exp, [fk], [fk], scale=-0.5)
                        stt(kkn[pi][:], kf[:], Pc('k_k', l, ct), f_[:], ALU.mult, ALU.mult, [kk_, 'prm', fk], [f'kkn{pi}'])
                        yield
                        f2, fk2 = ft()
                        ts('dve', f2[:], asig[pi][:], Pc('k_a', l, ct), ALU.mult, [f'asig{pi}', 'prm', 'drv'], [fk2],
                           omk(l, ct), ALU.add)
                        tt('pool', kmod[pi][:], kf[:], f2[:], ALU.mult, [kk_, fk2], [f'kmod{pi}'])
                        tt('dve', ktr[:, ct, :], kmod[pi][:], emL[pi][:], ALU.mult, [f'kmod{pi}', f'emL{pi}'], ['ktr'])
                        yield
                        f3, fk3 = ft()
                        tt('pool', f3[:], kkn[pi][:], asig[pi][:], ALU.mult, [f'kkn{pi}', f'asig{pi}'], [fk3])
                        tt('dve', btr[:, ct, :], f3[:], emL[pi][:], ALU.mult, [fk3, f'emL{pi}'], ['btr'])
                        yield
                        arv = ar[:, ct, :].rearrange("p (c x) -> p c x", x=256)
                        stt(arv[:, :, 0:128], v3(kkn[pi][:]), -1.0, v3(eLp[pi][:]), ALU.mult, ALU.mult,
                            [f'kkn{pi}', f'eLp{pi}'], ['ar'])
                        tt('dve', arv[:, :, 128:256], v3(rf[:]), v3(eL[:, ct, :]), ALU.mult, [rk_, 'eL'], ['ar'])
                        yield
                        chk(25)
                        stt(rkb[:], rf[:], Pc('r_k', l, ct), kmod[pi][:], ALU.mult, ALU.mult, [rk_, 'prm', f'kmod{pi}'], ['rkb'])
                        ps, pk = bank()
                        mm(ps[:, 0:TT], bones, rkb[:], True, True, ['cbt', 'rkb'], [pk])
                        tt('dve', bonus[:, ct, :], ps[:, 0:TT], vf[:], ALU.mult, [pk, vk_], ['bonus'])
                        yield
                        if debug and ct == 0:
                            for nm_, ap_, k_ in (('d_rf', rf[:], rk_), ('d_kf', kf[:], kk_), ('d_vf', vf[:], vk_), ('d_emL', emL[pi][:], f'emL{pi}'),
                                                 ('d_kmod', kmod[pi][:], f'kmod{pi}'), ('d_kkn', kkn[pi][:], f'kkn{pi}'), ('d_asig', asig[pi][:], f'asig{pi}'),
                                                 ('d_eL', eL[:, ct, :], 'eL'), ('d_eLp', eLp[pi][:], f'eLp{pi}'), ('d_ktr', ktr[:, ct, :], 'ktr'), ('d_btr', btr[:, ct, :], 'btr')):
                                S.dma('pool', dbg_d[nm_][:, tsl], ap_, reads=[k_], writes=['dbg_' + nm_])
                        chk(251)
                        cp('pool', vb[:], vf[:], [vk_], ['vb'])
                        chk(252)
                        for c in range(NCH):
                            csl = slice(c * 128, (c + 1) * 128)
                            for ii, (src_, sk_) in enumerate(((vb[:, csl], 'vb'), (ktr[:, ct, csl], 'ktr'), (btr[:, ct, csl], 'btr'))):
                                tp, tk_ = tbank()
                                import os
                                if str(ii) not in os.environ.get('XII', '012'):
                                    continue
                                if os.environ.get('XTR', '1') == '1':
                                    tr(tp[:, 0:128], src_, [sk_], [tk_])
                                if os.environ.get('XCP', '1') == '1':
                                    cp(os.environ.get('XENG', 'act'), tokT[:, c, ct, ii * 128:(ii + 1) * 128], tp[:, 0:128], [tk_], ['tokT'])
                                    yield
                        chk(26 + ct)

                    chk(3)
                    def stage_A(c):
                        csl = slice(c * 128, (c + 1) * 128)
                        q = c % 2
                        MN, Nbr, Tfin, Rn = MN2[q], Nbr2[q], Tfin2[q], Rn2[q]
                        for ct in range(4):
                            for hh in range(2):
                                h = ct * 2 + hh; po = hh * 64
                                kt_h = ktr[po:po + 64, ct, csl]; bt_h = btr[po:po + 64, ct, csl]
                                ar_h = ar[po:po + 64, ct, c * 256:(c + 1) * 256]
                                at_h = ar[po:po + 64, ct, c * 256:c * 256 + 128]
                                ps, pk = bank()
                                mm(ps[:, 0:256], kt_h, ar_h, True, True, ['ktr', 'ar'], [pk])
                                mm(ps[:, 256:512], bt_h, ar_h, True, True, ['btr', 'ar'], [pk])
                                ps2, pk2 = bank()
                                mm(ps2[:, 0:128], at_h, bt_h, True, True, ['ar', 'btr'], [pk2])
                                tt('dve', MN[h][:], ps[:, 0:256], m_suiu, ALU.mult, [pk, 'cbt'], [f'MN{q}_{h}'])
                                tt('dve', W0[h][:, 128:256], ps[:, 256:384], m_su, ALU.mult, [pk, 'cbt'], [f'W0_{h}'])
                                tt('dve', Nbr[h][:], ps[:, 384:512], m_iu, ALU.mult, [pk, 'cbt'], [f'Nbr{q}_{h}'])
                                tt('dve', W0[h][:, 0:128], ps2[:, 0:128], m_sl, ALU.mult, [pk2, 'cbt'], [f'W0_{h}'])
                                yield
                        for k in range(7):
                            for h in range(8):
                                cur, ck = (W0[h], f'W0_{h}') if k == 0 else (Wpp[h][(k - 1) % 2], f'Wpp{h}_{(k - 1) % 2}')
                                ps, pk = bank()
                                if k < 6:
                                    nxt, nk = Wpp[h][k % 2], f'Wpp{h}_{k % 2}'
                                    mm(ps[:, 0:128], cur[:, 128:256], cur[:, 0:128], True, True, [ck], [pk])
                                    mm(ps[:, 128:384], cur[:, 0:128], cur[:, 128:384], True, True, [ck], [pk])
                                    cp('act', nxt[:, 0:256], ps[:, 0:256], [pk], [nk])
                                    tt('dve', nxt[:, 256:384], ps[:, 256:384], cur[:, 256:384], ALU.add, [pk, ck], [nk])
                                else:
                                    mm(ps[:, 0:128], cur[:, 0:128], cur[:, 256:384], True, True, [ck], [pk])
                                    tt('dve', Tfin[h][:], ps[:, 0:128], cur[:, 256:384], ALU.add, [pk, ck], [f'Tfin{q}_{h}'])
                                    psn, pkn = bank()
                                    mm(psn[:, 0:128], W0[h][:, 0:128], Tfin[h][:], True, True, [f'W0_{h}', f'Tfin{q}_{h}'], [pkn])
                                    fn_, fnk = ft()
                                    tt('dve', fn_[:, 0:128], psn[:, 0:128], Tfin[h][:], ALU.subtract, [pkn, f'Tfin{q}_{h}'], [fnk])
                                    tt('pool', Rn[h][:], fn_[:, 0:128], ident, ALU.add, [fnk, 'cbt'], [f'Rn{q}_{h}'])
                                if h % 2 == 1:
                                    yield

                    def stage_B(c):
                        csl = slice(c * 128, (c + 1) * 128)
                        q = c % 2
                        MN, Nbr, Tfin, Rn = MN2[q], Nbr2[q], Tfin2[q], Rn2[q]
                        for ct in range(4):
                            ps, pk = bank()
                            for hh in range(2):
                                h = ct * 2 + hh; po = hh * 64
                                at_h = ar[po:po + 64, ct, c * 256:c * 256 + 128]
                                mm(ps[:, hh * 64:(hh + 1) * 64], at_h, Srb[po:po + 64, l, ct, :], True, False, ['ar', 'Srb'], [pk])
                                mm(ps[:, hh * 64:(hh + 1) * 64], MN[h][:, 0:128], tokT[:, c, ct, po:po + 64], False, True,
                                   [f'MN{q}_{h}', 'tokT'], [pk])
                            cp('act', XTb[ct][:], ps[:, 0:128], [pk], [f'XTb{ct}'])
                            yield
                        for ct in range(4):
                            ps, pk = bank()
                            for hh in range(2):
                                h = ct * 2 + hh
                                mm(ps[:, hh * 64:(hh + 1) * 64], Tfin[h][:], XTb[ct][:, hh * 64:(hh + 1) * 64], True, True,
                                   [f'Tfin{q}_{h}', f'XTb{ct}'], [pk])
                            cp('act', UTb[ct][:], ps[:, 0:128], [pk], [f'UTb{ct}'])
                            yield
                        for ct in range(4):
                            ps, pk = bank()
                            for hh in range(2):
                                h = ct * 2 + hh
                                mm(ps[:, hh * 64:(hh + 1) * 64], Rn[h][:], UTb[ct][:, hh * 64:(hh + 1) * 64], True, True,
                                   [f'Rn{q}_{h}', f'UTb{ct}'], [pk])
                            tt('dve', XTb[ct][:], ps[:, 0:128], UTb[ct][:], ALU.add, [pk, f'UTb{ct}'], [f'XTb{ct}'])
                            yield
                        for ct in range(4):
                            ps, pk = bank()
                            for hh in range(2):
                                h = ct * 2 + hh; po = hh * 64
                                rt_h = ar[po:po + 64, ct, c * 256 + 128:(c + 1) * 256]
                                o_ = ps[po:po + 64, 0:128]
                                mm(o_, Srb[po:po + 64, l, ct, :], rt_h, True, False, ['Srb', 'ar'], [pk])
                                mm(o_, tokT[:, c, ct, po:po + 64], MN[h][:, 128:256], False, False, ['tokT', f'MN{q}_{h}'], [pk])
                                mm(o_, XTb[ct][:, hh * 64:(hh + 1) * 64], Nbr[h][:], False, True, [f'XTb{ct}', f'Nbr{q}_{h}'], [pk])
                            cp('act', ysb[:, ct, csl], ps[:, 0:128], [pk], [f'ysb{ct}'])
                            ps2, pk2 = bank()
                            for hh in range(2):
                                po = hh * 64
                                o_ = ps2[po:po + 64, 0:64]
                                mm(o_, tokT[:, c, ct, 128 + po:128 + po + 64], tokT[:, c, ct, po:po + 64], True, False, ['tokT'], [pk2])
                                mm(o_, tokT[:, c, ct, 256 + po:256 + po + 64], XTb[ct][:, hh * 64:(hh + 1) * 64], False, True,
                                   ['tokT', f'XTb{ct}'], [pk2])
                            f_, fk = ft()
                            tt('dve', f_[:, 0:64], ps2[:, 0:64], Sr[:, l, ct, :], ALU.add, [pk2, 'Sr'], [fk])
                            ts('dve', Sr[:, l, ct, :], f_[:, 0:64], eL[:, ct, c * 128 + 127:c * 128 + 128], ALU.mult,
                               [fk, 'eL'], ['Sr'])
                            cp('pool', Srb[:, l, ct, :], Sr[:, l, ct, :], ['Sr'], ['Srb'])
                            yield

                    yield from stage_A(0)
                    for c in range(NCH):
                        yield from inter(stage_B(c), stage_A(c + 1) if c + 1 < NCH else None)

                for _ in inter(gla_thread(), rwkv_thread()):
                    pass
                if debug:
                    dst = dbg_d.get(f'rwkv_y{l}')
                    if dst is not None:
                        S.dma('sp', dst.rearrange("(k p) t -> p k t", p=128)[:, :, tsl], ysb[:], reads=[f'ysb{i}' for i in range(4)],
                              writes=['dbg_y'])

                chk(4)
                rsG = [(eLp[0], 'eLp0'), (eLp[1], 'eLp1'), (emL[0], 'emL0'), (emL[1], 'emL1')]
                rsR = [(kkn[0], 'kkn0'), (kkn[1], 'kkn1'), (kmod[0], 'kmod0'), (kmod[1], 'kmod1')]
                pg = {}
                for h in range(4):
                    act(sq[:, h, :], osb[:, h, :], AF.Square, ['osb'], [f'sq{h}'])
                for ct in range(4):
                    cp('dve' if ct < 2 else 'act', hT[:, ct, :], ysb[:, ct, :], [f'ysb{ct}'], [f'hTy{ct}', 'hT'])
                for h in range(4):
                    ps, pk = bank(); pg[h] = (ps, pk)
                    mm(ps[:, 0:TT], ones, sq[:, h, :], True, True, ['cbt', f'sq{h}'], [pk])
                for h in range(4):
                    ps, pk = pg[h]
                    act(rsG[h][0][:], ps[:, 0:TT], AF.Ln, [pk], [rsG[h][1]], scale=1.0 / 128, bias=NORM_EPS)
                for ct in range(4):
                    ps, pk = bank(); pg[ct] = (ps, pk)
                    mm(ps[:, 0:TT], bones64, hT[:, ct, :], True, True, ['cbt', f'hTy{ct}', 'hT'], [pk])
                for ct in range(4):
                    ps, pk = pg[ct]
                    tt('dve', ysb[:, ct, :], ysb[:, ct, :], ps[:, 0:TT], ALU.subtract, [f'ysb{ct}', pk], [f'ysb{ct}'])
                for h in range(4):
                    act(rsG[h][0][:], rsG[h][0][:], AF.Exp, [rsG[h][1]], [rsG[h][1]], scale=-0.5)
                for ct in range(4):
                    act(sq[:, 4 + ct, :], ysb[:, ct, :], AF.Square, [f'ysb{ct}'], [f'sq{4 + ct}'])
                for ct in range(4):
                    ps, pk = bank(); pg[ct] = (ps, pk)
                    mm(ps[:, 0:TT], bones64, sq[:, 4 + ct, :], True, True, ['cbt', f'sq{4 + ct}'], [pk])
                for h in range(4):
                    tt('dve', osb[:, h, :], osb[:, h, :], rsG[h][0][:], ALU.mult, ['osb', rsG[h][1]], ['osb'])
                for ct in range(4):
                    ps, pk = pg[ct]
                    act(rsR[ct][0][:], ps[:, 0:TT], AF.Ln, [pk], [rsR[ct][1]], bias=GN_EPS)
                for h in range(4):
                    stt(ob[:, h, :], osb[:, h, :], P('gng', l), sg[:, h, :], ALU.mult, ALU.mult, ['osb', 'prm', 'sg'], ['ob'])
                for ct in range(4):
                    act(rsR[ct][0][:], rsR[ct][0][:], AF.Exp, [rsR[ct][1]], [rsR[ct][1]], scale=-0.5)
                for ct in range(4):
                    tt('dve', ysb[:, ct, :], ysb[:, ct, :], rsR[ct][0][:], ALU.mult, [f'ysb{ct}', rsR[ct][1]], [f'ysb{ct}'])
                for ct in range(4):
                    ts('dve', ysb[:, ct, :], ysb[:, ct, :], Pc('gn_g', l, ct), ALU.mult, [f'ysb{ct}', 'prm'], [f'ysb{ct}'],
                       Pc('gn_b', l, ct), ALU.add)
                for ct in range(4):
                    tt('pool', ysb[:, ct, :], ysb[:, ct, :], bonus[:, ct, :], ALU.add, [f'ysb{ct}', 'bonus'], [f'ysb{ct}'])
                for ct in range(4):
                    tt('dve', ob[:, 4 + ct, :], ysb[:, ct, :], gate[:, ct, :], ALU.mult, [f'ysb{ct}', 'gate'], ['ob'])
                dump(f'oo{l}', ob[:], D, ti, ['ob'], 'pool')

                for j in range(8):
                    slot, sk = loadA(s_o[l, j], f's_o{l}')
                    ps, pk = bank()
                    for kc in range(8):
                        mm(ps[:, 0:TT], slot[:, kc, :], ob[:, kc, :], kc == 0, kc == 7, [sk, 'ob'], [pk])
                    stt(xT[:, j, :], ps[:, 0:TT], modc(l, 2, j), xT[:, j, :], ALU.mult, ALU.add, [pk, 'modt', 'xT'], ['xT'])
                dump(f'xmix{l}', xT[:], D, ti, ['xT'])

                chk(5)
                rmsnorm_mod(l, A_f, 3)
                par = ti % 2
                def ffn_finish(jp, accs_):
                    s_, skk = sgt[jp % 2], sgtk[jp % 2]
                    act(s_[:], accs_[0][0][:], AF.Silu, [accs_[0][1]], [skk])
                    tt('pool', mbuf[:, jp, :], s_[:], accs_[1][0][:], ALU.mult, [skk, accs_[1][1]], [f'mbuf{jp}'] + BIGK)
                ffn_pend = None
                for j in range(NFF):
                    accs = []
                    for half in range(2):
                        u = 2 * j + half
                        slot, sk = loadA(s_u[l, u], f's_u{l}')
                        ps, pk = bank()
                        for kc in range(8):
                            mm(ps[:, 0:TT], slot[:, kc, :], hT[:, kc, :], kc == 0, kc == 7, [sk, 'hT'], [pk])
                        a_ = acc[half * 2 + (j % 2)]; ak = acck[half * 2 + (j % 2)]
                        o, w = POFF[f'cw{l}']
                        cw = lambda tap: prm[:, o + u * 3 + tap:o + u * 3 + tap + 1]
                        ob_, wb_ = POFF[f'cb{l}']
                        ub, ubk = ptmp[half], f'ptmp{half}'
                        act(a_[:], ps[:, 0:TT], AF.Identity, [pk, 'prm'], [ak], scale=cw(2), bias=prm[:, ob_ + u:ob_ + u + 1])
                        cp('act', ub[:, 2:TT + 2], ps[:, 0:TT], [pk], [ubk])
                        cp('pool', ub[:, 0:2], chalo[:, par, l, u, :], ['chalo'], [ubk])
                        cp('pool', chalo[:, 1 - par, l, u, :], ub[:, TT:TT + 2], [ubk], ['chalo'])
                        stt(a_[:], ub[:, 1:TT + 1], cw(1), a_[:], ALU.mult, ALU.add, [ubk, 'prm', ak], [ak])
                        stt(a_[:], ub[:, 0:TT], cw(0), a_[:], ALU.mult, ALU.add, [ubk, 'prm', ak], [ak])
                        accs.append((a_, ak))
                    if ffn_pend is not None:
                        ffn_finish(*ffn_pend)
                    ffn_pend = (j, accs)
                if ffn_pend is not None:
                    ffn_finish(*ffn_pend)
                for jo in range(8):
                    slot, sk = loadD(s_d[l, jo], f's_d{l}')
                    ps, pk = bank()
                    for kc in range(NFF):
                        mm(ps[:, 0:TT], slot[:, kc, :], mbuf[:, kc, :], kc == 0, kc == NFF - 1, [sk, f'mbuf{kc}'] + BIGK, [pk])
                    stt(xT[:, jo, :], ps[:, 0:TT], modc(l, 5, jo), xT[:, jo, :], ALU.mult, ALU.add, [pk, 'modt', 'xT'], ['xT'])
                dump(f'x{l}', xT[:], D, ti, ['xT'])
                chk(51 + l)

            chk(6)
            act(sq[:], xT[:], AF.Square, ['xT'], [f'sq{i}' for i in range(8)])
            ps, pk = bank()
            for kc in range(8):
                mm(ps[:, 0:TT], ones, sq[:, kc, :], kc == 0, kc == 7, ['cbt', f'sq{kc}'], [pk])
            act(rstd[:], ps[:, 0:TT], AF.Ln, [pk], ['rstd'], scale=1.0 / D, bias=NORM_EPS)
            act(rstd[:], rstd[:], AF.Exp, ['rstd'], ['rstd'], scale=-0.5)
            fo, fw = POFF['final_g']
            for kc in range(8):
                dst_, dk_ = (osb, 'osb') if kc < 4 else (ysb, f'ysb{kc % 4}')
                stt(dst_[:, kc % 4, :], xT[:, kc, :], prm[:, fo + kc:fo + kc + 1], rstd[:], ALU.mult, ALU.mult,
                    ['xT', 'prm', 'rstd'], [dk_])
            odv = out_d.rearrange("(k p) t -> p k t", p=128)
            S.dma('sp', odv[:, 0:4, tsl], osb[:], reads=['osb'], writes=['out_d'])
            S.dma('sp', odv[:, 4:8, tsl], ysb[:], reads=[f'ysb{i}' for i in range(4)], writes=['out_d'])
        try:
            _main()
        except _Stop:
            pass
        S.barrier()
        print(f"[kernel] T={T} instructions={S.n_inst} waits={S.n_wait}", flush=True)
    return nc


def core_inputs(inputs, shared, b, T):
    m = dict(shared)
    m['xT'] = np.ascontiguousarray(np.asarray(inputs['x'], np.float32)[b, :T].T)
    P = shared['params'].copy()
    o, w = POFF['c']
    P[:, o:o + w] = _fm(np.asarray(inputs['c'], np.float32)[b])
    m['params'] = P
    return m


def kernel(**inputs):
    T = SEQ
    shared = prep_shared(inputs)
    nc = build_nc(T)
    in_maps = [core_inputs(inputs, shared, c // 2, T) for c in range(8)]
    res = run_bass_kernel_spmd(nc, in_maps, core_ids=list(range(8)))
    out = np.zeros((4, T, D), np.float32)
    H = T // 2
    for c in range(8):
        b, s = c // 2, c % 2
        o = res.results[c]["outT"]
        out[b, s * H:(s + 1) * H] = o[:, s * H:(s + 1) * H].T
    return out
```

```python
import contextlib
import numpy as np
import concourse.bass as bass
import concourse.mybir as mybir
from concourse.bass_utils import run_bass_kernel_spmd

F32 = mybir.dt.float32
BF16 = mybir.dt.bfloat16
AF = mybir.ActivationFunctionType
ALU = mybir.AluOpType

D = 1024
NKC = 8
TT = 256
C = 128
NCH = TT // C
NFF = 22
SEQ = 8192
GN_EPS = 64e-5
NORM_EPS = 1e-6
NFM = 24
RW0 = 9

def _playout():
    off = {}
    cur = 0
    def add(name, w):
        nonlocal cur
        off[name] = (cur, w)
        cur += w
    add('c', 8)
    add('final_g', 8)
    for l in range(2):
        add(f'ada_b{l}', 48)
        add(f'nmg{l}', 8)
        add(f'nfg{l}', 8)
        add(f'mu{l}', 15)
        for nm in ('w0', 'a0', 'v0', 'k_k', 'k_a', 'r_k', 'gn_g', 'gn_b'):
            add(f'{nm}{l}', 4)
        add(f'gng{l}', 1)
        add(f'cw{l}', 132)
        add(f'cb{l}', 44)
    return off, cur

POFF, NPARAM = _playout()


def _fm(vec):
    v = np.asarray(vec, np.float32).reshape(-1)
    n = v.shape[0] // 128
    return v.reshape(n, 128).T


def _wt(cols):
    K, w = cols.shape
    return cols.reshape(K // 128, 128, w).transpose(1, 0, 2)


def prep_shared(inp):
    f = lambda a: np.asarray(a, np.float32)
    w_in = f(inp['w_in']); w_vres = f(inp['w_in_vres'])
    sh = {}
    w_fm = np.zeros((2, NFM, 128, 8, 128), np.float32)
    RB = 1552
    for l in range(2):
        W = w_in[l]
        def put(idx, cols):
            w_fm[l, idx, :, :, :cols.shape[1]] = _wt(cols)
        put(0, W[:, 1536:1552])
        put(1, W[:, 0:128]); put(2, W[:, 128:256]); put(3, W[:, 256:384]); put(4, W[:, 384:512])
        for i in range(4):
            put(5 + i, W[:, 1024 + i * 128:1024 + (i + 1) * 128])
        put(9, W[:, RB + 1536:RB + 1664])
        put(10, W[:, RB + 1664:RB + 1792])
        if l == 1:
            put(11, w_vres[0])
        for ct in range(4):
            put(12 + 3 * ct, W[:, RB + ct * 128:RB + (ct + 1) * 128])
            put(13 + 3 * ct, W[:, RB + 512 + ct * 128:RB + 512 + (ct + 1) * 128])
            put(14 + 3 * ct, W[:, RB + 1024 + ct * 128:RB + 1024 + (ct + 1) * 128])
    sh['w_fm'] = w_fm
    sh['w_v'] = np.stack([_wt(w_in[l][:, 512:1024]) for l in range(2)])
    w_out = f(inp['w_out'])
    sh['w_o'] = np.stack([np.stack([_wt(w_out[l][:, j * 128:(j + 1) * 128]) for j in range(8)]) for l in range(2)])
    fu = f(inp['ffn_up'])
    wu = np.zeros((2, 44, 128, 8, 128), np.float32)
    for l in range(2):
        for j in range(NFF):
            wu[l, 2 * j] = _wt(fu[l][:, j * 128:(j + 1) * 128])
            wu[l, 2 * j + 1] = _wt(fu[l][:, 2816 + j * 128:2816 + (j + 1) * 128])
    sh['w_u'] = wu
    fd = f(inp['ffn_down'])
    sh['w_d'] = np.stack([np.stack([_wt(fd[l][:, j * 128:(j + 1) * 128]) for j in range(8)]) for l in range(2)])
    sh['ada'] = np.stack([_wt(f(inp['ada_w'])[l]) for l in range(2)])
    sh['wa_up'] = np.stack([np.concatenate([f(inp['w_lora_up'])[l], f(inp['a_lora_up'])[l]], 0) for l in range(2)])
    sh['g_up'] = f(inp['g_lora_up'])
    vup = np.zeros((128, 512), np.float32); vup[:32] = f(inp['v_lora_up'])[0]
    sh['v_up'] = vup
    gku = np.zeros((2, 128, 256), np.float32); gku[:, :16] = f(inp['gla_gk_up'])
    sh['gk_up'] = gku
    sh['gkb_bc'] = np.stack([np.broadcast_to(f(inp['gla_gk_b'])[l], (128, 256)) for l in range(2)]).copy()
    sh['w0_bc'] = np.stack([np.broadcast_to(f(inp['w0'])[l], (128, 512)) for l in range(2)]).copy()
    j = np.arange(128)[:, None]; t = np.arange(128)[None, :]
    su = (j < t).astype(np.float32); iu = (j <= t).astype(np.float32); sl = (j > t).astype(np.float32)
    eye = np.eye(128, dtype=np.float32)
    bones = np.zeros((128, 128), np.float32); bones[:64, :64] = 1; bones[64:, 64:] = 1
    cb = np.concatenate([eye, su, iu, sl, np.ones((128, 128), np.float32), bones, bones / 64.0], 1)
    sh['constb'] = cb
    ew = np.float32(np.exp(-0.5))
    sh['constf'] = np.concatenate([iu * ew, su * ew, iu / 16.0], 1).astype(np.float32)
    P = np.zeros((128, NPARAM), np.float32)
    def setp(name, arr):
        o, w = POFF[name]
        assert arr.shape == (128, w), (name, arr.shape, w)
        P[:, o:o + w] = arr
    setp('final_g', _fm(inp['final_g']))
    for l in range(2):
        setp(f'ada_b{l}', _fm(f(inp['ada_b'])[l]))
        setp(f'nmg{l}', _fm(f(inp['norm_mix_g'])[l]))
        setp(f'nfg{l}', _fm(f(inp['norm_ffn_g'])[l]))
        mu = f(inp['rwkv_mu'])[l]
        m = np.zeros((128, 15), np.float32)
        m[:, 0] = mu[1536:1664]; m[:, 1] = mu[1664:1792]
        if l == 1:
            m[:32, 2] = f(inp['rwkv_mu_vres'])[0]
        for ct in range(4):
            m[:, 3 + 3 * ct] = mu[ct * 128:(ct + 1) * 128]
            m[:, 4 + 3 * ct] = mu[512 + ct * 128:512 + (ct + 1) * 128]
            m[:, 5 + 3 * ct] = mu[1024 + ct * 128:1024 + (ct + 1) * 128]
        setp(f'mu{l}', m)
        for nm, key in (('w0', 'w0'), ('a0', 'a0'), ('k_k', 'k_k'), ('k_a', 'k_a'), ('gn_g', 'gn_g'), ('gn_b', 'gn_b')):
            setp(f'{nm}{l}', _fm(f(inp[key])[l]))
        setp(f'r_k{l}', _fm(f(inp['r_k'])[l].reshape(-1)))
        if l == 1:
            setp('v01', _fm(f(inp['v0'])[0]))
        setp(f'gng{l}', f(inp['gla_norm_g'])[l].reshape(128, 1))
        cw = f(inp['ffn_conv_w'])[l]; cbv = f(inp['ffn_conv_b'])[l]
        cwp = np.zeros((128, 44, 3), np.float32); cbp = np.zeros((128, 44), np.float32)
        for jj in range(NFF):
            for half in range(2):
                cols = slice(half * 2816 + jj * 128, half * 2816 + (jj + 1) * 128)
                cwp[:, 2 * jj + half, :] = cw[:, cols].T
                cbp[:, 2 * jj + half] = cbv[cols]
        setp(f'cw{l}', cwp.reshape(128, 132))
        setp(f'cb{l}', cbp)
    sh['params'] = P
    return sh


class Sched:
    def __init__(self, nc, es, ndma=8):
        self.nc = nc
        self.eng = {'pe': nc.tensor, 'act': nc.scalar, 'dve': nc.vector, 'pool': nc.gpsimd, 'sp': nc.sync}
        self.sem = {e: es.enter_context(nc.semaphore('s_' + e)) for e in ('pe', 'act', 'dve', 'pool')}
        self.cnt = {e: 0 for e in self.sem}
        self.dq = {}
        for q in ('sp', 'pool'):
            self.dq[q] = {'sems': [es.enter_context(nc.semaphore(f'd_{q}{i}')) for i in range(ndma)],
                          'cnt': [0] * ndma, 'nxt': 0}
        self.seen = {e: {} for e in self.eng}
        self.lastw = {}
        self.rd = {}
        self.n_inst = 0
        self.n_wait = 0
        self.pend = {}

    def _semobj(self, sk):
        if sk[0] == 'e':
            return self.sem[sk[1]]
        return self.dq[sk[1]]['sems'][sk[2]]

    def _wait(self, eng, tk):
        sk, val, src = tk
        if self.seen[eng].get(sk, 0) >= val:
            return
        self.eng[eng].wait_ge(self._semobj(sk), val)
        self.seen[eng][sk] = val
        self.n_wait += 1

    def _deps(self, eng, reads, writes):
        deps = []
        for k in reads:
            if k in self.lastw:
                deps.append(self.lastw[k])
        for k in writes:
            if k in self.lastw:
                deps.append(self.lastw[k])
            for tk in self.rd.get(k, {}).values():
                deps.append(tk)
        for tk in deps:
            if eng == 'pe' and tk[2] == 'pe':
                continue
            self._wait(eng, tk)

    def _record(self, tk, reads, writes):
        for k in writes:
            self.lastw[k] = tk
            self.rd[k] = {}
        for k in reads:
            self.rd.setdefault(k, {})[tk[0]] = tk

    def emit(self, eng, fn, reads=(), writes=(), signal=True):
        self._deps(eng, reads, writes)
        inst = fn()
        self.n_inst += 1
        pr, pw = self.pend.setdefault(eng, ([], []))
        if not signal:
            pr.extend(reads); pw.extend(writes)
            return
        self.cnt[eng] += 1
        inst.then_inc(self.sem[eng], 1)
        self._record((('e', eng), self.cnt[eng], eng), list(reads) + pr, list(writes) + pw)
        self.pend[eng] = ([], [])

    def dma(self, q, out, in_, reads=(), writes=()):
        Q = self.dq[q]
        i = Q['nxt']; Q['nxt'] = (i + 1) % len(Q['sems'])
        sk = ('d', q, i)
        if Q['cnt'][i] > 0:
            self._wait(q, (sk, Q['cnt'][i], 'dma'))
        self._deps(q, reads, writes)
        inst = self.eng[q].dma_start(out=out, in_=in_)
        Q['cnt'][i] += 16
        inst.then_inc(Q['sems'][i], 16)
        self._record((sk, Q['cnt'][i], 'dma'), reads, writes)
        self.n_inst += 1

    def barrier(self):
        for e in self.eng:
            for s in self.sem:
                if self.cnt[s] > 0:
                    self._wait(e, (('e', s), self.cnt[s], s))
            for q, Q in self.dq.items():
                for i, cval in enumerate(Q['cnt']):
                    if cval > 0:
                        self._wait(e, (('d', q, i), cval, 'dma'))


class _Stop(Exception):
    pass


def build_nc(T, debug=False, stop=None):
    NT = T // TT
    nc = bass.Bass("TRN2", target_bir_lowering=False)
    dr = lambda name, shape, dt=F32, kind="ExternalInput": nc.dram_tensor(name, list(shape), dt, kind=kind).ap()
    xT_d = dr("xT", [D, T])
    params_d = dr("params", [128, NPARAM])
    ada_d = dr("ada", [2, 128, 8, 6144])
    wfm_d = dr("w_fm", [2, NFM, 128, 8, 128])
    wv_d = dr("w_v", [2, 128, 8, 512])
    wo_d = dr("w_o", [2, 8, 128, 8, 128])
    wu_d = dr("w_u", [2, 44, 128, 8, 128])
    wd_d = dr("w_d", [2, 8, 128, NFF, 128])
    waup_d = dr("wa_up", [2, 128, 512])
    gup_d = dr("g_up", [2, 128, 512])
    vup_d = dr("v_up", [128, 512])
    gkup_d = dr("gk_up", [2, 128, 256])
    gkb_d = dr("gkb_bc", [2, 128, 256])
    w0bc_d = dr("w0_bc", [2, 128, 512])
    cb_d = dr("constb", [128, 7 * 128])
    cf_d = dr("constf", [128, 3 * 128])
    out_d = dr("outT", [D, T], F32, "ExternalOutput")
    s_fm = dr("s_fm", [2, NFM, 128, 8, 128], BF16, "Internal")
    s_v = dr("s_v", [2, 128, 8, 512], BF16, "Internal")
    s_o = dr("s_o", [2, 8, 128, 8, 128], BF16, "Internal")
    s_u = dr("s_u", [2, 44, 128, 8, 128], BF16, "Internal")
    s_d = dr("s_d", [2, 8, 128, NFF, 128], BF16, "Internal")
    dbg_d = {}
    if debug:
        for nm in ('d_rf', 'd_kf', 'd_vf', 'd_emL', 'd_kmod', 'd_kkn', 'd_asig', 'd_eL', 'd_eLp', 'd_ktr', 'd_btr'):
            dbg_d[nm] = dr("dbg_" + nm, [128, T], F32, "ExternalOutput")
        for nm in ('h0', 'oo0', 'xmix0', 'x0', 'h1', 'oo1', 'xmix1', 'x1', 'gla_o0', 'rwkv_y0'):
            rows = 512 if nm.startswith(('gla_o', 'rwkv_y')) else D
            dbg_d[nm] = dr("dbg_" + nm, [rows, T], F32, "ExternalOutput")

    with contextlib.ExitStack() as es:
        E = es.enter_context
        S = Sched(nc, es)
        sb = lambda name, shape, dt=F32: E(nc.sbuf_tensor("sb_" + name, list(shape), dt))

        prm = sb("prm", [128, NPARAM])
        cbt = sb("cbt", [128, 7 * 128], BF16)
        cft = sb("cft", [128, 3 * 128])
        waup = sb("waup", [128, 2, 512], BF16)
        gup = sb("gup", [128, 2, 512], BF16)
        vup = sb("vup", [128, 512], BF16)
        gkup = sb("gkup", [128, 2, 256], BF16)
        gkb = sb("gkb", [128, 2, 256])
        w0bc = sb("w0bc", [128, 2, 512])
        modt = sb("modt", [128, 2, 48])
        drv = sb("drv", [128, 2, 40])
        sc = sb("sc", [128, 8])
        ident = cbt[:, 0:128]; m_su = cbt[:, 128:256]; m_iu = cbt[:, 256:384]; m_sl = cbt[:, 384:512]
        m_suiu = cbt[:, 128:384]
        ones = cbt[:, 512:640]; bones = cbt[:, 640:768]; bones64 = cbt[:, 768:896]
        triW = cft[:, 0:256]; tri16 = cft[:, 256:384]

        def P(name, l=None):
            o, w = POFF[name if l is None else f'{name}{l}']
            return prm[:, o:o + w]

        def Pc(name, l, i):
            o, w = POFF[f'{name}{l}']
            return prm[:, o + i:o + i + 1]

        S.dma('sp', prm[:], params_d[:, :], writes=['prm'])
        S.dma('pool', cbt[:], cb_d[:, :], writes=['cbt'])
        S.dma('sp', cft[:], cf_d[:, :], writes=['cft'])
        for l in range(2):
            S.dma('pool', waup[:, l, :], waup_d[l], writes=['waup'])
            S.dma('pool', gup[:, l, :], gup_d[l], writes=['gup'])
            S.dma('pool', gkup[:, l, :], gkup_d[l], writes=['gkup'])
            S.dma('sp', gkb[:, l, :], gkb_d[l], writes=['gkb'])
            S.dma('sp', w0bc[:, l, :], w0bc_d[l], writes=['w0bc'])
        S.dma('pool', vup[:], vup_d[:, :], writes=['vup'])

        for l in range(2):
            for j in range(NFM):
                S.dma('pool', s_fm[l, j], wfm_d[l, j], writes=[f's_fm{l}'])
            for kc in range(8):
                S.dma('pool', s_v[l, :, kc, :], wv_d[l, :, kc, :], writes=[f's_v{l}'])
            for j in range(8):
                S.dma('pool', s_o[l, j], wo_d[l, j], writes=[f's_o{l}'])
            for j in range(44):
                S.dma('pool', s_u[l, j], wu_d[l, j], writes=[f's_u{l}'])
            for j in range(8):
                for h2 in range(2):
                    S.dma('pool', s_d[l, j, :, h2 * 11:(h2 + 1) * 11, :], wd_d[l, j, :, h2 * 11:(h2 + 1) * 11, :],
                          writes=[f's_d{l}'])

        NB = 6
        banks = [E(nc.psum_tensor(f"pb{i}", [128, 512], F32)) for i in range(NB)]
        ptbs = [E(nc.psum_tensor(f"ptb{i}", [128, 1024], BF16)) for i in range(2)]
        st = {'b': 0, 't': 0}

        def bank():
            i = st['b']; st['b'] = (i + 1) % NB
            return banks[i], f'pb{i}'

        def tbank():
            i = st['t']; st['t'] = (i + 1) % 2
            return ptbs[i][:, 0:512], f'pt{i}'

        def mm(out, lhsT, rhs, start, stop, r, w):
            S.emit('pe', lambda: nc.tensor.matmul(out, lhsT, rhs, start=start, stop=stop), r, w, signal=bool(stop))

        def tr(out, in_, r, w):
            S.emit('pe', lambda: nc.tensor.transpose(out, in_, ident), list(r) + ['cbt'], w)

        def act(out, in_, func, r, w, scale=1.0, bias=0.0):
            S.emit('act', lambda: nc.scalar.activation(out=out, in_=in_, func=func, bias=bias, scale=scale), r, w)

        def tt(eng, out, in0, in1, op, r, w):
            e = nc.vector if eng == 'dve' else nc.gpsimd
            S.emit(eng, lambda: e.tensor_tensor(out=out, in0=in0, in1=in1, op=op), r, w)

        def ts(eng, out, in0, s1, op0, r, w, s2=None, op1=None):
            e = nc.vector if eng == 'dve' else nc.gpsimd
            if op1 is None:
                S.emit(eng, lambda: e.tensor_scalar(out=out, in0=in0, scalar1=s1, scalar2=None, op0=op0), r, w)
            else:
                S.emit(eng, lambda: e.tensor_scalar(out=out, in0=in0, scalar1=s1, scalar2=s2, op0=op0, op1=op1), r, w)

        def stt(out, in0, scalar, in1, op0, op1, r, w):
            S.emit('dve', lambda: nc.vector.scalar_tensor_tensor(out=out, in0=in0, scalar=scalar, in1=in1,
                                                                  op0=op0, op1=op1), r, w)

        def cp(eng, out, in_, r, w):
            if eng == 'act':
                S.emit('act', lambda: nc.scalar.activation(out=out, in_=in_, func=AF.Copy), r, w)
            else:
                e = nc.vector if eng == 'dve' else nc.gpsimd
                S.emit(eng, lambda: e.tensor_copy(out=out, in_=in_), r, w)

        def recip(out, in_, r, w):
            S.emit('dve', lambda: nc.vector.reciprocal(out=out, in_=in_), r, w)

        def memz(eng, ap, w):
            e = nc.vector if eng == 'dve' else nc.gpsimd
            S.emit(eng, lambda: e.memset(ap, 0.0), [], w)

        act(sc[:], P('c'), AF.Silu, ['prm'], ['sc'])
        with contextlib.ExitStack() as es2:
            adat = es2.enter_context(nc.sbuf_tensor("adat", [128, 8, 1024], F32))
            for l in range(2):
                for g6 in range(6):
                    S.dma('sp', adat[:], ada_d[l, :, :, g6 * 1024:(g6 + 1) * 1024], writes=['adat'])
                    ps, pk = bank()
                    for j in range(8):
                        for kc in range(8):
                            mm(ps[:, j:j + 1], adat[:, kc, j * 128:(j + 1) * 128], sc[:, kc:kc + 1],
                               kc == 0, kc == 7, ['adat', 'sc'], [pk])
                    tt('dve', modt[:, l, g6 * 8:(g6 + 1) * 8], ps[:, 0:8], P('ada_b', l)[:, g6 * 8:(g6 + 1) * 8],
                       ALU.add, [pk, 'prm'], ['modt'])
            S.barrier()
        for l in range(2):
            ts('dve', drv[:, l, 0:8], modt[:, l, 8:16], 1.0, ALU.add, ['modt'], ['drv'])
            tt('dve', drv[:, l, 0:8], drv[:, l, 0:8], P('nmg', l), ALU.mult, ['drv', 'prm'], ['drv'])
            ts('dve', drv[:, l, 8:16], modt[:, l, 32:40], 1.0, ALU.add, ['modt'], ['drv'])
            tt('dve', drv[:, l, 8:16], drv[:, l, 8:16], P('nfg', l), ALU.mult, ['drv', 'prm'], ['drv'])
            ts('dve', drv[:, l, 16:20], P('k_a', l), -1.0, ALU.mult, ['prm'], ['drv'], 1.0, ALU.add)
            ts('dve', drv[:, l, 20:35], P('mu', l), -1.0, ALU.mult, ['prm'], ['drv'], 1.0, ALU.add)
        A_m = lambda l, kc: drv[:, l, kc:kc + 1]
        A_f = lambda l, kc: drv[:, l, 8 + kc:9 + kc]
        omk = lambda l, ct: drv[:, l, 16 + ct:17 + ct]
        modc = lambda l, m, kc: modt[:, l, m * 8 + kc:m * 8 + kc + 1]

        xT = sb("xT", [128, 8, TT])
        hT = sb("hT", [128, 8, TT], BF16)
        sq = sb("sq", [128, 8, TT], BF16)
        rstd = sb("rstd", [128, TT])
        ftmp = [sb(f"ftmp{i}", [128, TT]) for i in range(3)]
        NA = 8
        ringA = [sb(f"ra{i}", [128, 8, 128], BF16) for i in range(NA)]
        wvb = sb("wvb", [128, 8, 512], BF16)
        ringD = [sb(f"rd{i}", [128, NFF, 128], BF16) for i in range(2)]
        rs = {'a': 0, 'd': 0, 'f': 0}

        def loadA(src, key):
            i = rs['a']; rs['a'] = (i + 1) % NA
            S.dma('sp', ringA[i][:], src, reads=[key], writes=[f'ra{i}'])
            return ringA[i], f'ra{i}'

        def loadD(src, key):
            i = rs['d']; rs['d'] = (i + 1) % 2
            S.dma('sp', ringD[i][:], src, reads=[key], writes=[f'rd{i}'])
            return ringD[i], f'rd{i}'

        def ft():
            i = rs['f']; rs['f'] = (i + 1) % 3
            return ftmp[i], f'ftmp{i}'

        ob = sb("ob", [128, 8, TT], BF16)
        qf = sb("qf", [128, 2, TT]); kfg = sb("kfg", [128, 2, TT])
        sg = sb("sg", [128, 4, TT], BF16)
        zb = sb("zb", [128, TT], BF16)
        vtok = sb("vtok", [128, NCH, 512], BF16)
        sptok = sb("sptok", [128, NCH, 256])
        eG = sb("eG", [128, 2, TT]); emG = sb("emG", [128, TT])
        qt = sb("qt", [128, 2, TT], BF16); ktg = sb("ktg", [128, 2, TT], BF16)
        ktgT = sb("ktgT", [128, NCH, 256], BF16)
        ATb = [sb(f"ATb{i}", [128, 128], BF16) for i in range(4)]
        osb = sb("osb", [128, 4, TT])
        Sg = sb("Sg", [128, 2, 2, 128]); Sgb = sb("Sgb", [128, 2, 2, 128], BF16)
        ptmp = [sb(f"ptmp{i}", [128, TT + 2]) for i in range(2)]
        dtmp = [sb(f"dtmp{i}", [128, TT]) for i in range(2)]
        halo = sb("halo", [128, 2, 16])
        wab = sb("wab", [128, TT], BF16)
        sgl = sb("sgl", [128, TT], BF16)
        vl = sb("vl", [128, TT], BF16)
        rkv = [[sb(f"rkv{i}_{j}", [128, TT]) for j in range(3)] for i in range(2)]
        vfirst = sb("vfirst", [128, 4, TT])
        nlw = sb("nlw", [128, NCH, 512])
        eL = sb("eL", [128, 4, TT])
        emL = [sb(f"emL{i}", [128, TT]) for i in range(2)]
        eLp = [sb(f"eLp{i}", [128, TT]) for i in range(2)]
        asig = [sb(f"asig{i}", [128, TT]) for i in range(2)]
        kkn = [sb(f"kkn{i}", [128, TT]) for i in range(2)]
        kmod = [sb(f"kmod{i}", [128, TT]) for i in range(2)]
        kk2 = sb("kk2", [128, TT], BF16)
        rkb = sb("rkb", [128, TT], BF16)
        vb = sb("vb", [128, TT], BF16)
        big = sb("big", [128, 6144], BF16)
        tokT = big[:, 0:3072].rearrange("p (c t x) -> p c t x", c=NCH, t=4)
        ar = big[:, 3072:5120].rearrange("p (t x) -> p t x", t=4)
        ktr = big[:, 5120:6144].rearrange("p (t x) -> p t x", t=4)
        btr = sb("btr", [128, 4, TT], BF16)
        BIGK = ['tokT', 'ar', 'ktr']
        gate = sb("gate", [128, 4, TT], BF16)
        ysb = sb("ysb", [128, 4, TT])
        bonus = sb("bonus", [128, 4, TT])
        W0 = [sb(f"W0_{i}", [128, 384], BF16) for i in range(8)]
        Wpp = [[sb(f"Wpp{i}_{j}", [128, 384], BF16) for j in range(2)] for i in range(8)]
        MN2 = [[sb(f"MN{q}_{i}", [128, 256], BF16) for i in range(8)] for q in range(2)]
        Nbr2 = [[sb(f"Nbr{q}_{i}", [128, 128], BF16) for i in range(8)] for q in range(2)]
        Tfin2 = [[sb(f"Tfin{q}_{i}", [128, 128], BF16) for i in range(8)] for q in range(2)]
        Rn2 = [[sb(f"Rn{q}_{i}", [128, 128], BF16) for i in range(8)] for q in range(2)]
        XTb = [sb(f"XTb{i}", [128, 128], BF16) for i in range(4)]
        UTb = [sb(f"UTb{i}", [128, 128], BF16) for i in range(4)]
        Sr = sb("Sr", [128, 2, 4, 64]); Srb = sb("Srb", [128, 2, 4, 64], BF16)
        mbuf = big[:, 0:NFF * TT].rearrange("p (j x) -> p j x", j=NFF)
        acc = [emL[0], emL[1], eLp[0], eLp[1]]
        acck = ['emL0', 'emL1', 'eLp0', 'eLp1']
        sgt = [asig[0], asig[1]]
        sgtk = ['asig0', 'asig1']
        chalo = sb("chalo", [128, 2, 2, 44, 2])

        memz('dve', Sg[:], ['Sg']); memz('dve', Sgb[:], ['Sgb'])
        memz('dve', Sr[:], ['Sr']); memz('dve', Srb[:], ['Srb'])
        memz('pool', halo[:], ['halo']); memz('pool', chalo[:], ['chalo'])
        memz('pool', vl[:], ['vl'])
        for h in range(8):
            cp('pool', W0[h][:, 256:384], ident, ['cbt'], [f'W0_{h}'])

        def dump(nm, ap, rows, ti, r, q='sp'):
            if debug and nm in dbg_d:
                dst = dbg_d[nm].rearrange("(k p) t -> p k t", p=128)[:, :, ti * TT:(ti + 1) * TT]
                S.dma(q, dst, ap, reads=r, writes=['dbg_' + nm])

        def rmsnorm_mod(l, Afn, shift_m):
            act(sq[:, 0:4, :], xT[:, 0:4, :], AF.Square, ['xT'], [f'sq{i}' for i in range(4)])
            act(sq[:, 4:8, :], xT[:, 4:8, :], AF.Square, ['xT'], [f'sq{i}' for i in range(4, 8)])
            ps, pk = bank()
            for kc in range(8):
                mm(ps[:, 0:TT], ones, sq[:, kc, :], kc == 0, kc == 7, ['cbt', f'sq{kc}'], [pk])
            act(rstd[:], ps[:, 0:TT], AF.Ln, [pk], ['rstd'], scale=1.0 / D, bias=NORM_EPS)
            act(rstd[:], rstd[:], AF.Exp, ['rstd'], ['rstd'], scale=-0.5)
            for kc in range(8):
                f_, fk = ft()
                stt(f_[:], xT[:, kc, :], Afn(l, kc), rstd[:], ALU.mult, ALU.mult, ['xT', 'drv', 'rstd'], [fk])
                act(hT[:, kc, :], f_[:], AF.Identity, [fk, 'modt'], ['hT'], bias=modc(l, shift_m, kc))

        def proj_fm(l, idx, m):
            slot, sk = loadA(s_fm[l, idx], f's_fm{l}')
            ps, pk = bank()
            for kc in range(8):
                mm(ps[0:m, 0:TT], slot[:, kc, 0:m], hT[:, kc, :], kc == 0, kc == 7, [sk, 'hT'], [pk])
            return ps, pk

        def lerp(l, idx, ps, pk, m, out_ap, okey, pi):
            mi = idx - RW0
            pt_, ptk = ptmp[pi], f'ptmp{pi}'
            dt_, dtk = dtmp[pi], f'dtmp{pi}'
            act(dt_[0:m, :], ps[0:m, 0:TT], AF.Identity, [pk, 'drv'], [dtk], scale=drv[0:m, l, 20 + mi:21 + mi])
            cp('act', pt_[0:m, 1:TT + 1], ps[0:m, 0:TT], [pk], [ptk])
            cp('pool', pt_[0:m, 0:1], halo[0:m, l, mi:mi + 1], ['halo'], [ptk])
            cp('pool', halo[0:m, l, mi:mi + 1], pt_[0:m, TT:TT + 1], [ptk], ['halo'])
            o, w = POFF[f'mu{l}']
            stt(out_ap, pt_[0:m, 0:TT], prm[0:m, o + mi:o + mi + 1], dt_[0:m, :], ALU.mult, ALU.add,
                [dtk, ptk, 'prm'], [okey])

        def inter(*gens):
            gens = [g for g in gens if g is not None]
            while gens:
                for g in list(gens):
                    try:
                        next(g)
                    except StopIteration:
                        gens.remove(g)
                yield

        def chk(n):
            if stop is not None and stop == n:
                raise _Stop()

        def _main():
          for ti in range(NT if stop != 0 else 0):
            tsl = slice(ti * TT, (ti + 1) * TT)
            S.dma('sp', xT[:], xT_d.rearrange("(k p) t -> p k t", p=128)[:, :, tsl], writes=['xT'])
            for l in range(2):
                rmsnorm_mod(l, A_m, 0)
                dump(f'h{l}', hT[:], D, ti, ['hT'], 'pool')
                def gla_thread():
                    ps, pk = proj_fm(l, 0, 16)
                    cp('act', zb[0:16, :], ps[0:16, 0:TT], [pk], ['zb'])
                    yield
                    for i in range(2):
                        ps, pk = proj_fm(l, 1 + i, 128)
                        act(qf[:, i, :], ps[:, 0:TT], AF.Copy, [pk], ['qf'], scale=0.125)
                        yield
                    for i in range(2):
                        ps, pk = proj_fm(l, 3 + i, 128)
                        cp('act', kfg[:, i, :], ps[:, 0:TT], [pk], ['kfg'])
                        yield
                    for i in range(4):
                        ps, pk = proj_fm(l, 5 + i, 128)
                        act(sg[:, i, :], ps[:, 0:TT], AF.Silu, [pk], ['sg'])
                        yield
                    chk(1)
                    S.dma('sp', wvb[:], s_v[l], reads=[f's_v{l}'], writes=['wvb'])
                    for c in range(NCH):
                        ps, pk = bank()
                        for kc in range(8):
                            mm(ps[:, 0:512], hT[:, kc, c * 128:(c + 1) * 128], wvb[:, kc, :], kc == 0, kc == 7,
                               ['hT', 'wvb'], [pk])
                        cp('dve', vtok[:, c, :], ps[:, 0:512], [pk], ['vtok'])
                        yield
                    for c in range(NCH):
                        ps, pk = bank()
                        mm(ps[:, 0:256], zb[0:16, c * 128:(c + 1) * 128], gkup[0:16, l, :], True, True, ['zb', 'gkup'], [pk])
                        f_, fk = ft()
                        tt('dve', f_[:, 0:256], ps[:, 0:256], gkb[:, l, :], ALU.add, [pk, 'gkb'], [fk])
                        act(f_[:, 0:256], f_[:, 0:256], AF.Exp, [fk], [fk], scale=-1.0)
                        act(sptok[:, c, :], f_[:, 0:256], AF.Ln, [fk], ['sptok'], bias=1.0)
                        yield
                    for c2 in range(2):
                        ps, pk = bank()
                        for c in range(NCH):
                            mm(ps[:, c * 128:(c + 1) * 128], sptok[:, c, c2 * 128:(c2 + 1) * 128], tri16, True, True,
                               ['sptok', 'cft'], [pk])
                        act(eG[:, c2, :], ps[:, 0:TT], AF.Exp, [pk], ['eG'], scale=-1.0)
                        act(emG[:], ps[:, 0:TT], AF.Exp, [pk], ['emG'])
                        tt('dve', qt[:, c2, :], qf[:, c2, :], eG[:, c2, :], ALU.mult, ['qf', 'eG'], ['qt'])
                        tt('pool', ktg[:, c2, :], kfg[:, c2, :], emG[:], ALU.mult, ['kfg', 'emG'], ['ktg'])
                        for c in range(NCH):
                            tp, tk_ = tbank()
                            tr(tp[:, 0:128], ktg[:, c2, c * 128:(c + 1) * 128], ['ktg'], [tk_])
                            cp('act', ktgT[:, c, c2 * 128:(c2 + 1) * 128], tp[:, 0:128], [tk_], ['ktgT'])
                            yield
                    for c in range(NCH):
                        csl = slice(c * 128, (c + 1) * 128)
                        for h in range(4):
                            c2 = h // 2; po = (h % 2) * 64
                            ps, pk = bank()
                            mm(ps[:, 0:128], ktg[po:po + 64, c2, csl], qt[po:po + 64, c2, csl], True, True, ['ktg', 'qt'], [pk])
                            tt('dve', ATb[h][:], ps[:, 0:128], m_iu, ALU.mult, [pk, 'cbt'], [f'ATb{h}'])
                            ps2, pk2 = bank()
                            mm(ps2[:, 0:128], vtok[:, c, h * 128:(h + 1) * 128], ATb[h][:], True, False, ['vtok', f'ATb{h}'], [pk2])
                            mm(ps2[:, 0:128], Sgb[po:po + 64, l, c2, :], qt[po:po + 64, c2, csl], False, True, ['Sgb', 'qt'], [pk2])
                            cp('act', osb[:, h, csl], ps2[:, 0:128], [pk2], ['osb'])
                            yield
                        for c2 in range(2):
                            ps, pk = bank()
                            for hh in range(2):
                                h = c2 * 2 + hh; po = hh * 64
                                mm(ps[po:po + 64, 0:128], ktgT[:, c, c2 * 128 + po:c2 * 128 + po + 64],
                                   vtok[:, c, h * 128:(h + 1) * 128], True, True, ['ktgT', 'vtok'], [pk])
                            f_, fk = ft()
                            tt('dve', f_[:, 0:128], ps[:, 0:128], Sg[:, l, c2, :], ALU.add, [pk, 'Sg'], [fk])
                            ts('dve', Sg[:, l, c2, :], f_[:, 0:128], eG[:, c2, c * 128 + 127:c * 128 + 128], ALU.mult,
                               [fk, 'eG'], ['Sg'])
                            cp('pool', Sgb[:, l, c2, :], Sg[:, l, c2, :], ['Sg'], ['Sgb'])
                            yield
                    if debug:
                        dst = dbg_d.get(f'gla_o{l}')
                        if dst is not None:
                            S.dma('sp', dst.rearrange("(k p) t -> p k t", p=128)[:, :, tsl], osb[:], reads=['osb'],
                                  writes=['dbg_g'])

                def rwkv_thread():
                    chk(2)
                    ps, pk = proj_fm(l, 9, 128)
                    lerp(l, 9, ps, pk, 128, ftmp[0][:], 'ftmp0', 0)
                    act(wab[0:64, :], ftmp[0][0:64, :], AF.Tanh, ['ftmp0'], ['wab'])
                    cp('pool', wab[64:128, :], ftmp[0][64:128, :], ['ftmp0'], ['wab'])
                    yield
                    ps, pk = proj_fm(l, 10, 128)
                    lerp(l, 10, ps, pk, 128, ftmp[1][:], 'ftmp1', 1)
                    act(sgl[:], ftmp[1][:], AF.Sigmoid, ['ftmp1'], ['sgl'])
                    yield
                    if l == 1:
                        ps, pk = proj_fm(l, 11, 32)
                        lerp(l, 11, ps, pk, 32, ftmp[2][0:32, :], 'ftmp2', 0)
                        cp('pool', vl[0:32, :], ftmp[2][0:32, :], ['ftmp2'], ['vl'])
                        yield
                    chk(21)
                    for c in range(NCH):
                        ps, pk = bank()
                        mm(ps[:, 0:512], wab[0:64, c * 128:(c + 1) * 128], waup[0:64, l, :], True, True, ['wab', 'waup'], [pk])
                        tt('dve', nlw[:, c, :], ps[:, 0:512], w0bc[:, l, :], ALU.add, [pk, 'w0bc'], ['nlw'])
                        act(nlw[:, c, :], nlw[:, c, :], AF.Sigmoid, ['nlw'], ['nlw'])
                        yield
                    chk(22)
                    for ct in range(4):
                        pi = ct % 2
                        cts = slice(ct * 128, (ct + 1) * 128)
                        rf, kf, vf = rkv[pi]
                        rk_, kk_, vk_ = f'rkv{pi}_0', f'rkv{pi}_1', f'rkv{pi}_2'
                        ps, pk = proj_fm(l, 12 + 3 * ct, 128)
                        lerp(l, 12 + 3 * ct, ps, pk, 128, rf[:], rk_, 0)
                        yield
                        ps, pk = proj_fm(l, 13 + 3 * ct, 128)
                        lerp(l, 13 + 3 * ct, ps, pk, 128, kf[:], kk_, 1)
                        yield
                        ps, pk = proj_fm(l, 14 + 3 * ct, 128)
                        lerp(l, 14 + 3 * ct, ps, pk, 128, vf[:], vk_, 0)
                        yield
                        ps, pk = bank()
                        for c in range(NCH):
                            mm(ps[:, c * 256:(c + 1) * 256], nlw[:, c, cts], triW, True, True, ['nlw', 'cft'], [pk])
                        psv = ps[:, 0:NCH * 256].rearrange("p (c x) -> p c x", x=256)
                        v3 = lambda ap: ap.rearrange("p (c x) -> p c x", x=128)
                        act(v3(eL[:, ct, :]), psv[:, :, 0:128], AF.Exp, [pk], ['eL'], scale=-1.0)
                        act(v3(emL[pi][:]), psv[:, :, 0:128], AF.Exp, [pk], [f'emL{pi}'])
                        act(v3(eLp[pi][:]), psv[:, :, 128:256], AF.Exp, [pk], [f'eLp{pi}'], scale=-1.0)
                        yield
                        chk(23)
                        ps, pk = bank()
                        mm(ps[:, 0:TT], waup[64:128, l, cts], wab[64:128, :], True, True, ['waup', 'wab'], [pk])
                        act(asig[pi][:], ps[:, 0:TT], AF.Sigmoid, [pk, 'prm'], [f'asig{pi}'], bias=Pc('a0', l, ct))
                        yield
                        ps, pk = bank()
                        mm(ps[:, 0:TT], gup[:, l, cts], sgl[:], True, True, ['gup', 'sgl'], [pk])
                        cp('act', gate[:, ct, :], ps[:, 0:TT], [pk], ['gate'])
                        yield
                        if l == 1:
                            ps, pk = bank()
                            mm(ps[:, 0:TT], vup[0:32, cts], vl[0:32, :], True, True, ['vup', 'vl'], [pk])
                            f_, fk = ft()
                            act(f_[:], ps[:, 0:TT], AF.Sigmoid, [pk, 'prm'], [fk], bias=Pc('v0', 1, ct))
                            f2, fk2 = ft()
                            tt('pool', f2[:], vfirst[:, ct, :], vf[:], ALU.subtract, ['vfirst', vk_], [fk2])
                            tt('pool', f2[:], f2[:], f_[:], ALU.mult, [fk2, fk], [fk2])
                            tt('pool', vf[:], vf[:], f2[:], ALU.add, [vk_, fk2], [vk_])
                            yield
                        else:
                            cp('pool', vfirst[:, ct, :], vf[:], [vk_], ['vfirst'])
                            yield
                        chk(24)
                        act(kk2[:], kf[:], AF.Square, [kk_, 'prm'], ['kk2'], scale=Pc('k_k', l, ct))
                        ps, pk = bank()
                        mm(ps[:, 0:TT], bones, kk2[:], True, True, ['cbt', 'kk2'], [pk])
                        f_, fk = ft()
                        ts('dve', f_[:], ps[:, 0:TT], 1e-24, ALU.max, [pk], [fk])
                        act(f_[:], f_[:], AF.Ln, [fk], [fk])
                        act(f_[:], f_[:], AF.Exp, [fk], [fk], scale=-0.5)
                        stt(kkn[pi][:], kf[:], Pc('k_k', l, ct), f_[:], ALU.mult, ALU.mult, [kk_, 'prm', fk], [f'kkn{pi}'])
                        yield
                        f2, fk2 = ft()
                        ts('dve', f2[:], asig[pi][:], Pc('k_a', l, ct), ALU.mult, [f'asig{pi}', 'prm', 'drv'], [fk2],
                           omk(l, ct), ALU.add)
                        tt('pool', kmod[pi][:], kf[:], f2[:], ALU.mult, [kk_, fk2], [f'kmod{pi}'])
                        tt('dve', ktr[:, ct, :], kmod[pi][:], emL[pi][:], ALU.mult, [f'kmod{pi}', f'emL{pi}'], ['ktr'])
                        yield
                        f3, fk3 = ft()
                        tt('pool', f3[:], kkn[pi][:], asig[pi][:], ALU.mult, [f'kkn{pi}', f'asig{pi}'], [fk3])
                        tt('dve', btr[:, ct, :], f3[:], emL[pi][:], ALU.mult, [fk3, f'emL{pi}'], ['btr'])
                        yield
                        arv = ar[:, ct, :].rearrange("p (c x) -> p c x", x=256)
                        stt(arv[:, :, 0:128], v3(kkn[pi][:]), -1.0, v3(eLp[pi][:]), ALU.mult, ALU.mult,
                            [f'kkn{pi}', f'eLp{pi}'], ['ar'])
                        tt('dve', arv[:, :, 128:256], v3(rf[:]), v3(eL[:, ct, :]), ALU.mult, [rk_, 'eL'], ['ar'])
                        yield
                        chk(25)
                        stt(rkb[:], rf[:], Pc('r_k', l, ct), kmod[pi][:], ALU.mult, ALU.mult, [rk_, 'prm', f'kmod{pi}'], ['rkb'])
                        ps, pk = bank()
                        mm(ps[:, 0:TT], bones, rkb[:], True, True, ['cbt', 'rkb'], [pk])
                        tt('dve', bonus[:, ct, :], ps[:, 0:TT], vf[:], ALU.mult, [pk, vk_], ['bonus'])
                        yield
                        if debug and ct == 0:
                            for nm_, ap_, k_ in (('d_rf', rf[:], rk_), ('d_kf', kf[:], kk_), ('d_vf', vf[:], vk_), ('d_emL', emL[pi][:], f'emL{pi}'),
                                                 ('d_kmod', kmod[pi][:], f'kmod{pi}'), ('d_kkn', kkn[pi][:], f'kkn{pi}'), ('d_asig', asig[pi][:], f'asig{pi}'),
                                                 ('d_eL', eL[:, ct, :], 'eL'), ('d_eLp', eLp[pi][:], f'eLp{pi}'), ('d_ktr', ktr[:, ct, :], 'ktr'), ('d_btr', btr[:, ct, :], 'btr')):
                                S.dma('pool', dbg_d[nm_][:, tsl], ap_, reads=[k_], writes=['dbg_' + nm_])
                        chk(251)
                        cp('pool', vb[:], vf[:], [vk_], ['vb'])
                        chk(252)
                        for c in range(NCH):
                            csl = slice(c * 128, (c + 1) * 128)
                            for ii, (src_, sk_) in enumerate(((vb[:, csl], 'vb'), (ktr[:, ct, csl], 'ktr'), (btr[:, ct, csl], 'btr'))):
                                tp, tk_ = tbank()
                                tr(tp[:, 0:128], src_, [sk_], [tk_])
                                cp('act', tokT[:, c, ct, ii * 128:(ii + 1) * 128], tp[:, 0:128], [tk_], ['tokT'])
                                yield
                        chk(26 + ct)

                    chk(3)
                    def stage_A(c):
                        csl = slice(c * 128, (c + 1) * 128)
                        q = c % 2
                        MN, Nbr, Tfin, Rn = MN2[q], Nbr2[q], Tfin2[q], Rn2[q]
                        for ct in range(4):
                            for hh in range(2):
                                h = ct * 2 + hh; po = hh * 64
                                kt_h = ktr[po:po + 64, ct, csl]; bt_h = btr[po:po + 64, ct, csl]
                                ar_h = ar[po:po + 64, ct, c * 256:(c + 1) * 256]
                                at_h = ar[po:po + 64, ct, c * 256:c * 256 + 128]
                                ps, pk = bank()
                                mm(ps[:, 0:256], kt_h, ar_h, True, True, ['ktr', 'ar'], [pk])
                                mm(ps[:, 256:512], bt_h, ar_h, True, True, ['btr', 'ar'], [pk])
                                ps2, pk2 = bank()
                                mm(ps2[:, 0:128], at_h, bt_h, True, True, ['ar', 'btr'], [pk2])
                                tt('dve', MN[h][:], ps[:, 0:256], m_suiu, ALU.mult, [pk, 'cbt'], [f'MN{q}_{h}'])
                                tt('dve', W0[h][:, 128:256], ps[:, 256:384], m_su, ALU.mult, [pk, 'cbt'], [f'W0_{h}'])
                                tt('dve', Nbr[h][:], ps[:, 384:512], m_iu, ALU.mult, [pk, 'cbt'], [f'Nbr{q}_{h}'])
                                tt('dve', W0[h][:, 0:128], ps2[:, 0:128], m_sl, ALU.mult, [pk2, 'cbt'], [f'W0_{h}'])
                                yield
                        for k in range(7):
                            for h in range(8):
                                cur, ck = (W0[h], f'W0_{h}') if k == 0 else (Wpp[h][(k - 1) % 2], f'Wpp{h}_{(k - 1) % 2}')
                                ps, pk = bank()
                                if k < 6:
                                    nxt, nk = Wpp[h][k % 2], f'Wpp{h}_{k % 2}'
                                    mm(ps[:, 0:128], cur[:, 128:256], cur[:, 0:128], True, True, [ck], [pk])
                                    mm(ps[:, 128:384], cur[:, 0:128], cur[:, 128:384], True, True, [ck], [pk])
                                    cp('act', nxt[:, 0:256], ps[:, 0:256], [pk], [nk])
                                    tt('dve', nxt[:, 256:384], ps[:, 256:384], cur[:, 256:384], ALU.add, [pk, ck], [nk])
                                else:
                                    mm(ps[:, 0:128], cur[:, 0:128], cur[:, 256:384], True, True, [ck], [pk])
                                    tt('dve', Tfin[h][:], ps[:, 0:128], cur[:, 256:384], ALU.add, [pk, ck], [f'Tfin{q}_{h}'])
                                    psn, pkn = bank()
                                    mm(psn[:, 0:128], W0[h][:, 0:128], Tfin[h][:], True, True, [f'W0_{h}', f'Tfin{q}_{h}'], [pkn])
                                    fn_, fnk = ft()
                                    tt('dve', fn_[:, 0:128], psn[:, 0:128], Tfin[h][:], ALU.subtract, [pkn, f'Tfin{q}_{h}'], [fnk])
                                    tt('pool', Rn[h][:], fn_[:, 0:128], ident, ALU.add, [fnk, 'cbt'], [f'Rn{q}_{h}'])
                                if h % 2 == 1:
                                    yield

                    def stage_B(c):
                        csl = slice(c * 128, (c + 1) * 128)
                        q = c % 2
                        MN, Nbr, Tfin, Rn = MN2[q], Nbr2[q], Tfin2[q], Rn2[q]
                        for ct in range(4):
                            ps, pk = bank()
                            for hh in range(2):
                                h = ct * 2 + hh; po = hh * 64
                                at_h = ar[po:po + 64, ct, c * 256:c * 256 + 128]
                                mm(ps[:, hh * 64:(hh + 1) * 64], at_h, Srb[po:po + 64, l, ct, :], True, False, ['ar', 'Srb'], [pk])
                                mm(ps[:, hh * 64:(hh + 1) * 64], MN[h][:, 0:128], tokT[:, c, ct, po:po + 64], False, True,
                                   [f'MN{q}_{h}', 'tokT'], [pk])
                            cp('act', XTb[ct][:], ps[:, 0:128], [pk], [f'XTb{ct}'])
                            yield
                        for ct in range(4):
                            ps, pk = bank()
                            for hh in range(2):
                                h = ct * 2 + hh
                                mm(ps[:, hh * 64:(hh + 1) * 64], Tfin[h][:], XTb[ct][:, hh * 64:(hh + 1) * 64], True, True,
                                   [f'Tfin{q}_{h}', f'XTb{ct}'], [pk])
                            cp('act', UTb[ct][:], ps[:, 0:128], [pk], [f'UTb{ct}'])
                            yield
                        for ct in range(4):
                            ps, pk = bank()
                            for hh in range(2):
                                h = ct * 2 + hh
                                mm(ps[:, hh * 64:(hh + 1) * 64], Rn[h][:], UTb[ct][:, hh * 64:(hh + 1) * 64], True, True,
                                   [f'Rn{q}_{h}', f'UTb{ct}'], [pk])
                            tt('dve', XTb[ct][:], ps[:, 0:128], UTb[ct][:], ALU.add, [pk, f'UTb{ct}'], [f'XTb{ct}'])
                            yield
                        for ct in range(4):
                            ps, pk = bank()
                            for hh in range(2):
                                h = ct * 2 + hh; po = hh * 64
                                rt_h = ar[po:po + 64, ct, c * 256 + 128:(c + 1) * 256]
                                o_ = ps[po:po + 64, 0:128]
                                mm(o_, Srb[po:po + 64, l, ct, :], rt_h, True, False, ['Srb', 'ar'], [pk])
                                mm(o_, tokT[:, c, ct, po:po + 64], MN[h][:, 128:256], False, False, ['tokT', f'MN{q}_{h}'], [pk])
                                mm(o_, XTb[ct][:, hh * 64:(hh + 1) * 64], Nbr[h][:], False, True, [f'XTb{ct}', f'Nbr{q}_{h}'], [pk])
                            cp('act', ysb[:, ct, csl], ps[:, 0:128], [pk], [f'ysb{ct}'])
                            ps2, pk2 = bank()
                            for hh in range(2):
                                po = hh * 64
                                o_ = ps2[po:po + 64, 0:64]
                                mm(o_, tokT[:, c, ct, 128 + po:128 + po + 64], tokT[:, c, ct, po:po + 64], True, False, ['tokT'], [pk2])
                                mm(o_, tokT[:, c, ct, 256 + po:256 + po + 64], XTb[ct][:, hh * 64:(hh + 1) * 64], False, True,
                                   ['tokT', f'XTb{ct}'], [pk2])
                            f_, fk = ft()
                            tt('dve', f_[:, 0:64], ps2[:, 0:64], Sr[:, l, ct, :], ALU.add, [pk2, 'Sr'], [fk])
                            ts('dve', Sr[:, l, ct, :], f_[:, 0:64], eL[:, ct, c * 128 + 127:c * 128 + 128], ALU.mult,
                               [fk, 'eL'], ['Sr'])
                            cp('pool', Srb[:, l, ct, :], Sr[:, l, ct, :], ['Sr'], ['Srb'])
                            yield

                    yield from stage_A(0)
                    for c in range(NCH):
                        yield from inter(stage_B(c), stage_A(c + 1) if c + 1 < NCH else None)

                for _ in inter(gla_thread(), rwkv_thread()):
                    pass
                if debug:
                    dst = dbg_d.get(f'rwkv_y{l}')
                    if dst is not None:
                        S.dma('sp', dst.rearrange("(k p) t -> p k t", p=128)[:, :, tsl], ysb[:], reads=[f'ysb{i}' for i in range(4)],
                              writes=['dbg_y'])

                chk(4)
                rsG = [(eLp[0], 'eLp0'), (eLp[1], 'eLp1'), (emL[0], 'emL0'), (emL[1], 'emL1')]
                rsR = [(kkn[0], 'kkn0'), (kkn[1], 'kkn1'), (kmod[0], 'kmod0'), (kmod[1], 'kmod1')]
                pg = {}
                for h in range(4):
                    act(sq[:, h, :], osb[:, h, :], AF.Square, ['osb'], [f'sq{h}'])
                for ct in range(4):
                    cp('dve' if ct < 2 else 'act', hT[:, ct, :], ysb[:, ct, :], [f'ysb{ct}'], [f'hTy{ct}', 'hT'])
                for h in range(4):
                    ps, pk = bank(); pg[h] = (ps, pk)
                    mm(ps[:, 0:TT], ones, sq[:, h, :], True, True, ['cbt', f'sq{h}'], [pk])
                for h in range(4):
                    ps, pk = pg[h]
                    act(rsG[h][0][:], ps[:, 0:TT], AF.Ln, [pk], [rsG[h][1]], scale=1.0 / 128, bias=NORM_EPS)
                for ct in range(4):
                    ps, pk = bank(); pg[ct] = (ps, pk)
                    mm(ps[:, 0:TT], bones64, hT[:, ct, :], True, True, ['cbt', f'hTy{ct}', 'hT'], [pk])
                for ct in range(4):
                    ps, pk = pg[ct]
                    tt('dve', ysb[:, ct, :], ysb[:, ct, :], ps[:, 0:TT], ALU.subtract, [f'ysb{ct}', pk], [f'ysb{ct}'])
                for h in range(4):
                    act(rsG[h][0][:], rsG[h][0][:], AF.Exp, [rsG[h][1]], [rsG[h][1]], scale=-0.5)
                for ct in range(4):
                    act(sq[:, 4 + ct, :], ysb[:, ct, :], AF.Square, [f'ysb{ct}'], [f'sq{4 + ct}'])
                for ct in range(4):
                    ps, pk = bank(); pg[ct] = (ps, pk)
                    mm(ps[:, 0:TT], bones64, sq[:, 4 + ct, :], True, True, ['cbt', f'sq{4 + ct}'], [pk])
                for h in range(4):
                    tt('dve', osb[:, h, :], osb[:, h, :], rsG[h][0][:], ALU.mult, ['osb', rsG[h][1]], ['osb'])
                for ct in range(4):
                    ps, pk = pg[ct]
                    act(rsR[ct][0][:], ps[:, 0:TT], AF.Ln, [pk], [rsR[ct][1]], bias=GN_EPS)
                for h in range(4):
                    stt(ob[:, h, :], osb[:, h, :], P('gng', l), sg[:, h, :], ALU.mult, ALU.mult, ['osb', 'prm', 'sg'], ['ob'])
                for ct in range(4):
                    act(rsR[ct][0][:], rsR[ct][0][:], AF.Exp, [rsR[ct][1]], [rsR[ct][1]], scale=-0.5)
                for ct in range(4):
                    tt('dve', ysb[:, ct, :], ysb[:, ct, :], rsR[ct][0][:], ALU.mult, [f'ysb{ct}', rsR[ct][1]], [f'ysb{ct}'])
                for ct in range(4):
                    ts('dve', ysb[:, ct, :], ysb[:, ct, :], Pc('gn_g', l, ct), ALU.mult, [f'ysb{ct}', 'prm'], [f'ysb{ct}'],
                       Pc('gn_b', l, ct), ALU.add)
                for ct in range(4):
                    tt('pool', ysb[:, ct, :], ysb[:, ct, :], bonus[:, ct, :], ALU.add, [f'ysb{ct}', 'bonus'], [f'ysb{ct}'])
                for ct in range(4):
                    tt('dve', ob[:, 4 + ct, :], ysb[:, ct, :], gate[:, ct, :], ALU.mult, [f'ysb{ct}', 'gate'], ['ob'])
                dump(f'oo{l}', ob[:], D, ti, ['ob'], 'pool')

                for j in range(8):
                    slot, sk = loadA(s_o[l, j], f's_o{l}')
                    ps, pk = bank()
                    for kc in range(8):
                        mm(ps[:, 0:TT], slot[:, kc, :], ob[:, kc, :], kc == 0, kc == 7, [sk, 'ob'], [pk])
                    stt(xT[:, j, :], ps[:, 0:TT], modc(l, 2, j), xT[:, j, :], ALU.mult, ALU.add, [pk, 'modt', 'xT'], ['xT'])
                dump(f'xmix{l}', xT[:], D, ti, ['xT'])

                chk(5)
                rmsnorm_mod(l, A_f, 3)
                par = ti % 2
                def ffn_finish(jp, accs_):
                    s_, skk = sgt[jp % 2], sgtk[jp % 2]
                    act(s_[:], accs_[0][0][:], AF.Silu, [accs_[0][1]], [skk])
                    tt('pool', mbuf[:, jp, :], s_[:], accs_[1][0][:], ALU.mult, [skk, accs_[1][1]], [f'mbuf{jp}'] + BIGK)
                ffn_pend = None
                for j in range(NFF):
                    accs = []
                    for half in range(2):
                        u = 2 * j + half
                        slot, sk = loadA(s_u[l, u], f's_u{l}')
                        ps, pk = bank()
                        for kc in range(8):
                            mm(ps[:, 0:TT], slot[:, kc, :], hT[:, kc, :], kc == 0, kc == 7, [sk, 'hT'], [pk])
                        a_ = acc[half * 2 + (j % 2)]; ak = acck[half * 2 + (j % 2)]
                        o, w = POFF[f'cw{l}']
                        cw = lambda tap: prm[:, o + u * 3 + tap:o + u * 3 + tap + 1]
                        ob_, wb_ = POFF[f'cb{l}']
                        ub, ubk = ptmp[half], f'ptmp{half}'
                        act(a_[:], ps[:, 0:TT], AF.Identity, [pk, 'prm'], [ak], scale=cw(2), bias=prm[:, ob_ + u:ob_ + u + 1])
                        cp('act', ub[:, 2:TT + 2], ps[:, 0:TT], [pk], [ubk])
                        cp('pool', ub[:, 0:2], chalo[:, par, l, u, :], ['chalo'], [ubk])
                        cp('pool', chalo[:, 1 - par, l, u, :], ub[:, TT:TT + 2], [ubk], ['chalo'])
                        stt(a_[:], ub[:, 1:TT + 1], cw(1), a_[:], ALU.mult, ALU.add, [ubk, 'prm', ak], [ak])
                        stt(a_[:], ub[:, 0:TT], cw(0), a_[:], ALU.mult, ALU.add, [ubk, 'prm', ak], [ak])
                        accs.append((a_, ak))
                    if ffn_pend is not None:
                        ffn_finish(*ffn_pend)
                    ffn_pend = (j, accs)
                if ffn_pend is not None:
                    ffn_finish(*ffn_pend)
                for jo in range(8):
                    slot, sk = loadD(s_d[l, jo], f's_d{l}')
                    ps, pk = bank()
                    for kc in range(NFF):
                        mm(ps[:, 0:TT], slot[:, kc, :], mbuf[:, kc, :], kc == 0, kc == NFF - 1, [sk, f'mbuf{kc}'] + BIGK, [pk])
                    stt(xT[:, jo, :], ps[:, 0:TT], modc(l, 5, jo), xT[:, jo, :], ALU.mult, ALU.add, [pk, 'modt', 'xT'], ['xT'])
                dump(f'x{l}', xT[:], D, ti, ['xT'])
                chk(51 + l)

            chk(6)
            act(sq[:], xT[:], AF.Square, ['xT'], [f'sq{i}' for i in range(8)])
            ps, pk = bank()
            for kc in range(8):
                mm(ps[:, 0:TT], ones, sq[:, kc, :], kc == 0, kc == 7, ['cbt', f'sq{kc}'], [pk])
            act(rstd[:], ps[:, 0:TT], AF.Ln, [pk], ['rstd'], scale=1.0 / D, bias=NORM_EPS)
            act(rstd[:], rstd[:], AF.Exp, ['rstd'], ['rstd'], scale=-0.5)
            fo, fw = POFF['final_g']
            for kc in range(8):
                dst_, dk_ = (osb, 'osb') if kc < 4 else (ysb, f'ysb{kc % 4}')
                stt(dst_[:, kc % 4, :], xT[:, kc, :], prm[:, fo + kc:fo + kc + 1], rstd[:], ALU.mult, ALU.mult,
                    ['xT', 'prm', 'rstd'], [dk_])
            odv = out_d.rearrange("(k p) t -> p k t", p=128)
            S.dma('sp', odv[:, 0:4, tsl], osb[:], reads=['osb'], writes=['out_d'])
            S.dma('sp', odv[:, 4:8, tsl], ysb[:], reads=[f'ysb{i}' for i in range(4)], writes=['out_d'])
        try:
            _main()
        except _Stop:
            pass
        S.barrier()
        print(f"[kernel] T={T} instructions={S.n_inst} waits={S.n_wait}", flush=True)
    return nc


def core_inputs(inputs, shared, b, T):
    m = dict(shared)
    m['xT'] = np.ascontiguousarray(np.asarray(inputs['x'], np.float32)[b, :T].T)
    P = shared['params'].copy()
    o, w = POFF['c']
    P[:, o:o + w] = _fm(np.asarray(inputs['c'], np.float32)[b])
    m['params'] = P
    return m


def kernel(**inputs):
    T = SEQ
    shared = prep_shared(inputs)
    nc = build_nc(T)
    in_maps = [core_inputs(inputs, shared, c // 2, T) for c in range(8)]
    res = run_bass_kernel_spmd(nc, in_maps, core_ids=list(range(8)))
    out = np.zeros((4, T, D), np.float32)
    H = T // 2
    for c in range(8):
        b, s = c // 2, c % 2
        o = res.results[c]["outT"]
        out[b, s * H:(s + 1) * H] = o[:, s * H:(s + 1) * H].T
    return out
```
